# Optimizing a Trainium2 kernel written in Bass

```python
import jax, jax.numpy as jnp
from jax import lax
import numpy as np

D_MODEL = 2048
BATCH = 1
SEQ = 16384
DEPTH = 1

GRID_W = 64
CTX_LEN = 256
N_HEADS_ATTN = 16
HEAD_DIM = 64
D_ATTN = N_HEADS_ATTN * HEAD_DIM
D_MIX = D_MODEL
D_CONV = D_MIX - D_ATTN
N_CONV_GROUPS = 16
CONV_WIDTH = 3
WIN_H = 8
WIN_W = 16
ROW_BLOCK = 2
D_FF = 5632
ROPE_BASE = 10000.0
EPS = 1e-6
N_MOD = 9
SPLITS = (D_ATTN, 2 * D_ATTN, 3 * D_ATTN, 3 * D_ATTN + D_CONV, 3 * D_ATTN + 2 * D_CONV)
D_IN = 3 * D_ATTN + 3 * D_CONV

kernel_name = "hybrid_na_shortconv_macaron_dit_layer"


def rmsnorm(x, g):
    xf = x.astype(jnp.float32)
    y = xf * lax.rsqrt(jnp.mean(xf * xf, axis=-1, keepdims=True) + EPS)
    return (y * g.astype(jnp.float32)).astype(x.dtype)


def modulate(h, shift, scale):
    return h * (1 + scale) + shift


def ada_mod(cvec, w_ada, b_ada):
    m = jax.nn.silu(cvec) @ w_ada + b_ada
    return jnp.split(m[..., None, :], N_MOD, axis=-1)


def half_ffn(x, mods, g_norm, w_in, w_out):
    shift, scale, gate = mods
    h = modulate(rmsnorm(x, g_norm), shift, scale)
    a, b = jnp.split(h @ w_in, 2, axis=-1)
    return x + 0.5 * gate * ((jax.nn.silu(a) * b) @ w_out)


def heads(t):
    return t.reshape(t.shape[0], t.shape[1], -1, HEAD_DIM)


def _rotate(xa, pos):
    nf = xa.shape[-1] // 2
    inv = ROPE_BASE ** (-jnp.arange(nf, dtype=jnp.float32) / nf)
    ang = pos.astype(jnp.float32)[:, None] * inv[None, :]
    cos = jnp.cos(ang)[None, :, None, :].astype(xa.dtype)
    sin = jnp.sin(ang)[None, :, None, :].astype(xa.dtype)
    x1, x2 = xa[..., :nf], xa[..., nf:]
    return jnp.concatenate([x1 * cos - x2 * sin, x1 * sin + x2 * cos], axis=-1)


def axial_rope(t, pos_r, pos_c):
    half = t.shape[-1] // 2
    return jnp.concatenate([_rotate(t[..., :half], pos_r), _rotate(t[..., half:], pos_c)], axis=-1)


def neighbourhood_attention(q, k, v, k_ctx, v_ctx, rpb, rows):
    B, S, H, Dh = q.shape
    kh = min(WIN_H, rows)
    n_blocks = rows // ROW_BLOCK
    scale = Dh ** -0.5
    qg = q.reshape(B, rows, GRID_W, H, Dh)
    kg = k.reshape(B, rows, GRID_W, H, Dh)
    vg = v.reshape(B, rows, GRID_W, H, Dh)
    cols = np.arange(GRID_W)
    col_start = np.clip(cols - WIN_W // 2, 0, GRID_W - WIN_W)
    col_idx = col_start[:, None] + np.arange(WIN_W)[None, :]
    dc = col_idx - cols[:, None] + (WIN_W - 1)

    def block(bi):
        r = bi * ROW_BLOCK + jnp.arange(ROW_BLOCK)
        rs = jnp.clip(r - kh // 2, 0, rows - kh)
        row_idx = rs[:, None] + jnp.arange(kh)[None, :]
        dr = row_idx - r[:, None] + (WIN_H - 1)
        qb = lax.dynamic_slice_in_dim(qg, bi * ROW_BLOCK, ROW_BLOCK, axis=1)
        kb = kg[:, row_idx][:, :, :, col_idx]
        vb = vg[:, row_idx][:, :, :, col_idx]
        bias = rpb[:, dr[:, None, :, None], dc[None, :, None, :]]
        bias = jnp.transpose(bias, (1, 2, 0, 3, 4))
        s_loc = jnp.einsum('brwhd,brkwjhd->brwhkj', qb, kb) * scale + bias
        s_ctx = jnp.einsum('brwhd,blhd->brwhl', qb, k_ctx) * scale
        s = jnp.concatenate([s_loc.reshape(B, ROW_BLOCK, GRID_W, H, kh * WIN_W), s_ctx], axis=-1)
        p = jax.nn.softmax(s.astype(jnp.float32), axis=-1).astype(v.dtype)
        p_loc = p[..., :kh * WIN_W].reshape(B, ROW_BLOCK, GRID_W, H, kh, WIN_W)
        p_ctx = p[..., kh * WIN_W:]
        return (jnp.einsum('brwhkj,brkwjhd->brwhd', p_loc, vb)
                + jnp.einsum('brwhl,blhd->brwhd', p_ctx, v_ctx))

    out = lax.map(block, jnp.arange(n_blocks))
    return jnp.moveaxis(out, 0, 1).reshape(B, S, H * Dh)


def context_attention(q, k, v):
    B, L, H, Dh = q.shape
    s = jnp.einsum('blhd,bmhd->bhlm', q, k) * (Dh ** -0.5)
    p = jax.nn.softmax(s.astype(jnp.float32), axis=-1).astype(v.dtype)
    return jnp.einsum('bhlm,bmhd->blhd', p, v).reshape(B, L, H * Dh)


def gated_short_conv(bg, cg, u, conv_w, conv_b):
    z = cg * u
    S = z.shape[1]
    pad = CONV_WIDTH // 2
    zp = jnp.pad(z, ((0, 0), (pad, pad), (0, 0)))
    y = conv_b
    for j in range(CONV_WIDTH):
        y = y + zp[:, j:j + S] * conv_w[j]
    return bg * y


def setup_inputs(seed: int = 0) -> dict:
    key = jax.random.key(seed)
    ks = jax.random.split(key, 24)
    f32 = jnp.float32
    L, D = DEPTH, D_MODEL

    def nrm(k, shape, s):
        return jax.random.normal(k, shape, f32) * s

    def gain(k, shape):
        return 1.0 + 0.05 * jax.random.normal(k, shape, f32)

    return {
        "x": nrm(ks[0], (BATCH, SEQ, D), 1.0),
        "c": nrm(ks[1], (BATCH, D), 1.0),
        "ctx": nrm(ks[2], (BATCH, CTX_LEN, D), 1.0),
        "c_ctx": nrm(ks[3], (D,), 1.0),
        "w_ada": nrm(ks[4], (L, D, N_MOD * D), 0.5 * D ** -0.5),
        "b_ada": nrm(ks[5], (L, N_MOD * D), 0.02),
        "ff1_norm": gain(ks[6], (L, D)),
        "ff1_w_in": nrm(ks[7], (L, D, 2 * D_FF), D ** -0.5),
        "ff1_w_out": nrm(ks[8], (L, D_FF, D), D_FF ** -0.5),
        "mix_norm": gain(ks[9], (L, D)),
        "w_in": nrm(ks[10], (L, D, D_IN), D ** -0.5),
        "q_norm": gain(ks[11], (L, HEAD_DIM)),
        "k_norm": gain(ks[12], (L, HEAD_DIM)),
        "rpb": nrm(ks[13], (L, N_HEADS_ATTN, 2 * WIN_H - 1, 2 * WIN_W - 1), 0.1),
        "conv_w": nrm(ks[14], (L, CONV_WIDTH, D_CONV), CONV_WIDTH ** -0.5),
        "conv_b": nrm(ks[15], (L, D_CONV), 0.02),
        "out_norm_attn": gain(ks[16], (L, D_ATTN)),
        "out_norm_conv": gain(ks[17], (L, D_CONV)),
        "w_out": nrm(ks[18], (L, D_MIX, D), D_MIX ** -0.5),
        "ff2_norm": gain(ks[19], (L, D)),
        "ff2_w_in": nrm(ks[20], (L, D, 2 * D_FF), D ** -0.5),
        "ff2_w_out": nrm(ks[21], (L, D_FF, D), D_FF ** -0.5),
    }


def reference(x, c, ctx, c_ctx, w_ada, b_ada, ff1_norm, ff1_w_in, ff1_w_out, mix_norm, w_in,
              q_norm, k_norm, rpb, conv_w, conv_b, out_norm_attn, out_norm_conv, w_out,
              ff2_norm, ff2_w_in, ff2_w_out):
    B, S, _ = x.shape
    rows = S // GRID_W
    pos = jnp.arange(S, dtype=jnp.int32)
    pos_r, pos_c = pos // GRID_W, pos % GRID_W

    for l in range(DEPTH):
        update_ctx = l < DEPTH - 1
        mx = ada_mod(c, w_ada[l], b_ada[l])
        mc = ada_mod(c_ctx[None], w_ada[l], b_ada[l])

        x = half_ffn(x, mx[0:3], ff1_norm[l], ff1_w_in[l], ff1_w_out[l])
        ctx = half_ffn(ctx, mc[0:3], ff1_norm[l], ff1_w_in[l], ff1_w_out[l])

        hx = modulate(rmsnorm(x, mix_norm[l]), mx[3], mx[4])
        hc = modulate(rmsnorm(ctx, mix_norm[l]), mc[3], mc[4])
        q, k, v, bg, cg, u = jnp.split(hx @ w_in[l], SPLITS, axis=-1)
        q = axial_rope(rmsnorm(heads(q), q_norm[l]), pos_r, pos_c)
        k = axial_rope(rmsnorm(heads(k), k_norm[l]), pos_r, pos_c)
        v = heads(v)
        if update_ctx:
            qc, kc, vc, bgc, cgc, uc = jnp.split(hc @ w_in[l], SPLITS, axis=-1)
        else:
            kc, vc = jnp.split(hc @ w_in[l][:, D_ATTN:3 * D_ATTN], 2, axis=-1)
        kc = rmsnorm(heads(kc), k_norm[l])
        vc = heads(vc)

        attn = neighbourhood_attention(q, k, v, kc, vc, rpb[l], rows)
        conv = gated_short_conv(bg, cg, u, conv_w[l], conv_b[l])
        y = jnp.concatenate([rmsnorm(attn, out_norm_attn[l]), rmsnorm(conv, out_norm_conv[l])],
                            axis=-1) @ w_out[l]
        if update_ctx:
            attn_c = context_attention(rmsnorm(heads(qc), q_norm[l]), kc, vc)
            conv_c = gated_short_conv(bgc, cgc, uc, conv_w[l], conv_b[l])
            yc = jnp.concatenate([rmsnorm(attn_c, out_norm_attn[l]), rmsnorm(conv_c, out_norm_conv[l])],
                                 axis=-1) @ w_out[l]
            ctx = ctx + mc[5] * yc
        x = x + mx[5] * y

        x = half_ffn(x, mx[6:9], ff2_norm[l], ff2_w_in[l], ff2_w_out[l])
        if update_ctx:
            ctx = half_ffn(ctx, mc[6:9], ff2_norm[l], ff2_w_in[l], ff2_w_out[l])
    return x
```

```python
import contextlib
import numpy as np
import concourse.bass as bass
import concourse.mybir as mybir
from concourse.bass_utils import run_bass_kernel_spmd

F32 = mybir.dt.float32
BF16 = mybir.dt.bfloat16
ALU = mybir.AluOpType
AF = mybir.ActivationFunctionType

NCORES = 8
D = 2048
KC = 16
DFF = 5632
FC = 44
GW = 64
WROWS = 40
WT = WROWS * GW
CTX = 256
TT = WT + CTX
OWN0 = 4 * GW
OWN = 2048
EPS = 1e-6
NEG = -30000.0

C_BADA = 0
C_FF1N = C_BADA + 144
C_MIXN = C_FF1N + 16
C_FF2N = C_MIXN + 16
C_QG = C_FF2N + 16
C_KG = C_QG + 1
C_ONA = C_KG + 1
C_ONC = C_ONA + 8
C_CW = C_ONC + 8
C_CB = C_CW + 24
C_HALO = C_CB + 8
C_C2 = C_HALO + 2
NCONST = C_C2 + 32

ENGS = ("pe", "act", "dve", "pool", "sp")


class _Op:
    __slots__ = ("eng", "fn", "deps", "inc", "dma", "idx", "epoch")

    def __init__(self, eng, fn, dma, epoch):
        self.eng = eng
        self.fn = fn
        self.deps = {}
        self.inc = False
        self.dma = dma
        self.idx = -1
        self.epoch = epoch


class Sched:
    def __init__(self, nc):
        self.nc = nc
        self.ops = {e: [] for e in ENGS}
        self.lastw = {}
        self.readers = {}
        self.dma_cnt = {}
        self.known = {e: {} for e in ENGS}
        self.pending = {e: {} for e in ENGS}
        self.epoch = 0

    def _tok_epoch(self, key):
        return key[2] if key[0] == "e" else key[1][0]

    def _add_dep(self, op, tok, kind, force=False):
        key, val = tok
        if not force and self._tok_epoch(key) < self.epoch:
            return
        if key[0] == "e" and key[1] == op.eng:
            if op.eng == "pe" or kind == "war":
                return
        if self.known[op.eng].get(key, -1) >= val:
            return
        if op.deps.get(key, -1) < val:
            op.deps[key] = val

    def op(self, eng, fn, reads=(), writes=(), dma=None):
        if dma is not None:
            dma = (self.epoch, dma)
        o = _Op(eng, fn, dma, self.epoch)
        for key, val in self.pending[eng].items():
            self._add_dep(o, (key, val), "raw", force=True)
        self.pending[eng] = {}
        for r in reads:
            t = self.lastw.get(r)
            if t is not None:
                self._add_dep(o, t, "raw")
            if isinstance(r, tuple) and r[0] == "ps":
                for t in self.readers.get(r, ()):
                    if t[0][0] == "e" and t[0][1] != eng:
                        self._add_dep(o, t, "raw")
        for w in writes:
            t = self.lastw.get(w)
            if t is not None:
                self._add_dep(o, t, "waw")
            for t in self.readers.get(w, ()):
                self._add_dep(o, t, "war")
        o.idx = len(self.ops[eng])
        self.ops[eng].append(o)
        for key, val in o.deps.items():
            self.known[eng][key] = val
            if key[0] == "e":
                self.ops[key[1]][val].inc = True
        if dma is not None:
            c = self.dma_cnt.get(dma, 0) + 16
            self.dma_cnt[dma] = c
            tok = (("d", dma), c)
        else:
            tok = (("e", eng, self.epoch), o.idx)
        for r in reads:
            self.readers.setdefault(r, []).append(tok)
        for w in writes:
            self.lastw[w] = tok
            self.readers[w] = []
        return o

    def barrier(self):
        toks = {}
        for e in ENGS:
            if self.ops[e]:
                last = self.ops[e][-1]
                if last.dma is None and last.fn is not None:
                    toks[("e", e, last.epoch)] = last.idx
                else:
                    for o in reversed(self.ops[e]):
                        if o.dma is None and o.fn is not None:
                            toks[("e", e, o.epoch)] = o.idx
                            break
        for k, c in self.dma_cnt.items():
            if k[0] == self.epoch:
                toks[("d", k)] = c
        for e in ENGS:
            for key, val in toks.items():
                if key[0] == "e" and key[1] == e:
                    continue
                if self.pending[e].get(key, -1) < val:
                    self.pending[e][key] = val
        self.epoch += 1

    def emit(self):
        nc = self.nc
        self.barrier()
        self.op("sp", None)
        with contextlib.ExitStack() as st:
            cum = {}
            used = set()
            for e in ENGS:
                cnt = {}
                arr = []
                for o in self.ops[e]:
                    if o.inc and o.dma is None and o.fn is not None:
                        cnt[o.epoch] = cnt.get(o.epoch, 0) + 1
                        used.add((e, o.epoch))
                    arr.append(cnt.get(o.epoch, 0))
                cum[e] = arr
            esem = {k: st.enter_context(nc.semaphore("s_%s_%d" % k)) for k in sorted(used)}
            dsem = {k: st.enter_context(nc.semaphore("d_%d" % i))
                    for i, k in enumerate(self.dma_cnt)}
            self.nsem = len(esem) + len(dsem)
            block = st.enter_context(nc.Block())

            def run(ename, engine):
                for o in self.ops[ename]:
                    for key, val in o.deps.items():
                        if key[0] == "e":
                            v = cum[key[1]][val]
                            if v > 0:
                                engine.wait_ge(esem[(key[1], key[2])], v)
                        else:
                            engine.wait_ge(dsem[key[1]], val)
                    if o.fn is None:
                        continue
                    ins = o.fn(engine)
                    if o.dma is not None:
                        ins.then_inc(dsem[o.dma], 16)
                    elif o.inc:
                        ins.then_inc(esem[(ename, o.epoch)], 1)

            @block.tensor
            def _(eng):
                run("pe", eng)

            @block.scalar
            def _(eng):
                run("act", eng)

            @block.vector
            def _(eng):
                run("dve", eng)

            @block.gpsimd
            def _(eng):
                run("pool", eng)

            @block.sync
            def _(eng):
                run("sp", eng)


def _esize(dt):
    return 4 if dt == F32 else 2


class Mem:
    def __init__(self, nc):
        self.nc = nc
        self.base = (nc.sbuf_base + 63) // 64 * 64
        self.top = nc.sbuf_top
        self.off = self.base
        self.n = 0

    def mark(self):
        return self.off

    def reset(self, m):
        self.off = m

    def alloc(self, name, shape, dt):
        sz = int(np.prod(shape[1:])) * _esize(dt)
        sz = (sz + 63) // 64 * 64
        assert self.off + sz <= self.top, ("SBUF overflow", name, self.off, sz, self.top)
        self.n += 1
        t = self.nc.alloc_sbuf_tensor_at("%s_%d" % (name, self.n), list(shape), dt, offset=self.off)
        self.off += sz
        return t


def build_nc(stop_after=None, debug=False, skip_ffn1=False):
    nc = bass.Bass("TRN2", target_bir_lowering=False)
    dk = "ExternalOutput" if debug else "Internal"

    def din(name, shape, dt=F32):
        return nc.dram_tensor(name, list(shape), dt, kind="ExternalInput").ap()

    xin = din("xin", [KC, 128, TT])
    consts_d = din("consts", [128, NCONST])
    bones_d = din("bones", [128, 128])
    pmat_d = din("pmat", [128, 128])
    cs_d = din("cossin", [2, 128, TT])
    tstd_d = din("tstd", [8, 128, 2 * 5 * 128])
    tspec_d = din("tspec", [8, 128, 2 * 22 * 128])
    w_ada = din("w_ada", [D, 9 * D])
    w1i = din("ff1_w_in", [D, 2 * DFF])
    w1o = din("ff1_w_out", [DFF, D])
    w_in = din("w_in", [D, 6144])
    w_out = din("w_out", [D, D])
    w2i = din("ff2_w_in", [D, 2 * DFF])
    w2o = din("ff2_w_out", [DFF, D])
    yout = nc.dram_tensor("yout", [KC, 128, OWN], F32, kind="ExternalOutput").ap()

    x1 = nc.dram_tensor("x1s", [KC, 128, TT], F32, kind=("ExternalInput" if skip_ffn1 else dk)).ap()
    x2 = nc.dram_tensor("x2s", [KC, 128, OWN], F32, kind=dk).ap()
    qT = nc.dram_tensor("qTs", [8, 128, OWN], BF16, kind=dk).ap()
    kT = nc.dram_tensor("kTs", [8, 128, TT], BF16, kind=dk).ap()
    vtok = nc.dram_tensor("vtoks", [22, 128, 1024], BF16, kind=dk).ap()
    aoT = nc.dram_tensor("aoTs", [KC, 128, OWN], BF16, kind=dk).ap()
    modsd = nc.dram_tensor("modsd", [128, 288], F32, kind=dk).ap()

    s = Sched(nc)
    mem = Mem(nc)

    st_ps = [nc.alloc_psum_tensor("stps%d" % i, [128, 1024], F32) for i in range(2)]
    pb_ps = [nc.alloc_psum_tensor("pbps%d" % i, [128, 512], F32) for i in range(4)]

    def bank(k):
        if k < 4:
            return st_ps[k // 2][:, (k % 2) * 512:(k % 2) * 512 + 512]
        return pb_ps[k - 4][:, :]

    def bres(k):
        return ("ps", k)

    consts = mem.alloc("consts", [128, NCONST], F32)
    ones_bf = mem.alloc("ones_bf", [128, 128], BF16)
    ones_f = mem.alloc("ones_f", [128, 128], F32)
    bones = mem.alloc("bones", [128, 128], F32)
    pmat = mem.alloc("pmat", [128, 128], F32)
    mods = mem.alloc("mods", [128, 144, 2], F32)
    sc_names = ["A1", "B1", "G1", "A2", "B2", "GM", "A3", "B3", "G3"]
    scal = {n: mem.alloc("sc" + n, [128, 16, 2], F32) for n in sc_names}
    epsb = mem.alloc("epsb", [128, 1], F32)

    s.op("sp", lambda e: e.dma_start(out=consts[:], in_=consts_d), writes=["consts"], dma="c0")
    s.op("sp", lambda e: e.dma_start(out=bones[:], in_=bones_d), writes=["bones"], dma="c1")
    s.op("sp", lambda e: e.dma_start(out=pmat[:], in_=pmat_d), writes=["pmat"], dma="c2")
    s.op("pool", lambda e: e.memset(ones_bf[:], 1.0), writes=["ones_bf"])
    s.op("pool", lambda e: e.memset(ones_f[:], 1.0), writes=["ones_f"])
    s.op("pool", lambda e: e.memset(epsb[:], EPS), writes=["epsb"])

    pmark = mem.mark()

    sc_c = mem.alloc("sc_c", [128, 16, 2], BF16)
    wslots = [mem.alloc("wslot%d" % i, [128, 16, 512], BF16) for i in range(3)]
    s.op("act", lambda e: e.activation(out=sc_c[:], in_=consts[:, C_C2:C_C2 + 32].rearrange("p (k v) -> p k v", v=2),
                                       func=AF.Silu), reads=["consts"], writes=["sc_c"])
    w_ada_r = w_ada.rearrange("(k p) c -> p k c", p=128)
    MB = 4
    for t in range(36):
        sl = t % 3
        s.op("pool", lambda e, t=t, sl=sl: e.dma_start(out=wslots[sl][:], in_=w_ada_r[:, :, t * 512:(t + 1) * 512]),
             writes=[("w", sl, 0)], dma=("w", sl, 0))
        for jj in range(4):
            j = t * 4 + jj
            for kc in range(KC):
                s.op("pe", lambda e, sl=sl, jj=jj, j=j, kc=kc: e.matmul(
                    bank(MB)[:, 2 * j:2 * j + 2], wslots[sl][:, kc, jj * 128:(jj + 1) * 128], sc_c[:, kc, :],
                    start=(kc == 0), stop=(kc == KC - 1)),
                    reads=[("w", sl, 0), "sc_c"], writes=[bres(MB)])
    mps = bank(MB)[:, 0:288].rearrange("p (j v) -> p j v", v=2)
    for v in range(2):
        s.op("dve", lambda e, v=v: e.tensor_tensor(out=mods[:, :, v], in0=mps[:, :, v], in1=consts[:, C_BADA:C_BADA + 144],
                                                  op=ALU.add), reads=[bres(MB), "consts"], writes=["mods"])

    def mk_scal(name, modi, kind, normcol):
        dst = scal[name]
        for v in range(2):
            src = mods[:, modi * 16:(modi + 1) * 16, v]
            if kind == "A":
                s.op("dve", lambda e, v=v, src=src: e.scalar_tensor_tensor(
                    out=dst[:, :, v], in0=src, scalar=1.0, in1=consts[:, normcol:normcol + 16],
                    op0=ALU.add, op1=ALU.mult), reads=["mods", "consts"], writes=["scal"])
            elif kind == "B":
                s.op("dve", lambda e, v=v, src=src: e.tensor_copy(out=dst[:, :, v], in_=src),
                     reads=["mods"], writes=["scal"])
            else:
                s.op("dve", lambda e, v=v, src=src: e.tensor_scalar(
                    out=dst[:, :, v], in0=src, scalar1=(0.5 if kind == "G" else 1.0), scalar2=None, op0=ALU.mult),
                    reads=["mods"], writes=["scal"])

    mk_scal("B1", 0, "B", 0)
    mk_scal("A1", 1, "A", C_FF1N)
    mk_scal("G1", 2, "G", 0)
    mk_scal("B2", 3, "B", 0)
    mk_scal("A2", 4, "A", C_MIXN)
    mk_scal("GM", 5, "GM", 0)
    mk_scal("B3", 6, "B", 0)
    mk_scal("A3", 7, "A", C_FF2N)
    mk_scal("G3", 8, "G", 0)
    if debug:
        s.op("sp", lambda e: e.dma_start(out=modsd, in_=mods[:].rearrange("p j v -> p (j v)")), reads=["mods"],
             writes=["modsd"], dma="dbg")

    def norm_stage(src, col0, N, Asc, Bsc, v, dst_of, xb, sq, rstd, ssb):
        nx = len(xb)
        for kc in range(KC):
            xs = kc % nx
            s.op("sp", lambda e, kc=kc, xs=xs: e.dma_start(out=xb[xs][:, 0:N], in_=src[kc, :, col0:col0 + N]),
                 writes=[("xb", xs)], dma=("xb", xs))
            q2 = kc % 2
            s.op("act", lambda e, xs=xs, q2=q2: e.activation(out=sq[q2][:, 0:N], in_=xb[xs][:, 0:N], func=AF.Square),
                 reads=[("xb", xs)], writes=[("sq", q2)])
            s.op("pe", lambda e, q2=q2, kc=kc: e.matmul(bank(ssb)[:, 0:N], ones_bf[:], sq[q2][:, 0:N],
                                                       start=(kc == 0), stop=(kc == KC - 1)),
                 reads=[("sq", q2), "ones_bf"], writes=[bres(ssb)])
        s.op("act", lambda e: e.activation(out=rstd[:, 0:N], in_=bank(ssb)[:, 0:N], func=AF.Sqrt,
                                           scale=1.0 / D, bias=epsb[:, 0:1]),
             reads=[bres(ssb), "epsb"], writes=["rstd"])
        s.op("dve", lambda e: e.reciprocal(out=rstd[:, 0:N], in_=rstd[:, 0:N]), reads=["rstd"], writes=["rstd"])
        for kc in range(KC):
            xs = kc % nx
            s.op("sp", lambda e, kc=kc, xs=xs: e.dma_start(out=xb[xs][:, 0:N], in_=src[kc, :, col0:col0 + N]),
                 writes=[("xb", xs)], dma=("xb", xs))
            s.op("dve", lambda e, xs=xs: e.tensor_tensor(out=xb[xs][:, 0:N], in0=xb[xs][:, 0:N], in1=rstd[:, 0:N],
                                                        op=ALU.mult),
                 reads=[("xb", xs), "rstd"], writes=[("xb", xs)])
            dst, dres = dst_of(kc)
            s.op("act", lambda e, xs=xs, kc=kc, dst=dst: e.activation(
                out=dst, in_=xb[xs][:, 0:N], func=AF.Identity, bias=Bsc[:, kc, v:v + 1], scale=Asc[:, kc, v:v + 1]),
                reads=[("xb", xs), "scal"], writes=[dres])

    def ffn_phase(src, dst, dst_col0, tiles, wi, wo, Asc, Bsc, Gsc, tag):
        m0 = mem.mark()
        h = mem.alloc("h", [128, KC, 1024], BF16)
        g = mem.alloc("g", [128, FC, 1024], BF16)
        wsl = [mem.alloc("wsl%d" % i, [128, 16 * 512], BF16) for i in range(3)]
        xb = [mem.alloc("xb%d" % i, [128, 512], F32) for i in range(4)]
        sq = [mem.alloc("sq%d" % i, [128, 512], BF16) for i in range(2)]
        rstd = mem.alloc("rstd", [128, 512], F32)
        sa = [mem.alloc("sa%d" % i, [128, 512], F32) for i in range(2)]
        xr = [xb[0], xb[1]]
        xo = [xb[2], xb[3]]
        wi_r = wi.rearrange("(k p) c -> p k c", p=128)
        wo_r = wo.rearrange("(f p) c -> p f c", p=128)
        SSB = 6
        wcnt = [0]
        ecnt = [0]

        def next_slot():
            sl = wcnt[0] % 3
            wcnt[0] += 1
            return sl

        for tile in tiles:
            hoff = 0
            hoffs = []
            for (c0, N, v) in tile:
                ho = hoff
                norm_stage(src, c0, N, Asc, Bsc, v,
                           lambda kc, ho=ho, N=N: (h[:, kc, ho:ho + N], ("h", kc)),
                           xb, sq, rstd, SSB)
                hoffs.append(ho)
                hoff += N
            for jg in range(FC // 2):
                sl = next_slot()
                wv = wsl[sl][:].rearrange("p (k c) -> p k c", c=512)
                s.op("pool", lambda e, jg=jg, wv=wv: e.dma_start(out=wv[:, :, 0:256],
                                                               in_=wi_r[:, :, jg * 256:(jg + 1) * 256]),
                     writes=[("w", sl, 0)], dma=("w", sl, 0))
                s.op("pool", lambda e, jg=jg, wv=wv: e.dma_start(out=wv[:, :, 256:512],
                                                               in_=wi_r[:, :, DFF + jg * 256:DFF + (jg + 1) * 256]),
                     writes=[("w", sl, 1)], dma=("w", sl, 1))
                for jj in range(2):
                    j = jg * 2 + jj
                    for hi, (c0, N, v) in enumerate(tile):
                        ho = hoffs[hi]
                        e2 = ecnt[0] % 2
                        ecnt[0] += 1
                        pa, pb = e2 * 2, e2 * 2 + 1
                        for kc in range(KC):
                            s.op("pe", lambda e, wv=wv, jj=jj, kc=kc, ho=ho, N=N, pa=pa: e.matmul(
                                bank(pa)[:, 0:N], wv[:, kc, jj * 128:(jj + 1) * 128], h[:, kc, ho:ho + N],
                                start=(kc == 0), stop=(kc == KC - 1)),
                                reads=[("w", sl, 0), ("h", kc)], writes=[bres(pa)])
                        for kc in range(KC):
                            s.op("pe", lambda e, wv=wv, jj=jj, kc=kc, ho=ho, N=N, pb=pb: e.matmul(
                                bank(pb)[:, 0:N], wv[:, kc, 256 + jj * 128:256 + (jj + 1) * 128], h[:, kc, ho:ho + N],
                                start=(kc == 0), stop=(kc == KC - 1)),
                                reads=[("w", sl, 1), ("h", kc)], writes=[bres(pb)])
                        s.op("act", lambda e, e2=e2, pa=pa, N=N: e.activation(out=sa[e2][:, 0:N], in_=bank(pa)[:, 0:N],
                                                                             func=AF.Silu),
                             reads=[bres(pa)], writes=[("sa", e2)])
                        s.op("dve", lambda e, e2=e2, pb=pb, N=N, j=j, ho=ho: e.tensor_tensor(
                            out=g[:, j, ho:ho + N], in0=bank(pb)[:, 0:N], in1=sa[e2][:, 0:N], op=ALU.mult),
                            reads=[bres(pb), ("sa", e2)], writes=[("g", j)])
            for m in range(KC):
                sl = next_slot()
                wv = wsl[sl][:, 0:FC * 128].rearrange("p (f c) -> p f c", c=128)
                s.op("pool", lambda e, m=m, wv=wv: e.dma_start(out=wv, in_=wo_r[:, :, m * 128:(m + 1) * 128]),
                     writes=[("w", sl, 0), ("w", sl, 1)], dma=("w", sl, 0))
                for hi, (c0, N, v) in enumerate(tile):
                    ho = hoffs[hi]
                    e2 = ecnt[0] % 2
                    ecnt[0] += 1
                    po = 4 + e2
                    for f in range(FC):
                        s.op("pe", lambda e, wv=wv, f=f, ho=ho, N=N, po=po: e.matmul(
                            bank(po)[:, 0:N], wv[:, f, :], g[:, f, ho:ho + N],
                            start=(f == 0), stop=(f == FC - 1)),
                            reads=[("w", sl, 0), ("w", sl, 1), ("g", f)], writes=[bres(po)])
                    s.op("sp", lambda e, e2=e2, m=m, c0=c0, N=N: e.dma_start(out=xr[e2][:, 0:N], in_=src[m, :, c0:c0 + N]),
                         writes=[("xb", e2)], dma=("xb", e2))
                    s.op("dve", lambda e, e2=e2, po=po, N=N, m=m, v=v: e.scalar_tensor_tensor(
                        out=xo[e2][:, 0:N], in0=bank(po)[:, 0:N], scalar=Gsc[:, m, v:v + 1], in1=xr[e2][:, 0:N],
                        op0=ALU.mult, op1=ALU.add),
                        reads=[bres(po), ("xb", e2), "scal"], writes=[("xb", 2 + e2)])
                    dc = c0 - dst_col0
                    s.op("sp", lambda e, e2=e2, m=m, dc=dc, N=N: e.dma_start(out=dst[m, :, dc:dc + N], in_=xo[e2][:, 0:N]),
                         reads=[("xb", 2 + e2)], writes=[(tag, m)], dma=("xb", 2 + e2))
        mem.reset(m0)

    s.barrier()
    mem.reset(pmark)
    tiles1 = [[(0, 512, 0), (512, 512, 0)], [(1024, 512, 0), (1536, 512, 0)], [(2048, 512, 0), (2560, 256, 1)]]
    if stop_after != "ada" and not skip_ffn1:
        ffn_phase(xin, x1, 0, tiles1, w1i, w1o, scal["A1"], scal["B1"], scal["G1"], "x1")

    if stop_after in ("ada", "ffn1"):
        s.emit()
        return nc

    s.barrier()
    mem.reset(pmark)
    sqacc_a = mem.alloc("sqacc_a", [128, OWN], F32)
    sqacc_c = mem.alloc("sqacc_c", [128, OWN], F32)
    s.op("pool", lambda e: e.memset(sqacc_a[:], 0.0), writes=["sqacc_a"])
    s.op("pool", lambda e: e.memset(sqacc_c[:], 0.0), writes=["sqacc_c"])
    pmark2 = mem.mark()
    hm = mem.alloc("hm", [128, KC, TT], BF16)
    xb = [mem.alloc("xb%d" % i, [128, 512], F32) for i in range(4)]
    sq = [mem.alloc("sq%d" % i, [128, 512], BF16) for i in range(2)]
    rstd = mem.alloc("rstd", [128, 512], F32)
    wbig = mem.alloc("wbig", [128, KC, 512], BF16)
    cosT = mem.alloc("cosT", [128, TT], F32)
    sinT = mem.alloc("sinT", [128, TT], F32)
    s.op("sp", lambda e: e.dma_start(out=cosT[:], in_=cs_d[0]), writes=["cosT"], dma="cos")
    s.op("sp", lambda e: e.dma_start(out=sinT[:], in_=cs_d[1]), writes=["sinT"], dma="sin")
    wtiles = [(i * 512, 512, 0) for i in range(5)] + [(WT, CTX, 1)]
    for (c0, N, v) in wtiles:
        norm_stage(x1, c0, N, scal["A2"], scal["B2"], v,
                   lambda kc, c0=c0, N=N: (hm[:, kc, c0:c0 + N], ("hm", kc)), xb, sq, rstd, 6)
    if stop_after == "projN":
        s.emit()
        return nc
    w_in_r = w_in.rearrange("(k p) c -> p k c", p=128)

    vst = [mem.alloc("vst%d" % i, [128, 512], BF16) for i in range(2)]
    cnt = 0
    for hf in range(2):
        s.op("pool", lambda e, hf=hf: e.dma_start(out=wbig[:], in_=w_in_r[:, :, 2048 + hf * 512:2560 + hf * 512]),
             writes=["wbig0"], dma="wbig0")
        for tg in range(22):
            e2 = cnt % 2
            cnt += 1
            pbk = 4 + e2
            for kc in range(KC):
                s.op("pe", lambda e, tg=tg, hf=hf, kc=kc, pbk=pbk: e.matmul(
                    bank(pbk)[:, :], hm[:, kc, tg * 128:(tg + 1) * 128], wbig[:, kc, :],
                    start=(kc == 0), stop=(kc == KC - 1)),
                    reads=["wbig0", ("hm", kc)], writes=[bres(pbk)])
            s.op("act", lambda e, e2=e2, pbk=pbk: e.activation(out=vst[e2][:], in_=bank(pbk)[:, :], func=AF.Copy),
                 reads=[bres(pbk)], writes=[("vst", e2)])
            s.op("sp", lambda e, e2=e2, tg=tg, hf=hf: e.dma_start(out=vtok[tg, :, hf * 512:(hf + 1) * 512], in_=vst[e2][:]),
                 reads=[("vst", e2)], writes=["vtok"], dma=("vst", e2))

    if stop_after == "projV":
        s.emit()
        return nc
    NSL = 4
    wsm = [wbig[:, :, i * 128:(i + 1) * 128] for i in range(NSL)]
    wsm_cnt = [0]

    def load_w(colbase):
        sl = wsm_cnt[0] % NSL
        wsm_cnt[0] += 1
        s.op("pool", lambda e, sl=sl, colbase=colbase: e.dma_start(out=wsm[sl], in_=w_in_r[:, :, colbase:colbase + 128]),
             reads=[], writes=[("wsm", sl), "wbig0"] if wsm_cnt[0] <= NSL else [("wsm", sl)],
             dma=("wsm", sl))
        return sl

    sqf = [xb[0], xb[1]]
    raw = [xb[2], xb[3]]
    sd = [mem.alloc("sd%d" % i, [128, 512], F32) for i in range(2)]
    qn = [mem.alloc("qn%d" % i, [128, 512], F32) for i in range(2)]
    t1 = [mem.alloc("t1%d" % i, [128, 512], F32) for i in range(2)]
    t2 = [mem.alloc("t2%d" % i, [128, 512], F32) for i in range(2)]
    qr = [mem.alloc("qr%d" % i, [128, 512], BF16) for i in range(2)]
    pcnt = [0]

    def qk_tile(sl, hcol0, N, gcol, cscol0, dst_ap, dres):
        i2 = pcnt[0] % 2
        pcnt[0] += 1
        praw, pss, prot = 4 + i2, 0 + i2, 2 + i2
        for kc in range(KC):
            s.op("pe", lambda e, kc=kc: e.matmul(bank(praw)[:, 0:N], wsm[sl][:, kc, :], hm[:, kc, hcol0:hcol0 + N],
                                                 start=(kc == 0), stop=(kc == KC - 1)),
                 reads=[("wsm", sl), ("hm", kc)], writes=[bres(praw)])
        s.op("act", lambda e: e.activation(out=sqf[i2][:, 0:N], in_=bank(praw)[:, 0:N], func=AF.Square),
             reads=[bres(praw)], writes=[("xb", i2)])
        s.op("dve", lambda e: e.tensor_copy(out=raw[i2][:, 0:N], in_=bank(praw)[:, 0:N]),
             reads=[bres(praw)], writes=[("xb", 2 + i2)])
        s.op("pe", lambda e: e.matmul(bank(pss)[:, 0:N], bones[:], sqf[i2][:, 0:N], start=True, stop=True),
             reads=["bones", ("xb", i2)], writes=[bres(pss)])
        s.op("act", lambda e: e.activation(out=sd[i2][:, 0:N], in_=bank(pss)[:, 0:N], func=AF.Sqrt,
                                           scale=1.0 / 64.0, bias=epsb[:, 0:1]),
             reads=[bres(pss), "epsb"], writes=[("sd", i2)])
        s.op("dve", lambda e: e.reciprocal(out=sd[i2][:, 0:N], in_=sd[i2][:, 0:N]),
             reads=[("sd", i2)], writes=[("sd", i2)])
        s.op("dve", lambda e: e.scalar_tensor_tensor(out=qn[i2][:, 0:N], in0=raw[i2][:, 0:N],
                                                     scalar=consts[:, gcol:gcol + 1], in1=sd[i2][:, 0:N],
                                                     op0=ALU.mult, op1=ALU.mult),
             reads=[("xb", 2 + i2), ("sd", i2), "consts"], writes=[("qn", i2)])
        s.op("pe", lambda e: e.matmul(bank(prot)[:, 0:N], pmat[:], qn[i2][:, 0:N], start=True, stop=True),
             reads=["pmat", ("qn", i2)], writes=[bres(prot)])
        s.op("pool", lambda e: e.tensor_tensor(out=t1[i2][:, 0:N], in0=qn[i2][:, 0:N], in1=cosT[:, cscol0:cscol0 + N],
                                               op=ALU.mult),
             reads=[("qn", i2), "cosT"], writes=[("t1", i2)])
        s.op("dve", lambda e: e.tensor_tensor(out=t2[i2][:, 0:N], in0=bank(prot)[:, 0:N], in1=sinT[:, cscol0:cscol0 + N],
                                              op=ALU.mult),
             reads=[bres(prot), "sinT"], writes=[("t2", i2)])
        s.op("pool", lambda e: e.tensor_tensor(out=qr[i2][:, 0:N], in0=t1[i2][:, 0:N], in1=t2[i2][:, 0:N], op=ALU.add),
             reads=[("t1", i2), ("t2", i2)], writes=[("qr", i2)])
        s.op("sp", lambda e: e.dma_start(out=dst_ap, in_=qr[i2][:, 0:N]), reads=[("qr", i2)], writes=[dres],
             dma=("qr", i2))

    for c in range(8):
        if stop_after == "projK1" and c == 1:
            s.emit()
            return nc
        sl = load_w(1024 + c * 128)
        for (c0, N, v) in wtiles:
            qk_tile(sl, c0, N, C_KG, c0, kT[c, :, c0:c0 + N], ("kT", c))
        sl = load_w(c * 128)
        for i in range(4):
            c0 = OWN0 + i * 512
            qk_tile(sl, c0, 512, C_QG, c0, qT[c, :, i * 512:(i + 1) * 512], ("qT", c))

    if stop_after == "projQK":
        s.emit()
        return nc
    zbuf = mem.alloc("zbuf", [128, OWN + 2], F32)
    acc = mem.alloc("acc", [128, OWN], F32)
    cS = [mem.alloc("cS%d" % i, [128, 512], F32) for i in range(2)]
    zh = mem.alloc("zh", [128, 2], F32)
    yv = sqf
    ysq = raw
    ybf = qr
    ccnt = 0
    for cc in range(8):
        slB = load_w(3072 + cc * 128)
        slC = load_w(4096 + cc * 128)
        slU = load_w(5120 + cc * 128)
        for i in range(4):
            c0 = OWN0 + i * 512
            i2 = ccnt % 2
            ccnt += 1
            pC, pU = 0 + i2, 2 + i2
            for kc in range(KC):
                s.op("pe", lambda e, kc=kc, c0=c0, pC=pC, slC=slC: e.matmul(bank(pC)[:, :], wsm[slC][:, kc, :], hm[:, kc, c0:c0 + 512],
                                                                  start=(kc == 0), stop=(kc == KC - 1)),
                     reads=[("wsm", slC), ("hm", kc)], writes=[bres(pC)])
            for kc in range(KC):
                s.op("pe", lambda e, kc=kc, c0=c0, pU=pU, slU=slU: e.matmul(bank(pU)[:, :], wsm[slU][:, kc, :], hm[:, kc, c0:c0 + 512],
                                                                  start=(kc == 0), stop=(kc == KC - 1)),
                     reads=[("wsm", slU), ("hm", kc)], writes=[bres(pU)])
            s.op("act", lambda e, i2=i2, pC=pC: e.activation(out=cS[i2][:], in_=bank(pC)[:, :], func=AF.Copy),
                 reads=[bres(pC)], writes=[("cS", i2)])
            s.op("dve", lambda e, i2=i2, pU=pU, i=i: e.tensor_tensor(out=zbuf[:, 1 + i * 512:1 + (i + 1) * 512],
                                                                    in0=bank(pU)[:, :], in1=cS[i2][:], op=ALU.mult),
                 reads=[bres(pU), ("cS", i2)], writes=["zbuf"])
        pH = 6
        for hi, col in enumerate((OWN0 - 1, OWN0 + OWN)):
            for wi_, sl_ in enumerate((slC, slU)):
                for kc in range(KC):
                    s.op("pe", lambda e, kc=kc, col=col, hi=hi, wi_=wi_, sl_=sl_: e.matmul(
                        bank(pH)[:, wi_ * 2 + hi:wi_ * 2 + hi + 1], wsm[sl_][:, kc, :], hm[:, kc, col:col + 1],
                        start=(kc == 0), stop=(kc == KC - 1)),
                        reads=[("wsm", sl_), ("hm", kc)], writes=[bres(pH)])
        s.op("act", lambda e: e.activation(out=zh[:], in_=bank(pH)[:, 0:2], func=AF.Copy), reads=[bres(pH)], writes=["zh"])
        s.op("dve", lambda e: e.tensor_tensor(out=zh[:], in0=bank(pH)[:, 2:4], in1=zh[:], op=ALU.mult),
             reads=[bres(pH), "zh"], writes=["zh"])
        s.op("dve", lambda e: e.tensor_tensor(out=zbuf[:, 0:OWN + 2:OWN + 1], in0=zh[:], in1=consts[:, C_HALO:C_HALO + 2],
                                              op=ALU.mult),
             reads=["zh", "consts"], writes=["zbuf"])
        cw = C_CW + cc * 3
        s.op("dve", lambda e, cw=cw, cc=cc: e.tensor_scalar(out=acc[:], in0=zbuf[:, 1:OWN + 1], scalar1=consts[:, cw + 1:cw + 2],
                                                           scalar2=consts[:, C_CB + cc:C_CB + cc + 1], op0=ALU.mult, op1=ALU.add),
             reads=["zbuf", "consts"], writes=["acc"])
        s.op("dve", lambda e, cw=cw: e.scalar_tensor_tensor(out=acc[:], in0=zbuf[:, 0:OWN], scalar=consts[:, cw:cw + 1],
                                                            in1=acc[:], op0=ALU.mult, op1=ALU.add),
             reads=["zbuf", "consts", "acc"], writes=["acc"])
        s.op("dve", lambda e, cw=cw: e.scalar_tensor_tensor(out=acc[:], in0=zbuf[:, 2:OWN + 2], scalar=consts[:, cw + 2:cw + 3],
                                                           in1=acc[:], op0=ALU.mult, op1=ALU.add),
             reads=["zbuf", "consts", "acc"], writes=["acc"])
        for i in range(4):
            c0 = OWN0 + i * 512
            i2 = ccnt % 2
            ccnt += 1
            pB = 4 + i2
            for kc in range(KC):
                s.op("pe", lambda e, kc=kc, c0=c0, pB=pB, slB=slB: e.matmul(bank(pB)[:, :], wsm[slB][:, kc, :], hm[:, kc, c0:c0 + 512],
                                                                  start=(kc == 0), stop=(kc == KC - 1)),
                     reads=[("wsm", slB), ("hm", kc)], writes=[bres(pB)])
            s.op("dve", lambda e, i2=i2, pB=pB, i=i: e.tensor_tensor(out=yv[i2][:], in0=bank(pB)[:, :],
                                                                    in1=acc[:, i * 512:(i + 1) * 512], op=ALU.mult),
                 reads=[bres(pB), "acc"], writes=[("xb", i2)])
            s.op("act", lambda e, i2=i2: e.activation(out=ysq[i2][:], in_=yv[i2][:], func=AF.Square),
                 reads=[("xb", i2)], writes=[("xb", 2 + i2)])
            s.op("pool", lambda e, i2=i2, i=i: e.tensor_tensor(out=sqacc_c[:, i * 512:(i + 1) * 512],
                                                              in0=sqacc_c[:, i * 512:(i + 1) * 512], in1=ysq[i2][:], op=ALU.add),
                 reads=[("xb", 2 + i2), "sqacc_c"], writes=["sqacc_c"])
            s.op("act", lambda e, i2=i2, cc=cc: e.activation(out=ybf[i2][:], in_=yv[i2][:], func=AF.Identity,
                                                            scale=consts[:, C_ONC + cc:C_ONC + cc + 1]),
                 reads=[("xb", i2), "consts"], writes=[("qr", i2)])
            s.op("sp", lambda e, i2=i2, cc=cc, i=i: e.dma_start(out=aoT[8 + cc, :, i * 512:(i + 1) * 512], in_=ybf[i2][:]),
                 reads=[("qr", i2)], writes=[("aoT", 8 + cc)], dma=("qr", i2))

    if stop_after == "proj":
        s.emit()
        return nc

    s.barrier()
    mem.reset(pmark2)
    KTs = [mem.alloc("KT%d" % i, [128, TT], BF16) for i in range(2)]
    QTs = [mem.alloc("QT%d" % i, [128, OWN], BF16) for i in range(2)]
    Vs = [mem.alloc("V%d" % i, [128, 22, 128], BF16) for i in range(2)]
    Bstd = [mem.alloc("Bstd%d" % i, [128, 2 * 5 * 128], F32) for i in range(2)]
    Bspec = [mem.alloc("Bspec%d" % i, [128, 2 * 22 * 128], F32) for i in range(2)]
    aob = [mem.alloc("aob%d" % i, [128, OWN], BF16) for i in range(2)]
    tmp = [mem.alloc("tmp%d" % i, [128, 768], F32) for i in range(2)]
    pt = [mem.alloc("pt%d" % i, [128, 1024], BF16) for i in range(2)]
    rden = [mem.alloc("rden%d" % i, [128, 128], F32) for i in range(2)]
    at = [mem.alloc("at%d" % i, [128, 128], F32) for i in range(2)]
    asq = [mem.alloc("asq%d" % i, [128, 128], F32) for i in range(2)]
    vtok_r = vtok.rearrange("g p f -> p g f")
    hcnt = 0
    bcnt = 0
    for c in range(8):
        c2 = c % 2
        s.op("sp", lambda e, c=c, c2=c2: e.dma_start(out=KTs[c2][:], in_=kT[c]), reads=[("kT", c)], writes=[("KT", c2)],
             dma=("KT", c2))
        s.op("sp", lambda e, c=c, c2=c2: e.dma_start(out=QTs[c2][:], in_=qT[c]), reads=[("qT", c)], writes=[("QT", c2)],
             dma=("QT", c2))
        s.op("sp", lambda e, c=c, c2=c2: e.dma_start(out=Vs[c2][:], in_=vtok_r[:, :, c * 128:(c + 1) * 128]),
             reads=["vtok"], writes=[("V", c2)], dma=("V", c2))
        s.op("sp", lambda e, c=c, c2=c2: e.dma_start(out=Bstd[c2][:], in_=tstd_d[c]), writes=[("Bstd", c2)], dma=("Bstd", c2))
        s.op("sp", lambda e, c=c, c2=c2: e.dma_start(out=Bspec[c2][:], in_=tspec_d[c]), writes=[("Bspec", c2)],
             dma=("Bspec", c2))
        for j in range(16):
            if j == 0:
                pl, tb, so = list(range(0, 6)), Bspec, 0
            elif j == 1:
                pl, tb, so = list(range(1, 6)), Bspec, 6
            elif j == 14:
                pl, tb, so = list(range(14, 19)), Bspec, 11
            elif j == 15:
                pl, tb, so = list(range(14, 20)), Bspec, 16
            else:
                pl, tb, so = list(range(j, j + 5)), Bstd, 0
            nsl = 22 if tb is Bspec else 5
            tres = ("Bspec", c2) if tb is Bspec else ("Bstd", c2)
            L = len(pl)
            b2 = bcnt % 2
            bcnt += 1
            pnum, pden = 4 + b2, 6 + b2
            for hh in range(2):
                h2 = hcnt % 2
                hcnt += 1
                hp0, hp1 = hh * 64, hh * 64 + 64
                stt = st_ps[h2]
                sres = [bres(2 * h2), bres(2 * h2 + 1)]
                for si, p in enumerate(pl + [20, 21]):
                    kcol = p * 128
                    s.op("pe", lambda e, si=si, kcol=kcol, j=j, hp0=hp0, hp1=hp1, stt=stt, c2=c2: e.matmul(
                        stt[:, si * 128:(si + 1) * 128], KTs[c2][hp0:hp1, kcol:kcol + 128],
                        QTs[c2][hp0:hp1, j * 128:(j + 1) * 128], start=True, stop=True),
                        reads=[("KT", c2), ("QT", c2)], writes=[sres[si // 4]])
                tbv = tb[c2][:, (hh * nsl + so) * 128:(hh * nsl + so + L) * 128]
                s.op("dve", lambda e, h2=h2, stt=stt, L=L, tbv=tbv: e.scalar_tensor_tensor(
                    out=tmp[h2][:, 0:L * 128], in0=stt[:, 0:L * 128], scalar=0.125, in1=tbv, op0=ALU.mult, op1=ALU.add),
                    reads=sres + [tres], writes=[("tmp", h2)])
                s.op("act", lambda e, h2=h2, L=L: e.activation(out=pt[h2][:, 0:L * 128], in_=tmp[h2][:, 0:L * 128], func=AF.Exp),
                     reads=[("tmp", h2)], writes=[("pt", h2)])
                s.op("act", lambda e, h2=h2, L=L, stt=stt: e.activation(out=pt[h2][:, L * 128:(L + 2) * 128],
                                                                       in_=stt[:, L * 128:(L + 2) * 128], func=AF.Exp, scale=0.125),
                     reads=sres, writes=[("pt", h2)])
                for si, p in enumerate(pl + [20, 21]):
                    first, last = (si == 0), (si == L + 1)
                    s.op("pe", lambda e, si=si, p=p, hh=hh, h2=h2, hp0=hp0, hp1=hp1, first=first, last=last, pnum=pnum, c2=c2: e.matmul(
                        bank(pnum)[hp0:hp1, 0:128], Vs[c2][:, p, hh * 64:(hh + 1) * 64], pt[h2][:, si * 128:(si + 1) * 128],
                        start=first, stop=last),
                        reads=[("V", c2), ("pt", h2)], writes=[bres(pnum)])
                    s.op("pe", lambda e, si=si, h2=h2, hp0=hp0, hp1=hp1, first=first, last=last, pden=pden: e.matmul(
                        bank(pden)[hp0:hp1, 0:128], ones_bf[:, 0:64], pt[h2][:, si * 128:(si + 1) * 128],
                        start=first, stop=last),
                        reads=["ones_bf", ("pt", h2)], writes=[bres(pden)])
            s.op("dve", lambda e, b2=b2, pden=pden: e.reciprocal(out=rden[b2][:], in_=bank(pden)[:, 0:128]),
                 reads=[bres(pden)], writes=[("rden", b2)])
            s.op("dve", lambda e, b2=b2, pnum=pnum: e.tensor_tensor(out=at[b2][:], in0=bank(pnum)[:, 0:128], in1=rden[b2][:],
                                                                   op=ALU.mult),
                 reads=[bres(pnum), ("rden", b2)], writes=[("at", b2)])
            s.op("act", lambda e, b2=b2: e.activation(out=asq[b2][:], in_=at[b2][:], func=AF.Square),
                 reads=[("at", b2)], writes=[("asq", b2)])
            s.op("pool", lambda e, b2=b2, j=j: e.tensor_tensor(out=sqacc_a[:, j * 128:(j + 1) * 128],
                                                              in0=sqacc_a[:, j * 128:(j + 1) * 128], in1=asq[b2][:], op=ALU.add),
                 reads=[("asq", b2), "sqacc_a"], writes=["sqacc_a"])
            s.op("act", lambda e, b2=b2, j=j, c=c, c2=c2: e.activation(out=aob[c2][:, j * 128:(j + 1) * 128], in_=at[b2][:],
                                                                      func=AF.Identity, scale=consts[:, C_ONA + c:C_ONA + c + 1]),
                 reads=[("at", b2), "consts"], writes=[("aob", c2)])
        s.op("sp", lambda e, c=c, c2=c2: e.dma_start(out=aoT[c], in_=aob[c2][:]), reads=[("aob", c2)], writes=[("aoT", c)],
             dma=("aob", c2))

    if stop_after == "attn":
        s.emit()
        return nc

    s.barrier()
    mem.reset(pmark2)
    wo = mem.alloc("wo", [128, KC, D], BF16)
    w_out_r = w_out.rearrange("(k p) c -> p k c", p=128)
    for i in range(4):
        s.op("pool", lambda e, i=i: e.dma_start(out=wo[:, i * 4:(i + 1) * 4, :], in_=w_out_r[:, i * 4:(i + 1) * 4, :]),
             writes=[("wo", i)], dma=("wo", i))
    rs = [mem.alloc("rs%d" % i, [128, OWN], F32) for i in range(2)]
    for bi, sqa in enumerate((sqacc_a, sqacc_c)):
        rname = "sqacc_a" if bi == 0 else "sqacc_c"
        for i in range(4):
            pbk = i % 2
            s.op("pe", lambda e, i=i, sqa=sqa, pbk=pbk: e.matmul(bank(pbk)[:, :], ones_f[:], sqa[:, i * 512:(i + 1) * 512],
                                                                start=True, stop=True),
                 reads=["ones_f", rname], writes=[bres(pbk)])
            s.op("act", lambda e, i=i, bi=bi, pbk=pbk: e.activation(out=rs[bi][:, i * 512:(i + 1) * 512], in_=bank(pbk)[:, :],
                                                                   func=AF.Sqrt, scale=1.0 / 1024.0, bias=epsb[:, 0:1]),
                 reads=[bres(pbk), "epsb"], writes=[("rs", bi)])
        s.op("dve", lambda e, bi=bi: e.reciprocal(out=rs[bi][:], in_=rs[bi][:]), reads=[("rs", bi)], writes=[("rs", bi)])
    aot = [mem.alloc("aot%d" % i, [128, KC, 512], BF16) for i in range(2)]
    x1t = [mem.alloc("x1t%d" % i, [128, 512], F32) for i in range(2)]
    u1 = [mem.alloc("u1%d" % i, [128, 512], F32) for i in range(2)]
    u2 = [mem.alloc("u2%d" % i, [128, 512], F32) for i in range(2)]
    u3 = [mem.alloc("u3%d" % i, [128, 512], F32) for i in range(2)]
    aoT_r = aoT.rearrange("k p t -> p k t")
    mcnt = 0
    for n in range(4):
        n2 = n % 2
        s.op("sp", lambda e, n=n, n2=n2: e.dma_start(out=aot[n2][:], in_=aoT_r[:, :, n * 512:(n + 1) * 512]),
             reads=[("aoT", k) for k in range(KC)], writes=[("aot", n2)], dma=("aot", n2))
        for m in range(KC):
            i2 = mcnt % 2
            mcnt += 1
            pA, pC = 0 + i2, 2 + i2
            for kc in range(8):
                s.op("pe", lambda e, kc=kc, m=m, n2=n2, pA=pA: e.matmul(bank(pA)[:, :], wo[:, kc, m * 128:(m + 1) * 128],
                                                                        aot[n2][:, kc, :], start=(kc == 0), stop=(kc == 7)),
                     reads=[("wo", kc // 4), ("aot", n2)], writes=[bres(pA)])
            for kc in range(8, 16):
                s.op("pe", lambda e, kc=kc, m=m, n2=n2, pC=pC: e.matmul(bank(pC)[:, :], wo[:, kc, m * 128:(m + 1) * 128],
                                                                        aot[n2][:, kc, :], start=(kc == 8), stop=(kc == 15)),
                     reads=[("wo", kc // 4), ("aot", n2)], writes=[bres(pC)])
            s.op("sp", lambda e, i2=i2, m=m, n=n: e.dma_start(out=x1t[i2][:], in_=x1[m, :, OWN0 + n * 512:OWN0 + (n + 1) * 512]),
                 reads=[("x1", m)], writes=[("x1t", i2)], dma=("x1t", i2))
            s.op("dve", lambda e, i2=i2, pA=pA, n=n: e.tensor_tensor(out=u1[i2][:], in0=bank(pA)[:, :],
                                                                    in1=rs[0][:, n * 512:(n + 1) * 512], op=ALU.mult),
                 reads=[bres(pA), ("rs", 0)], writes=[("u1", i2)])
            s.op("dve", lambda e, i2=i2, pC=pC, n=n: e.tensor_tensor(out=u2[i2][:], in0=bank(pC)[:, :],
                                                                    in1=rs[1][:, n * 512:(n + 1) * 512], op=ALU.mult),
                 reads=[bres(pC), ("rs", 1)], writes=[("u2", i2)])
            s.op("pool", lambda e, i2=i2: e.tensor_tensor(out=u3[i2][:], in0=u1[i2][:], in1=u2[i2][:], op=ALU.add),
                 reads=[("u1", i2), ("u2", i2)], writes=[("u3", i2)])
            s.op("dve", lambda e, i2=i2, m=m: e.scalar_tensor_tensor(out=u3[i2][:], in0=u3[i2][:], scalar=scal["GM"][:, m, 0:1],
                                                                     in1=x1t[i2][:], op0=ALU.mult, op1=ALU.add),
                 reads=[("u3", i2), ("x1t", i2), "scal"], writes=[("u3", i2)])
            s.op("sp", lambda e, i2=i2, m=m, n=n: e.dma_start(out=x2[m, :, n * 512:(n + 1) * 512], in_=u3[i2][:]),
                 reads=[("u3", i2)], writes=[("x2", m)], dma=("u3", i2))

    if stop_after == "mix":
        s.emit()
        return nc

    s.barrier()
    mem.reset(pmark)
    tiles2 = [[(0, 512, 0), (512, 512, 0)], [(1024, 512, 0), (1536, 512, 0)]]
    ffn_phase(x2, yout, 0, tiles2, w2i, w2o, scal["A3"], scal["B3"], scal["G3"], "yout")
    s.emit()
    return nc


def _fm(vec, nchunk):
    return np.ascontiguousarray(np.asarray(vec, np.float32).reshape(nchunk, 128).T)


def _bias_table(rpb, core, j, pairs):
    H = rpb.shape[0]
    out = np.full((H, len(pairs), 128, 128), NEG, np.float32)
    rows = 256
    qrows = [32 * core + 2 * j, 32 * core + 2 * j + 1]
    cols = np.arange(64)
    cstart = np.clip(cols - 8, 0, 64 - 16)
    for si, p in enumerate(pairs):
        for ki in range(2):
            kr = 32 * core - 4 + 2 * p + ki
            if kr < 0 or kr >= rows:
                continue
            for qi, qr in enumerate(qrows):
                rs = min(max(qr - 4, 0), rows - 8)
                if not (rs <= kr < rs + 8):
                    continue
                dr = kr - qr + 7
                kcg, qcg = np.meshgrid(cols, cols, indexing="ij")
                valid = (kcg >= cstart[qcg]) & (kcg < cstart[qcg] + 16)
                dc = kcg - qcg + 15
                dcc = np.clip(dc, 0, 30)
                blk = np.where(valid[None], rpb[:, dr][:, dcc], np.float32(NEG))
                out[:, si, ki * 64:(ki + 1) * 64, qi * 64:(qi + 1) * 64] = blk
    return out


def _prep_core(core, inp, shared):
    x = inp["x"][0]
    r0 = 32 * core - 4
    win = np.zeros((WROWS * GW, D), np.float32)
    lo, hi = max(r0, 0), min(r0 + WROWS, 256)
    win[(lo - r0) * GW:(hi - r0) * GW] = x[lo * GW:hi * GW]
    xin = np.concatenate([win, inp["ctx"][0]], axis=0)
    xin = np.ascontiguousarray(xin.T.reshape(KC, 128, TT))
    consts = shared["consts"].copy()
    consts[:, C_HALO] = 0.0 if core == 0 else 1.0
    consts[:, C_HALO + 1] = 0.0 if core == NCORES - 1 else 1.0
    t = np.arange(WT)
    prow = (r0 + t // GW).astype(np.float32)
    pcol = (t % GW).astype(np.float32)
    p = np.arange(128)
    d = p % 64
    inv = (10000.0 ** (-(np.arange(16, dtype=np.float32)) / 16)).astype(np.float32)
    invp = inv[d % 16]
    pos = np.where((d // 32 == 0)[:, None], prow[None, :], pcol[None, :]).astype(np.float32)
    ang = (pos * invp[:, None]).astype(np.float32)
    cs = np.zeros((2, 128, TT), np.float32)
    cs[0, :, :WT] = np.cos(ang)
    cs[1, :, :WT] = np.sin(ang)
    cs[0, :, WT:] = 1.0
    rpb = inp["rpb"][0]
    spec = []
    for j, pairs in ((0, list(range(0, 6))), (1, list(range(1, 6))), (14, list(range(14, 19))), (15, list(range(14, 20)))):
        spec.append(_bias_table(rpb, core, j, pairs))
    spec = np.concatenate(spec, axis=1)
    tspec = spec.reshape(8, 2, 22, 128, 128).transpose(0, 3, 1, 2, 4).reshape(8, 128, 2 * 22 * 128)
    m = {"xin": xin, "consts": consts, "cossin": cs, "tspec": np.ascontiguousarray(tspec)}
    return m


def _prep_shared(inp):
    consts = np.zeros((128, NCONST), np.float32)
    consts[:, C_BADA:C_BADA + 144] = _fm(inp["b_ada"][0], 144)
    consts[:, C_FF1N:C_FF1N + 16] = _fm(inp["ff1_norm"][0], 16)
    consts[:, C_MIXN:C_MIXN + 16] = _fm(inp["mix_norm"][0], 16)
    consts[:, C_FF2N:C_FF2N + 16] = _fm(inp["ff2_norm"][0], 16)
    consts[:, C_QG] = np.tile(inp["q_norm"][0], 2)
    consts[:, C_KG] = np.tile(inp["k_norm"][0], 2)
    consts[:, C_ONA:C_ONA + 8] = _fm(inp["out_norm_attn"][0], 8)
    consts[:, C_ONC:C_ONC + 8] = _fm(inp["out_norm_conv"][0], 8)
    cw = inp["conv_w"][0]
    for cc in range(8):
        for jj in range(3):
            consts[:, C_CW + cc * 3 + jj] = cw[jj, cc * 128:(cc + 1) * 128]
    consts[:, C_CB:C_CB + 8] = _fm(inp["conv_b"][0], 8)
    c2 = np.stack([_fm(inp["c"][0], 16), _fm(inp["c_ctx"], 16)], axis=-1)
    consts[:, C_C2:C_C2 + 32] = c2.reshape(128, 32)
    bones = np.zeros((128, 128), np.float32)
    bones[:64, :64] = 1.0
    bones[64:, 64:] = 1.0
    pm = np.zeros((128, 128), np.float32)
    for mm in range(128):
        if (mm % 32) < 16:
            pm[mm + 16, mm] = -1.0
        else:
            pm[mm - 16, mm] = 1.0
    std = _bias_table(inp["rpb"][0], 1, 5, list(range(5, 10)))
    tstd = std.reshape(8, 2, 5, 128, 128).transpose(0, 3, 1, 2, 4).reshape(8, 128, 2 * 5 * 128)
    sh = {
        "consts": consts, "bones": bones, "pmat": pm, "tstd": np.ascontiguousarray(tstd),
        "w_ada": np.ascontiguousarray(inp["w_ada"][0]),
        "ff1_w_in": np.ascontiguousarray(inp["ff1_w_in"][0]), "ff1_w_out": np.ascontiguousarray(inp["ff1_w_out"][0]),
        "w_in": np.ascontiguousarray(inp["w_in"][0]), "w_out": np.ascontiguousarray(inp["w_out"][0]),
        "ff2_w_in": np.ascontiguousarray(inp["ff2_w_in"][0]), "ff2_w_out": np.ascontiguousarray(inp["ff2_w_out"][0]),
    }
    return sh


def make_in_maps(inp):
    inp = {k: np.asarray(v) for k, v in inp.items()}
    sh = _prep_shared(inp)
    maps = []
    for core in range(NCORES):
        m = dict(sh)
        m.update(_prep_core(core, inp, sh))
        maps.append(m)
    return maps


def kernel(**inputs):
    maps = make_in_maps(inputs)
    nc = build_nc()
    res = run_bass_kernel_spmd(nc, maps, core_ids=list(range(NCORES)))
    outs = []
    for r in res.results:
        y = np.asarray(r["yout"]).reshape(D, OWN)
        outs.append(y.T)
    out = np.concatenate(outs, axis=0).reshape(1, 16384, D).astype(np.float32)
    return out
```

```python
import contextlib
import numpy as np
import concourse.bass as bass
import concourse.mybir as mybir
from concourse.bass_utils import run_bass_kernel_spmd

F32 = mybir.dt.float32
BF16 = mybir.dt.bfloat16
ALU = mybir.AluOpType
AF = mybir.ActivationFunctionType

NCORES = 8
D = 2048
KC = 16
DFF = 5632
FC = 44
GW = 64
WROWS = 40
WT = WROWS * GW
CTX = 256
TT = WT + CTX
OWN0 = 4 * GW
OWN = 2048
EPS = 1e-6
NEG = -30000.0

C_BADA = 0
C_FF1N = C_BADA + 144
C_MIXN = C_FF1N + 16
C_FF2N = C_MIXN + 16
C_QG = C_FF2N + 16
C_KG = C_QG + 1
C_ONA = C_KG + 1
C_ONC = C_ONA + 8
C_CW = C_ONC + 8
C_CB = C_CW + 24
C_HALO = C_CB + 8
C_C2 = C_HALO + 2
NCONST = C_C2 + 32

ENGS = ("pe", "act", "dve", "pool", "sp")


class _Op:
    __slots__ = ("eng", "fn", "deps", "inc", "dma", "idx", "epoch")

    def __init__(self, eng, fn, dma, epoch):
        self.eng = eng
        self.fn = fn
        self.deps = {}
        self.inc = False
        self.dma = dma
        self.idx = -1
        self.epoch = epoch


class Sched:
    def __init__(self, nc):
        self.nc = nc
        self.ops = {e: [] for e in ENGS}
        self.lastw = {}
        self.readers = {}
        self.dma_cnt = {}
        self.known = {e: {} for e in ENGS}
        self.pending = {e: {} for e in ENGS}
        self.epoch = 0

    def _tok_epoch(self, key):
        return key[2] if key[0] == "e" else key[1][0]

    def _add_dep(self, op, tok, kind, force=False):
        key, val = tok
        if not force and self._tok_epoch(key) < self.epoch:
            return
        if key[0] == "e" and key[1] == op.eng:
            if op.eng == "pe":
                return
        if self.known[op.eng].get(key, -1) >= val:
            return
        if op.deps.get(key, -1) < val:
            op.deps[key] = val

    def op(self, eng, fn, reads=(), writes=(), dma=None):
        if dma is not None:
            dma = (self.epoch, dma)
        o = _Op(eng, fn, dma, self.epoch)
        for key, val in self.pending[eng].items():
            self._add_dep(o, (key, val), "raw", force=True)
        self.pending[eng] = {}
        for r in reads:
            t = self.lastw.get(r)
            if t is not None:
                self._add_dep(o, t, "raw")
            if isinstance(r, tuple) and r[0] == "ps":
                for t in self.readers.get(r, ()):
                    if t[0][0] == "e" and t[0][1] != eng:
                        self._add_dep(o, t, "raw")
        for w in writes:
            t = self.lastw.get(w)
            if t is not None:
                self._add_dep(o, t, "waw")
            for t in self.readers.get(w, ()):
                self._add_dep(o, t, "war")
        o.idx = len(self.ops[eng])
        self.ops[eng].append(o)
        for key, val in o.deps.items():
            self.known[eng][key] = val
            if key[0] == "e":
                self.ops[key[1]][val].inc = True
        if dma is not None:
            c = self.dma_cnt.get(dma, 0) + 16
            self.dma_cnt[dma] = c
            tok = (("d", dma), c)
        else:
            tok = (("e", eng, self.epoch), o.idx)
        for r in reads:
            self.readers.setdefault(r, []).append(tok)
        for w in writes:
            self.lastw[w] = tok
            self.readers[w] = []
        return o

    def barrier(self):
        toks = {}
        for e in ENGS:
            if self.ops[e]:
                last = self.ops[e][-1]
                if last.dma is None and last.fn is not None:
                    toks[("e", e, last.epoch)] = last.idx
                else:
                    for o in reversed(self.ops[e]):
                        if o.dma is None and o.fn is not None:
                            toks[("e", e, o.epoch)] = o.idx
                            break
        for k, c in self.dma_cnt.items():
            if k[0] == self.epoch:
                toks[("d", k)] = c
        for e in ENGS:
            for key, val in toks.items():
                if key[0] == "e" and key[1] == e:
                    continue
                if self.pending[e].get(key, -1) < val:
                    self.pending[e][key] = val
        self.epoch += 1

    def emit(self):
        nc = self.nc
        self.barrier()
        self.op("sp", None)
        with contextlib.ExitStack() as st:
            cum = {}
            used = set()
            for e in ENGS:
                cnt = {}
                arr = []
                for o in self.ops[e]:
                    if o.inc and o.dma is None and o.fn is not None:
                        cnt[o.epoch] = cnt.get(o.epoch, 0) + 1
                        used.add((e, o.epoch))
                    arr.append(cnt.get(o.epoch, 0))
                cum[e] = arr
            esem = {k: st.enter_context(nc.semaphore("s_%s_%d" % k)) for k in sorted(used)}
            dsem = {k: st.enter_context(nc.semaphore("d_%d" % i))
                    for i, k in enumerate(self.dma_cnt)}
            self.nsem = len(esem) + len(dsem)
            block = st.enter_context(nc.Block())

            def run(ename, engine):
                for o in self.ops[ename]:
                    for key, val in o.deps.items():
                        if key[0] == "e":
                            v = cum[key[1]][val]
                            if v > 0:
                                engine.wait_ge(esem[(key[1], key[2])], v)
                        else:
                            engine.wait_ge(dsem[key[1]], val)
                    if o.fn is None:
                        continue
                    ins = o.fn(engine)
                    if o.dma is not None:
                        ins.then_inc(dsem[o.dma], 16)
                    elif o.inc:
                        ins.then_inc(esem[(ename, o.epoch)], 1)

            @block.tensor
            def _(eng):
                run("pe", eng)

            @block.scalar
            def _(eng):
                run("act", eng)

            @block.vector
            def _(eng):
                run("dve", eng)

            @block.gpsimd
            def _(eng):
                run("pool", eng)

            @block.sync
            def _(eng):
                run("sp", eng)


def _esize(dt):
    return 4 if dt == F32 else 2


class Mem:
    def __init__(self, nc):
        self.nc = nc
        self.base = (nc.sbuf_base + 63) // 64 * 64
        self.top = nc.sbuf_top
        self.off = self.base
        self.n = 0

    def mark(self):
        return self.off

    def reset(self, m):
        self.off = m

    def alloc(self, name, shape, dt):
        sz = int(np.prod(shape[1:])) * _esize(dt)
        sz = (sz + 63) // 64 * 64
        assert self.off + sz <= self.top, ("SBUF overflow", name, self.off, sz, self.top)
        self.n += 1
        t = self.nc.alloc_sbuf_tensor_at("%s_%d" % (name, self.n), list(shape), dt, offset=self.off)
        self.off += sz
        return t


def build_nc(stop_after=None, debug=False, skip_ffn1=False):
    nc = bass.Bass("TRN2", target_bir_lowering=False)
    dk = "ExternalOutput" if debug else "Internal"

    def din(name, shape, dt=F32):
        return nc.dram_tensor(name, list(shape), dt, kind="ExternalInput").ap()

    xin = din("xin", [KC, 128, TT])
    consts_d = din("consts", [128, NCONST])
    bones_d = din("bones", [128, 128])
    pmat_d = din("pmat", [128, 128])
    cs_d = din("cossin", [2, 128, TT])
    tstd_d = din("tstd", [8, 128, 2 * 5 * 128])
    tspec_d = din("tspec", [8, 128, 2 * 22 * 128])
    w_ada = din("w_ada", [D, 9 * D])
    w1i = din("ff1_w_in", [D, 2 * DFF])
    w1o = din("ff1_w_out", [DFF, D])
    w_in = din("w_in", [D, 6144])
    w_out = din("w_out", [D, D])
    w2i = din("ff2_w_in", [D, 2 * DFF])
    w2o = din("ff2_w_out", [DFF, D])
    yout = nc.dram_tensor("yout", [KC, 128, OWN], F32, kind="ExternalOutput").ap()

    x1 = nc.dram_tensor("x1s", [KC, 128, TT], F32, kind=("ExternalInput" if skip_ffn1 else dk)).ap()
    x2 = nc.dram_tensor("x2s", [KC, 128, OWN], F32, kind=dk).ap()
    qT = nc.dram_tensor("qTs", [8, 128, OWN], BF16, kind=dk).ap()
    kT = nc.dram_tensor("kTs", [8, 128, TT], BF16, kind=dk).ap()
    vtok = nc.dram_tensor("vtoks", [22, 128, 1024], BF16, kind=dk).ap()
    aoT = nc.dram_tensor("aoTs", [KC, 128, OWN], BF16, kind=dk).ap()
    modsd = nc.dram_tensor("modsd", [128, 288], F32, kind=dk).ap()

    s = Sched(nc)
    mem = Mem(nc)

    st_ps = [nc.alloc_psum_tensor("stps%d" % i, [128, 1024], F32) for i in range(2)]
    pb_ps = [nc.alloc_psum_tensor("pbps%d" % i, [128, 512], F32) for i in range(4)]

    def bank(k):
        if k < 4:
            return st_ps[k // 2][:, (k % 2) * 512:(k % 2) * 512 + 512]
        return pb_ps[k - 4][:, :]

    def bres(k):
        return ("ps", k)

    consts = mem.alloc("consts", [128, NCONST], F32)
    ones_bf = mem.alloc("ones_bf", [128, 128], BF16)
    ones_f = mem.alloc("ones_f", [128, 128], F32)
    bones = mem.alloc("bones", [128, 128], F32)
    pmat = mem.alloc("pmat", [128, 128], F32)
    mods = mem.alloc("mods", [128, 144, 2], F32)
    sc_names = ["A1", "B1", "G1", "A2", "B2", "GM", "A3", "B3", "G3"]
    scal = {n: mem.alloc("sc" + n, [128, 16, 2], F32) for n in sc_names}
    epsb = mem.alloc("epsb", [128, 1], F32)

    s.op("sp", lambda e: e.dma_start(out=consts[:], in_=consts_d), writes=["consts"], dma="c0")
    s.op("sp", lambda e: e.dma_start(out=bones[:], in_=bones_d), writes=["bones"], dma="c1")
    s.op("sp", lambda e: e.dma_start(out=pmat[:], in_=pmat_d), writes=["pmat"], dma="c2")
    s.op("pool", lambda e: e.memset(ones_bf[:], 1.0), writes=["ones_bf"])
    s.op("pool", lambda e: e.memset(ones_f[:], 1.0), writes=["ones_f"])
    s.op("pool", lambda e: e.memset(epsb[:], EPS), writes=["epsb"])

    pmark = mem.mark()

    sc_c = mem.alloc("sc_c", [128, 16, 2], BF16)
    wslots = [mem.alloc("wslot%d" % i, [128, 16, 512], BF16) for i in range(3)]
    s.op("act", lambda e: e.activation(out=sc_c[:], in_=consts[:, C_C2:C_C2 + 32].rearrange("p (k v) -> p k v", v=2),
                                       func=AF.Silu), reads=["consts"], writes=["sc_c"])
    w_ada_r = w_ada.rearrange("(k p) c -> p k c", p=128)
    MB = 4
    for t in range(36):
        sl = t % 3
        s.op("pool", lambda e, t=t, sl=sl: e.dma_start(out=wslots[sl][:], in_=w_ada_r[:, :, t * 512:(t + 1) * 512]),
             writes=[("w", sl, 0)], dma=("w", sl, 0))
        for jj in range(4):
            j = t * 4 + jj
            for kc in range(KC):
                s.op("pe", lambda e, sl=sl, jj=jj, j=j, kc=kc: e.matmul(
                    bank(MB)[:, 2 * j:2 * j + 2], wslots[sl][:, kc, jj * 128:(jj + 1) * 128], sc_c[:, kc, :],
                    start=(kc == 0), stop=(kc == KC - 1)),
                    reads=[("w", sl, 0), "sc_c"], writes=[bres(MB)])
    mps = bank(MB)[:, 0:288].rearrange("p (j v) -> p j v", v=2)
    for v in range(2):
        s.op("dve", lambda e, v=v: e.tensor_tensor(out=mods[:, :, v], in0=mps[:, :, v], in1=consts[:, C_BADA:C_BADA + 144],
                                                  op=ALU.add), reads=[bres(MB), "consts"], writes=["mods"])

    def mk_scal(name, modi, kind, normcol):
        dst = scal[name]
        for v in range(2):
            src = mods[:, modi * 16:(modi + 1) * 16, v]
            if kind == "A":
                s.op("dve", lambda e, v=v, src=src: e.scalar_tensor_tensor(
                    out=dst[:, :, v], in0=src, scalar=1.0, in1=consts[:, normcol:normcol + 16],
                    op0=ALU.add, op1=ALU.mult), reads=["mods", "consts"], writes=["scal"])
            elif kind == "B":
                s.op("dve", lambda e, v=v, src=src: e.tensor_copy(out=dst[:, :, v], in_=src),
                     reads=["mods"], writes=["scal"])
            else:
                s.op("dve", lambda e, v=v, src=src: e.tensor_scalar(
                    out=dst[:, :, v], in0=src, scalar1=(0.5 if kind == "G" else 1.0), scalar2=None, op0=ALU.mult),
                    reads=["mods"], writes=["scal"])

    mk_scal("B1", 0, "B", 0)
    mk_scal("A1", 1, "A", C_FF1N)
    mk_scal("G1", 2, "G", 0)
    mk_scal("B2", 3, "B", 0)
    mk_scal("A2", 4, "A", C_MIXN)
    mk_scal("GM", 5, "GM", 0)
    mk_scal("B3", 6, "B", 0)
    mk_scal("A3", 7, "A", C_FF2N)
    mk_scal("G3", 8, "G", 0)
    if debug:
        s.op("sp", lambda e: e.dma_start(out=modsd, in_=mods[:].rearrange("p j v -> p (j v)")), reads=["mods"],
             writes=["modsd"], dma="dbg")

    def norm_stage(src, col0, N, Asc, Bsc, v, dst_of, xb, sq, rstd, ssb):
        nx = len(xb)
        for kc in range(KC):
            xs = kc % nx
            s.op("sp", lambda e, kc=kc, xs=xs: e.dma_start(out=xb[xs][:, 0:N], in_=src[kc, :, col0:col0 + N]),
                 writes=[("xb", xs)], dma=("xb", xs))
            q2 = kc % 2
            s.op("act", lambda e, xs=xs, q2=q2: e.activation(out=sq[q2][:, 0:N], in_=xb[xs][:, 0:N], func=AF.Square),
                 reads=[("xb", xs)], writes=[("sq", q2)])
            s.op("pe", lambda e, q2=q2, kc=kc: e.matmul(bank(ssb)[:, 0:N], ones_bf[:], sq[q2][:, 0:N],
                                                       start=(kc == 0), stop=(kc == KC - 1)),
                 reads=[("sq", q2), "ones_bf"], writes=[bres(ssb)])
        s.op("act", lambda e: e.activation(out=rstd[:, 0:N], in_=bank(ssb)[:, 0:N], func=AF.Sqrt,
                                           scale=1.0 / D, bias=epsb[:, 0:1]),
             reads=[bres(ssb), "epsb"], writes=["rstd"])
        s.op("dve", lambda e: e.reciprocal(out=rstd[:, 0:N], in_=rstd[:, 0:N]), reads=["rstd"], writes=["rstd"])
        for kc in range(KC):
            xs = kc % nx
            s.op("sp", lambda e, kc=kc, xs=xs: e.dma_start(out=xb[xs][:, 0:N], in_=src[kc, :, col0:col0 + N]),
                 writes=[("xb", xs)], dma=("xb", xs))
            s.op("dve", lambda e, xs=xs: e.tensor_tensor(out=xb[xs][:, 0:N], in0=xb[xs][:, 0:N], in1=rstd[:, 0:N],
                                                        op=ALU.mult),
                 reads=[("xb", xs), "rstd"], writes=[("xb", xs)])
            dst, dres = dst_of(kc)
            s.op("act", lambda e, xs=xs, kc=kc, dst=dst: e.activation(
                out=dst, in_=xb[xs][:, 0:N], func=AF.Identity, bias=Bsc[:, kc, v:v + 1], scale=Asc[:, kc, v:v + 1]),
                reads=[("xb", xs), "scal"], writes=[dres])

    def ffn_phase(src, dst, dst_col0, tiles, wi, wo, Asc, Bsc, Gsc, tag):
        m0 = mem.mark()
        h = mem.alloc("h", [128, KC, 1024], BF16)
        g = mem.alloc("g", [128, FC, 1024], BF16)
        wsl = [mem.alloc("wsl%d" % i, [128, 16 * 512], BF16) for i in range(3)]
        xb = [mem.alloc("xb%d" % i, [128, 512], F32) for i in range(4)]
        sq = [mem.alloc("sq%d" % i, [128, 512], BF16) for i in range(2)]
        rstd = mem.alloc("rstd", [128, 512], F32)
        sa = [mem.alloc("sa%d" % i, [128, 512], F32) for i in range(2)]
        xr = [xb[0], xb[1]]
        xo = [xb[2], xb[3]]
        wi_r = wi.rearrange("(k p) c -> p k c", p=128)
        wo_r = wo.rearrange("(f p) c -> p f c", p=128)
        SSB = 6
        wcnt = [0]
        ecnt = [0]

        def next_slot():
            sl = wcnt[0] % 3
            wcnt[0] += 1
            return sl

        for tile in tiles:
            hoff = 0
            hoffs = []
            for (c0, N, v) in tile:
                ho = hoff
                norm_stage(src, c0, N, Asc, Bsc, v,
                           lambda kc, ho=ho, N=N: (h[:, kc, ho:ho + N], ("h", kc)),
                           xb, sq, rstd, SSB)
                hoffs.append(ho)
                hoff += N
            for jg in range(FC // 2):
                sl = next_slot()
                wv = wsl[sl][:].rearrange("p (k c) -> p k c", c=512)
                s.op("pool", lambda e, jg=jg, wv=wv: e.dma_start(out=wv[:, :, 0:256],
                                                               in_=wi_r[:, :, jg * 256:(jg + 1) * 256]),
                     writes=[("w", sl, 0)], dma=("w", sl, 0))
                s.op("pool", lambda e, jg=jg, wv=wv: e.dma_start(out=wv[:, :, 256:512],
                                                               in_=wi_r[:, :, DFF + jg * 256:DFF + (jg + 1) * 256]),
                     writes=[("w", sl, 1)], dma=("w", sl, 1))
                for jj in range(2):
                    j = jg * 2 + jj
                    for hi, (c0, N, v) in enumerate(tile):
                        ho = hoffs[hi]
                        e2 = ecnt[0] % 2
                        ecnt[0] += 1
                        pa, pb = e2 * 2, e2 * 2 + 1
                        for kc in range(KC):
                            s.op("pe", lambda e, wv=wv, jj=jj, kc=kc, ho=ho, N=N, pa=pa: e.matmul(
                                bank(pa)[:, 0:N], wv[:, kc, jj * 128:(jj + 1) * 128], h[:, kc, ho:ho + N],
                                start=(kc == 0), stop=(kc == KC - 1)),
                                reads=[("w", sl, 0), ("h", kc)], writes=[bres(pa)])
                        for kc in range(KC):
                            s.op("pe", lambda e, wv=wv, jj=jj, kc=kc, ho=ho, N=N, pb=pb: e.matmul(
                                bank(pb)[:, 0:N], wv[:, kc, 256 + jj * 128:256 + (jj + 1) * 128], h[:, kc, ho:ho + N],
                                start=(kc == 0), stop=(kc == KC - 1)),
                                reads=[("w", sl, 1), ("h", kc)], writes=[bres(pb)])
                        s.op("act", lambda e, e2=e2, pa=pa, N=N: e.activation(out=sa[e2][:, 0:N], in_=bank(pa)[:, 0:N],
                                                                             func=AF.Silu),
                             reads=[bres(pa)], writes=[("sa", e2)])
                        s.op("dve", lambda e, e2=e2, pb=pb, N=N, j=j, ho=ho: e.tensor_tensor(
                            out=g[:, j, ho:ho + N], in0=bank(pb)[:, 0:N], in1=sa[e2][:, 0:N], op=ALU.mult),
                            reads=[bres(pb), ("sa", e2)], writes=[("g", j)])
            for m in range(KC):
                sl = next_slot()
                wv = wsl[sl][:, 0:FC * 128].rearrange("p (f c) -> p f c", c=128)
                s.op("pool", lambda e, m=m, wv=wv: e.dma_start(out=wv, in_=wo_r[:, :, m * 128:(m + 1) * 128]),
                     writes=[("w", sl, 0), ("w", sl, 1)], dma=("w", sl, 0))
                for hi, (c0, N, v) in enumerate(tile):
                    ho = hoffs[hi]
                    e2 = ecnt[0] % 2
                    ecnt[0] += 1
                    po = 4 + e2
                    for f in range(FC):
                        s.op("pe", lambda e, wv=wv, f=f, ho=ho, N=N, po=po: e.matmul(
                            bank(po)[:, 0:N], wv[:, f, :], g[:, f, ho:ho + N],
                            start=(f == 0), stop=(f == FC - 1)),
                            reads=[("w", sl, 0), ("w", sl, 1), ("g", f)], writes=[bres(po)])
                    s.op("sp", lambda e, e2=e2, m=m, c0=c0, N=N: e.dma_start(out=xr[e2][:, 0:N], in_=src[m, :, c0:c0 + N]),
                         writes=[("xb", e2)], dma=("xb", e2))
                    s.op("dve", lambda e, e2=e2, po=po, N=N, m=m, v=v: e.scalar_tensor_tensor(
                        out=xo[e2][:, 0:N], in0=bank(po)[:, 0:N], scalar=Gsc[:, m, v:v + 1], in1=xr[e2][:, 0:N],
                        op0=ALU.mult, op1=ALU.add),
                        reads=[bres(po), ("xb", e2), "scal"], writes=[("xb", 2 + e2)])
                    dc = c0 - dst_col0
                    s.op("sp", lambda e, e2=e2, m=m, dc=dc, N=N: e.dma_start(out=dst[m, :, dc:dc + N], in_=xo[e2][:, 0:N]),
                         reads=[("xb", 2 + e2)], writes=[(tag, m)], dma=("xb", 2 + e2))
        mem.reset(m0)

    s.barrier()
    mem.reset(pmark)
    tiles1 = [[(0, 512, 0), (512, 512, 0)], [(1024, 512, 0), (1536, 512, 0)], [(2048, 512, 0), (2560, 256, 1)]]
    if stop_after != "ada" and not skip_ffn1:
        ffn_phase(xin, x1, 0, tiles1, w1i, w1o, scal["A1"], scal["B1"], scal["G1"], "x1")

    if stop_after in ("ada", "ffn1"):
        s.emit()
        return nc

    s.barrier()
    mem.reset(pmark)
    sqacc_a = mem.alloc("sqacc_a", [128, OWN], F32)
    sqacc_c = mem.alloc("sqacc_c", [128, OWN], F32)
    s.op("pool", lambda e: e.memset(sqacc_a[:], 0.0), writes=["sqacc_a"])
    s.op("pool", lambda e: e.memset(sqacc_c[:], 0.0), writes=["sqacc_c"])
    pmark2 = mem.mark()
    hm = mem.alloc("hm", [128, KC, TT], BF16)
    xb = [mem.alloc("xb%d" % i, [128, 512], F32) for i in range(4)]
    sq = [mem.alloc("sq%d" % i, [128, 512], BF16) for i in range(2)]
    rstd = mem.alloc("rstd", [128, 512], F32)
    wbig = mem.alloc("wbig", [128, KC, 512], BF16)
    cosT = mem.alloc("cosT", [128, TT], F32)
    sinT = mem.alloc("sinT", [128, TT], F32)
    s.op("sp", lambda e: e.dma_start(out=cosT[:], in_=cs_d[0]), writes=["cosT"], dma="cos")
    s.op("sp", lambda e: e.dma_start(out=sinT[:], in_=cs_d[1]), writes=["sinT"], dma="sin")
    wtiles = [(i * 512, 512, 0) for i in range(5)] + [(WT, CTX, 1)]
    for (c0, N, v) in wtiles:
        norm_stage(x1, c0, N, scal["A2"], scal["B2"], v,
                   lambda kc, c0=c0, N=N: (hm[:, kc, c0:c0 + N], ("hm", kc)), xb, sq, rstd, 6)
    if stop_after == "projN":
        s.emit()
        return nc
    w_in_r = w_in.rearrange("(k p) c -> p k c", p=128)

    vst = [mem.alloc("vst%d" % i, [128, 512], BF16) for i in range(2)]
    cnt = 0
    for hf in range(2):
        s.op("pool", lambda e, hf=hf: e.dma_start(out=wbig[:], in_=w_in_r[:, :, 2048 + hf * 512:2560 + hf * 512]),
             writes=["wbig0"], dma="wbig0")
        for tg in range(22):
            e2 = cnt % 2
            cnt += 1
            pbk = 4 + e2
            for kc in range(KC):
                s.op("pe", lambda e, tg=tg, hf=hf, kc=kc, pbk=pbk: e.matmul(
                    bank(pbk)[:, :], hm[:, kc, tg * 128:(tg + 1) * 128], wbig[:, kc, :],
                    start=(kc == 0), stop=(kc == KC - 1)),
                    reads=["wbig0", ("hm", kc)], writes=[bres(pbk)])
            s.op("act", lambda e, e2=e2, pbk=pbk: e.activation(out=vst[e2][:], in_=bank(pbk)[:, :], func=AF.Copy),
                 reads=[bres(pbk)], writes=[("vst", e2)])
            s.op("sp", lambda e, e2=e2, tg=tg, hf=hf: e.dma_start(out=vtok[tg, :, hf * 512:(hf + 1) * 512], in_=vst[e2][:]),
                 reads=[("vst", e2)], writes=["vtok"], dma=("vst", e2))

    if stop_after == "projV":
        s.emit()
        return nc
    NSL = 4
    wsm = [wbig[:, :, i * 128:(i + 1) * 128] for i in range(NSL)]
    wsm_cnt = [0]

    def load_w(colbase):
        sl = wsm_cnt[0] % NSL
        wsm_cnt[0] += 1
        s.op("pool", lambda e, sl=sl, colbase=colbase: e.dma_start(out=wsm[sl], in_=w_in_r[:, :, colbase:colbase + 128]),
             reads=[], writes=[("wsm", sl), "wbig0"] if wsm_cnt[0] <= NSL else [("wsm", sl)],
             dma=("wsm", sl))
        return sl

    sqf = [xb[0], xb[1]]
    raw = [xb[2], xb[3]]
    sd = [mem.alloc("sd%d" % i, [128, 512], F32) for i in range(2)]
    qn = [mem.alloc("qn%d" % i, [128, 512], F32) for i in range(2)]
    t1 = [mem.alloc("t1%d" % i, [128, 512], F32) for i in range(2)]
    t2 = [mem.alloc("t2%d" % i, [128, 512], F32) for i in range(2)]
    qr = [mem.alloc("qr%d" % i, [128, 512], BF16) for i in range(2)]
    pcnt = [0]

    def qk_stages(sl, hcol0, N, gcol, cscol0, dst_ap, dres):
        i2 = pcnt[0] % 2
        pcnt[0] += 1
        praw, pss, prot = 4 + i2, 0 + i2, 2 + i2

        def stA():
            for kc in range(KC):
                s.op("pe", lambda e, kc=kc: e.matmul(bank(praw)[:, 0:N], wsm[sl][:, kc, :], hm[:, kc, hcol0:hcol0 + N],
                                                     start=(kc == 0), stop=(kc == KC - 1)),
                     reads=[("wsm", sl), ("hm", kc)], writes=[bres(praw)])
            s.op("act", lambda e: e.activation(out=sqf[i2][:, 0:N], in_=bank(praw)[:, 0:N], func=AF.Square),
                 reads=[bres(praw)], writes=[("xb", i2)])
            s.op("dve", lambda e: e.tensor_copy(out=raw[i2][:, 0:N], in_=bank(praw)[:, 0:N]),
                 reads=[bres(praw)], writes=[("xb", 2 + i2)])

        def stB():
            s.op("pe", lambda e: e.matmul(bank(pss)[:, 0:N], bones[:], sqf[i2][:, 0:N], start=True, stop=True),
                 reads=["bones", ("xb", i2)], writes=[bres(pss)])
            s.op("act", lambda e: e.activation(out=sd[i2][:, 0:N], in_=bank(pss)[:, 0:N], func=AF.Sqrt,
                                               scale=1.0 / 64.0, bias=epsb[:, 0:1]),
                 reads=[bres(pss), "epsb"], writes=[("sd", i2)])
            s.op("dve", lambda e: e.reciprocal(out=sd[i2][:, 0:N], in_=sd[i2][:, 0:N]),
                 reads=[("sd", i2)], writes=[("sd", i2)])
            s.op("dve", lambda e: e.scalar_tensor_tensor(out=qn[i2][:, 0:N], in0=raw[i2][:, 0:N],
                                                         scalar=consts[:, gcol:gcol + 1], in1=sd[i2][:, 0:N],
                                                         op0=ALU.mult, op1=ALU.mult),
                 reads=[("xb", 2 + i2), ("sd", i2), "consts"], writes=[("qn", i2)])

        def stC():
            s.op("pe", lambda e: e.matmul(bank(prot)[:, 0:N], pmat[:], qn[i2][:, 0:N], start=True, stop=True),
                 reads=["pmat", ("qn", i2)], writes=[bres(prot)])
            s.op("pool", lambda e: e.tensor_tensor(out=t1[i2][:, 0:N], in0=qn[i2][:, 0:N], in1=cosT[:, cscol0:cscol0 + N],
                                                   op=ALU.mult),
                 reads=[("qn", i2), "cosT"], writes=[("t1", i2)])
            s.op("dve", lambda e: e.tensor_tensor(out=t2[i2][:, 0:N], in0=bank(prot)[:, 0:N], in1=sinT[:, cscol0:cscol0 + N],
                                                  op=ALU.mult),
                 reads=[bres(prot), "sinT"], writes=[("t2", i2)])
            s.op("pool", lambda e: e.tensor_tensor(out=qr[i2][:, 0:N], in0=t1[i2][:, 0:N], in1=t2[i2][:, 0:N], op=ALU.add),
                 reads=[("t1", i2), ("t2", i2)], writes=[("qr", i2)])
            s.op("sp", lambda e: e.dma_start(out=dst_ap, in_=qr[i2][:, 0:N]), reads=[("qr", i2)], writes=[dres],
                 dma=("qr", i2))

        return stA, stB, stC

    chunks = []
    for c in range(8):
        chunks.append((1024 + c * 128,
                       [(c0, N, C_KG, c0, kT[c, :, c0:c0 + N], ("kT", c)) for (c0, N, v) in wtiles]))
        chunks.append((c * 128,
                       [(OWN0 + i * 512, 512, C_QG, OWN0 + i * 512, qT[c, :, i * 512:(i + 1) * 512], ("qT", c))
                        for i in range(4)]))
    slots = {0: load_w(chunks[0][0])}
    pend = []
    for k, (colbase, tl) in enumerate(chunks):
        if k + 1 < len(chunks):
            slots[k + 1] = load_w(chunks[k + 1][0])
        for targs in tl:
            stA, stB, stC = qk_stages(slots[k], *targs)
            stA()
            pend.append((stB, stC))
            if len(pend) >= 2:
                pend[-2][0]()
            if len(pend) >= 3:
                pend[-3][1]()
    pend[-1][0]()
    pend[-2][1]()
    pend[-1][1]()

    if stop_after == "projQK":
        s.emit()
        return nc
    zbuf = mem.alloc("zbuf", [128, OWN + 2], F32)
    acc = mem.alloc("acc", [128, OWN], F32)
    cS = [mem.alloc("cS%d" % i, [128, 512], F32) for i in range(2)]
    zh = mem.alloc("zh", [128, 2], F32)
    yv = sqf
    ysq = raw
    ybf = qr
    ccnt = 0
    for cc in range(8):
        slB = load_w(3072 + cc * 128)
        slC = load_w(4096 + cc * 128)
        slU = load_w(5120 + cc * 128)
        for i in range(4):
            c0 = OWN0 + i * 512
            i2 = ccnt % 2
            ccnt += 1
            pC, pU = 0 + i2, 2 + i2
            for kc in range(KC):
                s.op("pe", lambda e, kc=kc, c0=c0, pC=pC, slC=slC: e.matmul(bank(pC)[:, :], wsm[slC][:, kc, :], hm[:, kc, c0:c0 + 512],
                                                                  start=(kc == 0), stop=(kc == KC - 1)),
                     reads=[("wsm", slC), ("hm", kc)], writes=[bres(pC)])
            for kc in range(KC):
                s.op("pe", lambda e, kc=kc, c0=c0, pU=pU, slU=slU: e.matmul(bank(pU)[:, :], wsm[slU][:, kc, :], hm[:, kc, c0:c0 + 512],
                                                                  start=(kc == 0), stop=(kc == KC - 1)),
                     reads=[("wsm", slU), ("hm", kc)], writes=[bres(pU)])
            s.op("act", lambda e, i2=i2, pC=pC: e.activation(out=cS[i2][:], in_=bank(pC)[:, :], func=AF.Copy),
                 reads=[bres(pC)], writes=[("cS", i2)])
            s.op("dve", lambda e, i2=i2, pU=pU, i=i: e.tensor_tensor(out=zbuf[:, 1 + i * 512:1 + (i + 1) * 512],
                                                                    in0=bank(pU)[:, :], in1=cS[i2][:], op=ALU.mult),
                 reads=[bres(pU), ("cS", i2)], writes=["zbuf"])
        pH = 6
        for hi, col in enumerate((OWN0 - 1, OWN0 + OWN)):
            for wi_, sl_ in enumerate((slC, slU)):
                for kc in range(KC):
                    s.op("pe", lambda e, kc=kc, col=col, hi=hi, wi_=wi_, sl_=sl_: e.matmul(
                        bank(pH)[:, wi_ * 2 + hi:wi_ * 2 + hi + 1], wsm[sl_][:, kc, :], hm[:, kc, col:col + 1],
                        start=(kc == 0), stop=(kc == KC - 1)),
                        reads=[("wsm", sl_), ("hm", kc)], writes=[bres(pH)])
        s.op("act", lambda e: e.activation(out=zh[:], in_=bank(pH)[:, 0:2], func=AF.Copy), reads=[bres(pH)], writes=["zh"])
        s.op("dve", lambda e: e.tensor_tensor(out=zh[:], in0=bank(pH)[:, 2:4], in1=zh[:], op=ALU.mult),
             reads=[bres(pH), "zh"], writes=["zh"])
        s.op("dve", lambda e: e.tensor_tensor(out=zbuf[:, 0:OWN + 2:OWN + 1], in0=zh[:], in1=consts[:, C_HALO:C_HALO + 2],
                                              op=ALU.mult),
             reads=["zh", "consts"], writes=["zbuf"])
        cw = C_CW + cc * 3
        s.op("dve", lambda e, cw=cw, cc=cc: e.tensor_scalar(out=acc[:], in0=zbuf[:, 1:OWN + 1], scalar1=consts[:, cw + 1:cw + 2],
                                                           scalar2=consts[:, C_CB + cc:C_CB + cc + 1], op0=ALU.mult, op1=ALU.add),
             reads=["zbuf", "consts"], writes=["acc"])
        s.op("dve", lambda e, cw=cw: e.scalar_tensor_tensor(out=acc[:], in0=zbuf[:, 0:OWN], scalar=consts[:, cw:cw + 1],
                                                            in1=acc[:], op0=ALU.mult, op1=ALU.add),
             reads=["zbuf", "consts", "acc"], writes=["acc"])
        s.op("dve", lambda e, cw=cw: e.scalar_tensor_tensor(out=acc[:], in0=zbuf[:, 2:OWN + 2], scalar=consts[:, cw + 2:cw + 3],
                                                           in1=acc[:], op0=ALU.mult, op1=ALU.add),
             reads=["zbuf", "consts", "acc"], writes=["acc"])
        for i in range(4):
            c0 = OWN0 + i * 512
            i2 = ccnt % 2
            ccnt += 1
            pB = 4 + i2
            for kc in range(KC):
                s.op("pe", lambda e, kc=kc, c0=c0, pB=pB, slB=slB: e.matmul(bank(pB)[:, :], wsm[slB][:, kc, :], hm[:, kc, c0:c0 + 512],
                                                                  start=(kc == 0), stop=(kc == KC - 1)),
                     reads=[("wsm", slB), ("hm", kc)], writes=[bres(pB)])
            s.op("dve", lambda e, i2=i2, pB=pB, i=i: e.tensor_tensor(out=yv[i2][:], in0=bank(pB)[:, :],
                                                                    in1=acc[:, i * 512:(i + 1) * 512], op=ALU.mult),
                 reads=[bres(pB), "acc"], writes=[("xb", i2)])
            s.op("act", lambda e, i2=i2: e.activation(out=ysq[i2][:], in_=yv[i2][:], func=AF.Square),
                 reads=[("xb", i2)], writes=[("xb", 2 + i2)])
            s.op("pool", lambda e, i2=i2, i=i: e.tensor_tensor(out=sqacc_c[:, i * 512:(i + 1) * 512],
                                                              in0=sqacc_c[:, i * 512:(i + 1) * 512], in1=ysq[i2][:], op=ALU.add),
                 reads=[("xb", 2 + i2), "sqacc_c"], writes=["sqacc_c"])
            s.op("act", lambda e, i2=i2, cc=cc: e.activation(out=ybf[i2][:], in_=yv[i2][:], func=AF.Identity,
                                                            scale=consts[:, C_ONC + cc:C_ONC + cc + 1]),
                 reads=[("xb", i2), "consts"], writes=[("qr", i2)])
            s.op("sp", lambda e, i2=i2, cc=cc, i=i: e.dma_start(out=aoT[8 + cc, :, i * 512:(i + 1) * 512], in_=ybf[i2][:]),
                 reads=[("qr", i2)], writes=[("aoT", 8 + cc)], dma=("qr", i2))

    if stop_after == "proj":
        s.emit()
        return nc

    s.barrier()
    mem.reset(pmark2)
    KTs = [mem.alloc("KT%d" % i, [128, TT], BF16) for i in range(2)]
    QTs = [mem.alloc("QT%d" % i, [128, OWN], BF16) for i in range(2)]
    Vs = [mem.alloc("V%d" % i, [128, 22, 128], BF16) for i in range(2)]
    Bstd = [mem.alloc("Bstd%d" % i, [128, 2 * 5 * 128], F32) for i in range(2)]
    Bspec = [mem.alloc("Bspec%d" % i, [128, 2 * 22 * 128], F32) for i in range(2)]
    aob = [mem.alloc("aob%d" % i, [128, OWN], BF16) for i in range(2)]
    tmp = [mem.alloc("tmp%d" % i, [128, 768], F32) for i in range(2)]
    pt = [mem.alloc("pt%d" % i, [128, 1024], BF16) for i in range(2)]
    rden = [mem.alloc("rden%d" % i, [128, 128], F32) for i in range(2)]
    at = [mem.alloc("at%d" % i, [128, 128], F32) for i in range(2)]
    asq = [mem.alloc("asq%d" % i, [128, 128], F32) for i in range(2)]
    vtok_r = vtok.rearrange("g p f -> p g f")

    def attn_loads(c):
        c2 = c % 2
        s.op("sp", lambda e: e.dma_start(out=KTs[c2][:], in_=kT[c]), reads=[("kT", c)], writes=[("KT", c2)],
             dma=("KT", c2))
        s.op("sp", lambda e: e.dma_start(out=QTs[c2][:], in_=qT[c]), reads=[("qT", c)], writes=[("QT", c2)],
             dma=("QT", c2))
        s.op("sp", lambda e: e.dma_start(out=Vs[c2][:], in_=vtok_r[:, :, c * 128:(c + 1) * 128]),
             reads=["vtok"], writes=[("V", c2)], dma=("V", c2))
        s.op("sp", lambda e: e.dma_start(out=Bstd[c2][:], in_=tstd_d[c]), writes=[("Bstd", c2)], dma=("Bstd", c2))
        s.op("sp", lambda e: e.dma_start(out=Bspec[c2][:], in_=tspec_d[c]), writes=[("Bspec", c2)],
             dma=("Bspec", c2))

    def head_block(idx, c, j, hh):
        c2 = c % 2
        if j == 0:
            pl, tb, so = list(range(0, 6)), Bspec, 0
        elif j == 1:
            pl, tb, so = list(range(1, 6)), Bspec, 6
        elif j == 14:
            pl, tb, so = list(range(14, 19)), Bspec, 11
        elif j == 15:
            pl, tb, so = list(range(14, 20)), Bspec, 16
        else:
            pl, tb, so = list(range(j, j + 5)), Bstd, 0
        nsl = 22 if tb is Bspec else 5
        tres = ("Bspec", c2) if tb is Bspec else ("Bstd", c2)
        L = len(pl)
        b2 = (c * 16 + j) % 2
        pnum, pden = 4 + b2, 6 + b2
        h2 = idx % 2
        hp0, hp1 = hh * 64, hh * 64 + 64
        stt = st_ps[h2]
        sres = [bres(2 * h2), bres(2 * h2 + 1)]
        plx = pl + [20, 21]

        def st_S():
            for si, p in enumerate(plx):
                kcol = p * 128
                s.op("pe", lambda e, si=si, kcol=kcol: e.matmul(
                    stt[:, si * 128:(si + 1) * 128], KTs[c2][hp0:hp1, kcol:kcol + 128],
                    QTs[c2][hp0:hp1, j * 128:(j + 1) * 128], start=True, stop=True),
                    reads=[("KT", c2), ("QT", c2)], writes=[sres[si // 4]])

        def st_R():
            tbv = tb[c2][:, (hh * nsl + so) * 128:(hh * nsl + so + L) * 128]
            s.op("act", lambda e: e.activation(out=pt[h2][:, L * 128:(L + 2) * 128],
                                               in_=stt[:, L * 128:(L + 2) * 128], func=AF.Exp, scale=0.125),
                 reads=sres[1:], writes=[("ptc", h2)])
            s.op("dve", lambda e: e.scalar_tensor_tensor(
                out=tmp[h2][:, 0:L * 128], in0=stt[:, 0:L * 128], scalar=0.125, in1=tbv, op0=ALU.mult, op1=ALU.add),
                reads=sres + [tres], writes=[("tmp", h2)])
            s.op("act", lambda e: e.activation(out=pt[h2][:, 0:L * 128], in_=tmp[h2][:, 0:L * 128], func=AF.Exp),
                 reads=[("tmp", h2)], writes=[("pt", h2)])
            for si, p in enumerate(plx):
                first, last = (si == 0), (si == L + 1)
                s.op("pe", lambda e, si=si, p=p, first=first, last=last: e.matmul(
                    bank(pnum)[hp0:hp1, 0:128], Vs[c2][:, p, hh * 64:(hh + 1) * 64], pt[h2][:, si * 128:(si + 1) * 128],
                    start=first, stop=last),
                    reads=[("V", c2), ("pt", h2), ("ptc", h2)], writes=[bres(pnum)])
                s.op("pe", lambda e, si=si, first=first, last=last: e.matmul(
                    bank(pden)[hp0:hp1, 0:128], ones_bf[:, 0:64], pt[h2][:, si * 128:(si + 1) * 128],
                    start=first, stop=last),
                    reads=["ones_bf", ("pt", h2), ("ptc", h2)], writes=[bres(pden)])
            if hh == 1:
                s.op("dve", lambda e: e.reciprocal(out=rden[b2][:], in_=bank(pden)[:, 0:128]),
                     reads=[bres(pden)], writes=[("rden", b2)])
                s.op("dve", lambda e: e.tensor_tensor(out=at[b2][:], in0=bank(pnum)[:, 0:128], in1=rden[b2][:],
                                                      op=ALU.mult),
                     reads=[bres(pnum), ("rden", b2)], writes=[("at", b2)])
                s.op("act", lambda e: e.activation(out=asq[b2][:], in_=at[b2][:], func=AF.Square),
                     reads=[("at", b2)], writes=[("asq", b2)])
                s.op("pool", lambda e: e.tensor_tensor(out=sqacc_a[:, j * 128:(j + 1) * 128],
                                                       in0=sqacc_a[:, j * 128:(j + 1) * 128], in1=asq[b2][:], op=ALU.add),
                     reads=[("asq", b2), "sqacc_a"], writes=["sqacc_a"])
                s.op("act", lambda e: e.activation(out=aob[c2][:, j * 128:(j + 1) * 128], in_=at[b2][:],
                                                   func=AF.Identity, scale=consts[:, C_ONA + c:C_ONA + c + 1]),
                     reads=[("at", b2), "consts"], writes=[("aob", c2)])
                if j == 15:
                    s.op("sp", lambda e: e.dma_start(out=aoT[c], in_=aob[c2][:]), reads=[("aob", c2)],
                         writes=[("aoT", c)], dma=("aob", c2))

        return st_S, st_R

    hbs = [(c, j, hh) for c in range(8) for j in range(16) for hh in range(2)]
    attn_loads(0)
    attn_loads(1)
    cur = head_block(0, *hbs[0])
    cur[0]()
    for i in range(len(hbs)):
        nxt = None
        if i + 1 < len(hbs):
            nxt = head_block(i + 1, *hbs[i + 1])
            nxt[0]()
        cur[1]()
        c, j, hh = hbs[i]
        if j == 15 and hh == 1 and c + 2 < 8:
            attn_loads(c + 2)
        cur = nxt

    if stop_after == "attn":
        s.emit()
        return nc

    s.barrier()
    mem.reset(pmark2)
    wo = mem.alloc("wo", [128, KC, D], BF16)
    w_out_r = w_out.rearrange("(k p) c -> p k c", p=128)
    for i in range(4):
        s.op("pool", lambda e, i=i: e.dma_start(out=wo[:, i * 4:(i + 1) * 4, :], in_=w_out_r[:, i * 4:(i + 1) * 4, :]),
             writes=[("wo", i)], dma=("wo", i))
    rs = [mem.alloc("rs%d" % i, [128, OWN], F32) for i in range(2)]
    for bi, sqa in enumerate((sqacc_a, sqacc_c)):
        rname = "sqacc_a" if bi == 0 else "sqacc_c"
        for i in range(4):
            pbk = i % 2
            s.op("pe", lambda e, i=i, sqa=sqa, pbk=pbk: e.matmul(bank(pbk)[:, :], ones_f[:], sqa[:, i * 512:(i + 1) * 512],
                                                                start=True, stop=True),
                 reads=["ones_f", rname], writes=[bres(pbk)])
            s.op("act", lambda e, i=i, bi=bi, pbk=pbk: e.activation(out=rs[bi][:, i * 512:(i + 1) * 512], in_=bank(pbk)[:, :],
                                                                   func=AF.Sqrt, scale=1.0 / 1024.0, bias=epsb[:, 0:1]),
                 reads=[bres(pbk), "epsb"], writes=[("rs", bi)])
        s.op("dve", lambda e, bi=bi: e.reciprocal(out=rs[bi][:], in_=rs[bi][:]), reads=[("rs", bi)], writes=[("rs", bi)])
    aot = [mem.alloc("aot%d" % i, [128, KC, 512], BF16) for i in range(2)]
    x1t = [mem.alloc("x1t%d" % i, [128, 512], F32) for i in range(2)]
    u1 = [mem.alloc("u1%d" % i, [128, 512], F32) for i in range(2)]
    u2 = [mem.alloc("u2%d" % i, [128, 512], F32) for i in range(2)]
    u3 = [mem.alloc("u3%d" % i, [128, 512], F32) for i in range(2)]
    aoT_r = aoT.rearrange("k p t -> p k t")
    mcnt = 0
    def load_aot(n):
        n2 = n % 2
        s.op("sp", lambda e: e.dma_start(out=aot[n2][:], in_=aoT_r[:, :, n * 512:(n + 1) * 512]),
             reads=[("aoT", k) for k in range(KC)], writes=[("aot", n2)], dma=("aot", n2))

    load_aot(0)
    for n in range(4):
        n2 = n % 2
        if n + 1 < 4:
            load_aot(n + 1)
        for m in range(KC):
            i2 = mcnt % 2
            mcnt += 1
            pA, pC = 0 + i2, 2 + i2
            for kc in range(8):
                s.op("pe", lambda e, kc=kc, m=m, n2=n2, pA=pA: e.matmul(bank(pA)[:, :], wo[:, kc, m * 128:(m + 1) * 128],
                                                                        aot[n2][:, kc, :], start=(kc == 0), stop=(kc == 7)),
                     reads=[("wo", kc // 4), ("aot", n2)], writes=[bres(pA)])
            for kc in range(8, 16):
                s.op("pe", lambda e, kc=kc, m=m, n2=n2, pC=pC: e.matmul(bank(pC)[:, :], wo[:, kc, m * 128:(m + 1) * 128],
                                                                        aot[n2][:, kc, :], start=(kc == 8), stop=(kc == 15)),
                     reads=[("wo", kc // 4), ("aot", n2)], writes=[bres(pC)])
            s.op("sp", lambda e, i2=i2, m=m, n=n: e.dma_start(out=x1t[i2][:], in_=x1[m, :, OWN0 + n * 512:OWN0 + (n + 1) * 512]),
                 reads=[("x1", m)], writes=[("x1t", i2)], dma=("x1t", i2))
            s.op("dve", lambda e, i2=i2, pA=pA, n=n: e.tensor_tensor(out=u1[i2][:], in0=bank(pA)[:, :],
                                                                    in1=rs[0][:, n * 512:(n + 1) * 512], op=ALU.mult),
                 reads=[bres(pA), ("rs", 0)], writes=[("u1", i2)])
            s.op("dve", lambda e, i2=i2, pC=pC, n=n: e.tensor_tensor(out=u2[i2][:], in0=bank(pC)[:, :],
                                                                    in1=rs[1][:, n * 512:(n + 1) * 512], op=ALU.mult),
                 reads=[bres(pC), ("rs", 1)], writes=[("u2", i2)])
            s.op("pool", lambda e, i2=i2: e.tensor_tensor(out=u3[i2][:], in0=u1[i2][:], in1=u2[i2][:], op=ALU.add),
                 reads=[("u1", i2), ("u2", i2)], writes=[("u3", i2)])
            s.op("dve", lambda e, i2=i2, m=m: e.scalar_tensor_tensor(out=u3[i2][:], in0=u3[i2][:], scalar=scal["GM"][:, m, 0:1],
                                                                     in1=x1t[i2][:], op0=ALU.mult, op1=ALU.add),
                 reads=[("u3", i2), ("x1t", i2), "scal"], writes=[("u3", i2)])
            s.op("sp", lambda e, i2=i2, m=m, n=n: e.dma_start(out=x2[m, :, n * 512:(n + 1) * 512], in_=u3[i2][:]),
                 reads=[("u3", i2)], writes=[("x2", m)], dma=("u3", i2))

    if stop_after == "mix":
        s.emit()
        return nc

    s.barrier()
    mem.reset(pmark)
    tiles2 = [[(0, 512, 0), (512, 512, 0)], [(1024, 512, 0), (1536, 512, 0)]]
    ffn_phase(x2, yout, 0, tiles2, w2i, w2o, scal["A3"], scal["B3"], scal["G3"], "yout")
    s.emit()
    return nc


def _fm(vec, nchunk):
    return np.ascontiguousarray(np.asarray(vec, np.float32).reshape(nchunk, 128).T)


def _bias_table(rpb, core, j, pairs):
    H = rpb.shape[0]
    out = np.full((H, len(pairs), 128, 128), NEG, np.float32)
    rows = 256
    qrows = [32 * core + 2 * j, 32 * core + 2 * j + 1]
    cols = np.arange(64)
    cstart = np.clip(cols - 8, 0, 64 - 16)
    for si, p in enumerate(pairs):
        for ki in range(2):
            kr = 32 * core - 4 + 2 * p + ki
            if kr < 0 or kr >= rows:
                continue
            for qi, qr in enumerate(qrows):
                rs = min(max(qr - 4, 0), rows - 8)
                if not (rs <= kr < rs + 8):
                    continue
                dr = kr - qr + 7
                kcg, qcg = np.meshgrid(cols, cols, indexing="ij")
                valid = (kcg >= cstart[qcg]) & (kcg < cstart[qcg] + 16)
                dc = kcg - qcg + 15
                dcc = np.clip(dc, 0, 30)
                blk = np.where(valid[None], rpb[:, dr][:, dcc], np.float32(NEG))
                out[:, si, ki * 64:(ki + 1) * 64, qi * 64:(qi + 1) * 64] = blk
    return out


def _prep_core(core, inp, shared):
    x = inp["x"][0]
    r0 = 32 * core - 4
    win = np.zeros((WROWS * GW, D), np.float32)
    lo, hi = max(r0, 0), min(r0 + WROWS, 256)
    win[(lo - r0) * GW:(hi - r0) * GW] = x[lo * GW:hi * GW]
    xin = np.concatenate([win, inp["ctx"][0]], axis=0)
    xin = np.ascontiguousarray(xin.T.reshape(KC, 128, TT))
    consts = shared["consts"].copy()
    consts[:, C_HALO] = 0.0 if core == 0 else 1.0
    consts[:, C_HALO + 1] = 0.0 if core == NCORES - 1 else 1.0
    t = np.arange(WT)
    prow = (r0 + t // GW).astype(np.float32)
    pcol = (t % GW).astype(np.float32)
    p = np.arange(128)
    d = p % 64
    inv = (10000.0 ** (-(np.arange(16, dtype=np.float32)) / 16)).astype(np.float32)
    invp = inv[d % 16]
    pos = np.where((d // 32 == 0)[:, None], prow[None, :], pcol[None, :]).astype(np.float32)
    ang = (pos * invp[:, None]).astype(np.float32)
    cs = np.zeros((2, 128, TT), np.float32)
    cs[0, :, :WT] = np.cos(ang)
    cs[1, :, :WT] = np.sin(ang)
    cs[0, :, WT:] = 1.0
    rpb = inp["rpb"][0]
    spec = []
    for j, pairs in ((0, list(range(0, 6))), (1, list(range(1, 6))), (14, list(range(14, 19))), (15, list(range(14, 20)))):
        spec.append(_bias_table(rpb, core, j, pairs))
    spec = np.concatenate(spec, axis=1)
    tspec = spec.reshape(8, 2, 22, 128, 128).transpose(0, 3, 1, 2, 4).reshape(8, 128, 2 * 22 * 128)
    m = {"xin": xin, "consts": consts, "cossin": cs, "tspec": np.ascontiguousarray(tspec)}
    return m


def _prep_shared(inp):
    consts = np.zeros((128, NCONST), np.float32)
    consts[:, C_BADA:C_BADA + 144] = _fm(inp["b_ada"][0], 144)
    consts[:, C_FF1N:C_FF1N + 16] = _fm(inp["ff1_norm"][0], 16)
    consts[:, C_MIXN:C_MIXN + 16] = _fm(inp["mix_norm"][0], 16)
    consts[:, C_FF2N:C_FF2N + 16] = _fm(inp["ff2_norm"][0], 16)
    consts[:, C_QG] = np.tile(inp["q_norm"][0], 2)
    consts[:, C_KG] = np.tile(inp["k_norm"][0], 2)
    consts[:, C_ONA:C_ONA + 8] = _fm(inp["out_norm_attn"][0], 8)
    consts[:, C_ONC:C_ONC + 8] = _fm(inp["out_norm_conv"][0], 8)
    cw = inp["conv_w"][0]
    for cc in range(8):
        for jj in range(3):
            consts[:, C_CW + cc * 3 + jj] = cw[jj, cc * 128:(cc + 1) * 128]
    consts[:, C_CB:C_CB + 8] = _fm(inp["conv_b"][0], 8)
    c2 = np.stack([_fm(inp["c"][0], 16), _fm(inp["c_ctx"], 16)], axis=-1)
    consts[:, C_C2:C_C2 + 32] = c2.reshape(128, 32)
    bones = np.zeros((128, 128), np.float32)
    bones[:64, :64] = 1.0
    bones[64:, 64:] = 1.0
    pm = np.zeros((128, 128), np.float32)
    for mm in range(128):
        if (mm % 32) < 16:
            pm[mm + 16, mm] = -1.0
        else:
            pm[mm - 16, mm] = 1.0
    std = _bias_table(inp["rpb"][0], 1, 5, list(range(5, 10)))
    tstd = std.reshape(8, 2, 5, 128, 128).transpose(0, 3, 1, 2, 4).reshape(8, 128, 2 * 5 * 128)
    sh = {
        "consts": consts, "bones": bones, "pmat": pm, "tstd": np.ascontiguousarray(tstd),
        "w_ada": np.ascontiguousarray(inp["w_ada"][0]),
        "ff1_w_in": np.ascontiguousarray(inp["ff1_w_in"][0]), "ff1_w_out": np.ascontiguousarray(inp["ff1_w_out"][0]),
        "w_in": np.ascontiguousarray(inp["w_in"][0]), "w_out": np.ascontiguousarray(inp["w_out"][0]),
        "ff2_w_in": np.ascontiguousarray(inp["ff2_w_in"][0]), "ff2_w_out": np.ascontiguousarray(inp["ff2_w_out"][0]),
    }
    return sh


def make_in_maps(inp):
    inp = {k: np.asarray(v) for k, v in inp.items()}
    sh = _prep_shared(inp)
    maps = []
    for core in range(NCORES):
        m = dict(sh)
        m.update(_prep_core(core, inp, sh))
        maps.append(m)
    return maps


def kernel(**inputs):
    maps = make_in_maps(inputs)
    nc = build_nc()
    res = run_bass_kernel_spmd(nc, maps, core_ids=list(range(NCORES)))
    outs = []
    for r in res.results:
        y = np.asarray(r["yout"]).reshape(D, OWN)
        outs.append(y.T)
    out = np.concatenate(outs, axis=0).reshape(1, 16384, D).astype(np.float32)
    return out
```

```python
import contextlib
import numpy as np
import concourse.bass as bass
import concourse.mybir as mybir
from concourse.bass_utils import run_bass_kernel_spmd

F32 = mybir.dt.float32
BF16 = mybir.dt.bfloat16
ALU = mybir.AluOpType
AF = mybir.ActivationFunctionType

NCORES = 8
D = 2048
KC = 16
DFF = 5632
FC = 44
GW = 64
WROWS = 40
WT = WROWS * GW
CTX = 256
TT = WT + CTX
OWN0 = 4 * GW
OWN = 2048
EPS = 1e-6
NEG = -30000.0

C_BADA = 0
C_FF1N = C_BADA + 144
C_MIXN = C_FF1N + 16
C_FF2N = C_MIXN + 16
C_QG = C_FF2N + 16
C_KG = C_QG + 1
C_ONA = C_KG + 1
C_ONC = C_ONA + 8
C_CW = C_ONC + 8
C_CB = C_CW + 24
C_HALO = C_CB + 8
C_C2 = C_HALO + 2
NCONST = C_C2 + 32

ENGS = ("pe", "act", "dve", "pool", "sp")


class _Op:
    __slots__ = ("eng", "fn", "deps", "inc", "dma", "idx", "epoch")

    def __init__(self, eng, fn, dma, epoch):
        self.eng = eng
        self.fn = fn
        self.deps = {}
        self.inc = False
        self.dma = dma
        self.idx = -1
        self.epoch = epoch


class Sched:
    def __init__(self, nc):
        self.nc = nc
        self.ops = {e: [] for e in ENGS}
        self.lastw = {}
        self.readers = {}
        self.dma_cnt = {}
        self.known = {e: {} for e in ENGS}
        self.pending = {e: {} for e in ENGS}
        self.epoch = 0

    def _tok_epoch(self, key):
        return key[2] if key[0] == "e" else key[1][0]

    def _add_dep(self, op, tok, kind, force=False):
        key, val = tok
        if not force and self._tok_epoch(key) < self.epoch:
            return
        if key[0] == "e" and key[1] == op.eng:
            if op.eng == "pe":
                return
        if self.known[op.eng].get(key, -1) >= val:
            return
        if op.deps.get(key, -1) < val:
            op.deps[key] = val

    def op(self, eng, fn, reads=(), writes=(), dma=None):
        if dma is not None:
            dma = (self.epoch, dma)
        o = _Op(eng, fn, dma, self.epoch)
        for key, val in self.pending[eng].items():
            self._add_dep(o, (key, val), "raw", force=True)
        self.pending[eng] = {}
        for r in reads:
            t = self.lastw.get(r)
            if t is not None:
                self._add_dep(o, t, "raw")
            if isinstance(r, tuple) and r[0] == "ps":
                for t in self.readers.get(r, ()):
                    if t[0][0] == "e" and t[0][1] != eng:
                        self._add_dep(o, t, "raw")
        for w in writes:
            t = self.lastw.get(w)
            if t is not None:
                self._add_dep(o, t, "waw")
            for t in self.readers.get(w, ()):
                self._add_dep(o, t, "war")
        o.idx = len(self.ops[eng])
        self.ops[eng].append(o)
        for key, val in o.deps.items():
            self.known[eng][key] = val
            if key[0] == "e":
                self.ops[key[1]][val].inc = True
        if dma is not None:
            c = self.dma_cnt.get(dma, 0) + 16
            self.dma_cnt[dma] = c
            tok = (("d", dma), c)
        else:
            tok = (("e", eng, self.epoch), o.idx)
        for r in reads:
            self.readers.setdefault(r, []).append(tok)
        for w in writes:
            self.lastw[w] = tok
            self.readers[w] = []
        return o

    def barrier(self):
        toks = {}
        for e in ENGS:
            if self.ops[e]:
                last = self.ops[e][-1]
                if last.dma is None and last.fn is not None:
                    toks[("e", e, last.epoch)] = last.idx
                else:
                    for o in reversed(self.ops[e]):
                        if o.dma is None and o.fn is not None:
                            toks[("e", e, o.epoch)] = o.idx
                            break
        for k, c in self.dma_cnt.items():
            if k[0] == self.epoch:
                toks[("d", k)] = c
        for e in ENGS:
            for key, val in toks.items():
                if key[0] == "e" and key[1] == e:
                    continue
                if self.pending[e].get(key, -1) < val:
                    self.pending[e][key] = val
        self.epoch += 1

    def emit(self):
        nc = self.nc
        self.barrier()
        self.op("sp", None)
        with contextlib.ExitStack() as st:
            cum = {}
            used = set()
            for e in ENGS:
                cnt = {}
                arr = []
                for o in self.ops[e]:
                    if o.inc and o.dma is None and o.fn is not None:
                        cnt[o.epoch] = cnt.get(o.epoch, 0) + 1
                        used.add((e, o.epoch))
                    arr.append(cnt.get(o.epoch, 0))
                cum[e] = arr
            esem = {k: st.enter_context(nc.semaphore("s_%s_%d" % k)) for k in sorted(used)}
            dsem = {k: st.enter_context(nc.semaphore("d_%d" % i))
                    for i, k in enumerate(self.dma_cnt)}
            self.nsem = len(esem) + len(dsem)
            block = st.enter_context(nc.Block())

            def run(ename, engine):
                for o in self.ops[ename]:
                    for key, val in o.deps.items():
                        if key[0] == "e":
                            v = cum[key[1]][val]
                            if v > 0:
                                engine.wait_ge(esem[(key[1], key[2])], v)
                        else:
                            engine.wait_ge(dsem[key[1]], val)
                    if o.fn is None:
                        continue
                    ins = o.fn(engine)
                    if o.dma is not None:
                        ins.then_inc(dsem[o.dma], 16)
                    elif o.inc:
                        ins.then_inc(esem[(ename, o.epoch)], 1)

            @block.tensor
            def _(eng):
                run("pe", eng)

            @block.scalar
            def _(eng):
                run("act", eng)

            @block.vector
            def _(eng):
                run("dve", eng)

            @block.gpsimd
            def _(eng):
                run("pool", eng)

            @block.sync
            def _(eng):
                run("sp", eng)


def _esize(dt):
    return 4 if dt == F32 else 2


class Mem:
    def __init__(self, nc):
        self.nc = nc
        self.base = (nc.sbuf_base + 63) // 64 * 64
        self.top = nc.sbuf_top
        self.off = self.base
        self.n = 0

    def mark(self):
        return self.off

    def reset(self, m):
        self.off = m

    def alloc(self, name, shape, dt):
        sz = int(np.prod(shape[1:])) * _esize(dt)
        sz = (sz + 63) // 64 * 64
        assert self.off + sz <= self.top, ("SBUF overflow", name, self.off, sz, self.top)
        self.n += 1
        t = self.nc.alloc_sbuf_tensor_at("%s_%d" % (name, self.n), list(shape), dt, offset=self.off)
        self.off += sz
        return t


def build_nc(stop_after=None, debug=False, skip_ffn1=False):
    nc = bass.Bass("TRN2", target_bir_lowering=False)
    dk = "ExternalOutput" if debug else "Internal"

    def din(name, shape, dt=F32):
        return nc.dram_tensor(name, list(shape), dt, kind="ExternalInput").ap()

    xin = din("xin", [KC, 128, TT])
    consts_d = din("consts", [128, NCONST])
    bones_d = din("bones", [128, 128])
    pmat_d = din("pmat", [128, 128])
    cs_d = din("cossin", [2, 128, TT])
    tstd_d = din("tstd", [8, 128, 2 * 5 * 128])
    tspec_d = din("tspec", [8, 128, 2 * 22 * 128])
    w_ada = din("w_ada", [D, 9 * D])
    w1i = din("ff1_w_in", [D, 2 * DFF])
    w1o = din("ff1_w_out", [DFF, D])
    w_in = din("w_in", [D, 6144])
    w_out = din("w_out", [D, D])
    w2i = din("ff2_w_in", [D, 2 * DFF])
    w2o = din("ff2_w_out", [DFF, D])
    yout = nc.dram_tensor("yout", [KC, 128, OWN], F32, kind="ExternalOutput").ap()

    x1 = nc.dram_tensor("x1s", [KC, 128, TT], F32, kind=("ExternalInput" if skip_ffn1 else dk)).ap()
    x2 = nc.dram_tensor("x2s", [KC, 128, OWN], F32, kind=dk).ap()
    qT = nc.dram_tensor("qTs", [8, 128, OWN], BF16, kind=dk).ap()
    kT = nc.dram_tensor("kTs", [8, 128, TT], BF16, kind=dk).ap()
    vtok = nc.dram_tensor("vtoks", [22, 128, 1024], BF16, kind=dk).ap()
    aoT = nc.dram_tensor("aoTs", [KC, 128, OWN], BF16, kind=dk).ap()
    modsd = nc.dram_tensor("modsd", [128, 288], F32, kind=dk).ap()

    s = Sched(nc)
    mem = Mem(nc)

    st_ps = [nc.alloc_psum_tensor("stps%d" % i, [128, 1024], F32) for i in range(3)]
    pb_ps = [nc.alloc_psum_tensor("pbps%d" % i, [128, 512], F32) for i in range(2)]

    def bank(k):
        if k < 6:
            return st_ps[k // 2][:, (k % 2) * 512:(k % 2) * 512 + 512]
        return pb_ps[k - 6][:, :]

    def bres(k):
        return ("ps", k)

    consts = mem.alloc("consts", [128, NCONST], F32)
    ones_bf = mem.alloc("ones_bf", [128, 128], BF16)
    ones_f = mem.alloc("ones_f", [128, 128], F32)
    bones = mem.alloc("bones", [128, 128], F32)
    pmat = mem.alloc("pmat", [128, 128], F32)
    mods = mem.alloc("mods", [128, 144, 2], F32)
    sc_names = ["A1", "B1", "G1", "A2", "B2", "GM", "A3", "B3", "G3"]
    scal = {n: mem.alloc("sc" + n, [128, 16, 2], F32) for n in sc_names}
    epsb = mem.alloc("epsb", [128, 1], F32)

    s.op("sp", lambda e: e.dma_start(out=consts[:], in_=consts_d), writes=["consts"], dma="c0")
    s.op("sp", lambda e: e.dma_start(out=bones[:], in_=bones_d), writes=["bones"], dma="c1")
    s.op("sp", lambda e: e.dma_start(out=pmat[:], in_=pmat_d), writes=["pmat"], dma="c2")
    s.op("pool", lambda e: e.memset(ones_bf[:], 1.0), writes=["ones_bf"])
    s.op("pool", lambda e: e.memset(ones_f[:], 1.0), writes=["ones_f"])
    s.op("pool", lambda e: e.memset(epsb[:], EPS), writes=["epsb"])

    pmark = mem.mark()

    sc_c = mem.alloc("sc_c", [128, 16, 2], BF16)
    wslots = [mem.alloc("wslot%d" % i, [128, 16, 512], BF16) for i in range(3)]
    s.op("act", lambda e: e.activation(out=sc_c[:], in_=consts[:, C_C2:C_C2 + 32].rearrange("p (k v) -> p k v", v=2),
                                       func=AF.Silu), reads=["consts"], writes=["sc_c"])
    w_ada_r = w_ada.rearrange("(k p) c -> p k c", p=128)
    MB = 4
    for t in range(36):
        sl = t % 3
        s.op("pool", lambda e, t=t, sl=sl: e.dma_start(out=wslots[sl][:], in_=w_ada_r[:, :, t * 512:(t + 1) * 512]),
             writes=[("w", sl, 0)], dma=("w", sl, 0))
        for jj in range(4):
            j = t * 4 + jj
            for kc in range(KC):
                s.op("pe", lambda e, sl=sl, jj=jj, j=j, kc=kc: e.matmul(
                    bank(MB)[:, 2 * j:2 * j + 2], wslots[sl][:, kc, jj * 128:(jj + 1) * 128], sc_c[:, kc, :],
                    start=(kc == 0), stop=(kc == KC - 1)),
                    reads=[("w", sl, 0), "sc_c"], writes=[bres(MB)])
    mps = bank(MB)[:, 0:288].rearrange("p (j v) -> p j v", v=2)
    for v in range(2):
        s.op("dve", lambda e, v=v: e.tensor_tensor(out=mods[:, :, v], in0=mps[:, :, v], in1=consts[:, C_BADA:C_BADA + 144],
                                                  op=ALU.add), reads=[bres(MB), "consts"], writes=["mods"])

    def mk_scal(name, modi, kind, normcol):
        dst = scal[name]
        for v in range(2):
            src = mods[:, modi * 16:(modi + 1) * 16, v]
            if kind == "A":
                s.op("dve", lambda e, v=v, src=src: e.scalar_tensor_tensor(
                    out=dst[:, :, v], in0=src, scalar=1.0, in1=consts[:, normcol:normcol + 16],
                    op0=ALU.add, op1=ALU.mult), reads=["mods", "consts"], writes=["scal"])
            elif kind == "B":
                s.op("dve", lambda e, v=v, src=src: e.tensor_copy(out=dst[:, :, v], in_=src),
                     reads=["mods"], writes=["scal"])
            else:
                s.op("dve", lambda e, v=v, src=src: e.tensor_scalar(
                    out=dst[:, :, v], in0=src, scalar1=(0.5 if kind == "G" else 1.0), scalar2=None, op0=ALU.mult),
                    reads=["mods"], writes=["scal"])

    mk_scal("B1", 0, "B", 0)
    mk_scal("A1", 1, "A", C_FF1N)
    mk_scal("G1", 2, "G", 0)
    mk_scal("B2", 3, "B", 0)
    mk_scal("A2", 4, "A", C_MIXN)
    mk_scal("GM", 5, "GM", 0)
    mk_scal("B3", 6, "B", 0)
    mk_scal("A3", 7, "A", C_FF2N)
    mk_scal("G3", 8, "G", 0)
    if debug:
        s.op("sp", lambda e: e.dma_start(out=modsd, in_=mods[:].rearrange("p j v -> p (j v)")), reads=["mods"],
             writes=["modsd"], dma="dbg")

    def norm_stage(src, col0, N, Asc, Bsc, v, dst_of, xb, sq, rstd, ssb):
        nx = len(xb)
        for kc in range(KC):
            xs = kc % nx
            s.op("sp", lambda e, kc=kc, xs=xs: e.dma_start(out=xb[xs][:, 0:N], in_=src[kc, :, col0:col0 + N]),
                 writes=[("xb", xs)], dma=("xb", xs))
            q2 = kc % 2
            s.op("act", lambda e, xs=xs, q2=q2: e.activation(out=sq[q2][:, 0:N], in_=xb[xs][:, 0:N], func=AF.Square),
                 reads=[("xb", xs)], writes=[("sq", q2)])
            s.op("pe", lambda e, q2=q2, kc=kc: e.matmul(bank(ssb)[:, 0:N], ones_bf[:], sq[q2][:, 0:N],
                                                       start=(kc == 0), stop=(kc == KC - 1)),
                 reads=[("sq", q2), "ones_bf"], writes=[bres(ssb)])
        s.op("act", lambda e: e.activation(out=rstd[:, 0:N], in_=bank(ssb)[:, 0:N], func=AF.Sqrt,
                                           scale=1.0 / D, bias=epsb[:, 0:1]),
             reads=[bres(ssb), "epsb"], writes=["rstd"])
        s.op("dve", lambda e: e.reciprocal(out=rstd[:, 0:N], in_=rstd[:, 0:N]), reads=["rstd"], writes=["rstd"])
        for kc in range(KC):
            xs = kc % nx
            s.op("sp", lambda e, kc=kc, xs=xs: e.dma_start(out=xb[xs][:, 0:N], in_=src[kc, :, col0:col0 + N]),
                 writes=[("xb", xs)], dma=("xb", xs))
            s.op("dve", lambda e, xs=xs: e.tensor_tensor(out=xb[xs][:, 0:N], in0=xb[xs][:, 0:N], in1=rstd[:, 0:N],
                                                        op=ALU.mult),
                 reads=[("xb", xs), "rstd"], writes=[("xb", xs)])
            dst, dres = dst_of(kc)
            s.op("act", lambda e, xs=xs, kc=kc, dst=dst: e.activation(
                out=dst, in_=xb[xs][:, 0:N], func=AF.Identity, bias=Bsc[:, kc, v:v + 1], scale=Asc[:, kc, v:v + 1]),
                reads=[("xb", xs), "scal"], writes=[dres])

    def ffn_phase(src, dst, dst_col0, tiles, wi, wo, Asc, Bsc, Gsc, tag):
        m0 = mem.mark()
        h = mem.alloc("h", [128, KC, 1024], BF16)
        g = mem.alloc("g", [128, FC, 1024], BF16)
        wsl = [mem.alloc("wsl%d" % i, [128, 16 * 512], BF16) for i in range(3)]
        xb = [mem.alloc("xb%d" % i, [128, 512], F32) for i in range(4)]
        sq = [mem.alloc("sq%d" % i, [128, 512], BF16) for i in range(2)]
        rstd = mem.alloc("rstd", [128, 512], F32)
        sa = [mem.alloc("sa%d" % i, [128, 512], F32) for i in range(2)]
        xr = [xb[0], xb[1]]
        xo = [xb[2], xb[3]]
        wi_r = wi.rearrange("(k p) c -> p k c", p=128)
        wo_r = wo.rearrange("(f p) c -> p f c", p=128)
        SSB = 6
        wcnt = [0]
        ecnt = [0]

        def next_slot():
            sl = wcnt[0] % 3
            wcnt[0] += 1
            return sl

        for tile in tiles:
            hoff = 0
            hoffs = []
            for (c0, N, v) in tile:
                ho = hoff
                norm_stage(src, c0, N, Asc, Bsc, v,
                           lambda kc, ho=ho, N=N: (h[:, kc, ho:ho + N], ("h", kc)),
                           xb, sq, rstd, SSB)
                hoffs.append(ho)
                hoff += N
            for jg in range(FC // 2):
                sl = next_slot()
                wv = wsl[sl][:].rearrange("p (k c) -> p k c", c=512)
                s.op("pool", lambda e, jg=jg, wv=wv: e.dma_start(out=wv[:, :, 0:256],
                                                               in_=wi_r[:, :, jg * 256:(jg + 1) * 256]),
                     writes=[("w", sl, 0)], dma=("w", sl, 0))
                s.op("pool", lambda e, jg=jg, wv=wv: e.dma_start(out=wv[:, :, 256:512],
                                                               in_=wi_r[:, :, DFF + jg * 256:DFF + (jg + 1) * 256]),
                     writes=[("w", sl, 1)], dma=("w", sl, 1))
                for jj in range(2):
                    j = jg * 2 + jj
                    for hi, (c0, N, v) in enumerate(tile):
                        ho = hoffs[hi]
                        e2 = ecnt[0] % 2
                        ecnt[0] += 1
                        pa, pb = e2 * 2, e2 * 2 + 1
                        for kc in range(KC):
                            s.op("pe", lambda e, wv=wv, jj=jj, kc=kc, ho=ho, N=N, pa=pa: e.matmul(
                                bank(pa)[:, 0:N], wv[:, kc, jj * 128:(jj + 1) * 128], h[:, kc, ho:ho + N],
                                start=(kc == 0), stop=(kc == KC - 1)),
                                reads=[("w", sl, 0), ("h", kc)], writes=[bres(pa)])
                        for kc in range(KC):
                            s.op("pe", lambda e, wv=wv, jj=jj, kc=kc, ho=ho, N=N, pb=pb: e.matmul(
                                bank(pb)[:, 0:N], wv[:, kc, 256 + jj * 128:256 + (jj + 1) * 128], h[:, kc, ho:ho + N],
                                start=(kc == 0), stop=(kc == KC - 1)),
                                reads=[("w", sl, 1), ("h", kc)], writes=[bres(pb)])
                        s.op("act", lambda e, e2=e2, pa=pa, N=N: e.activation(out=sa[e2][:, 0:N], in_=bank(pa)[:, 0:N],
                                                                             func=AF.Silu),
                             reads=[bres(pa)], writes=[("sa", e2)])
                        s.op("dve", lambda e, e2=e2, pb=pb, N=N, j=j, ho=ho: e.tensor_tensor(
                            out=g[:, j, ho:ho + N], in0=bank(pb)[:, 0:N], in1=sa[e2][:, 0:N], op=ALU.mult),
                            reads=[bres(pb), ("sa", e2)], writes=[("g", j)])
            for m in range(KC):
                sl = next_slot()
                wv = wsl[sl][:, 0:FC * 128].rearrange("p (f c) -> p f c", c=128)
                s.op("pool", lambda e, m=m, wv=wv: e.dma_start(out=wv, in_=wo_r[:, :, m * 128:(m + 1) * 128]),
                     writes=[("w", sl, 0), ("w", sl, 1)], dma=("w", sl, 0))
                for hi, (c0, N, v) in enumerate(tile):
                    ho = hoffs[hi]
                    e2 = ecnt[0] % 2
                    ecnt[0] += 1
                    po = 4 + e2
                    for f in range(FC):
                        s.op("pe", lambda e, wv=wv, f=f, ho=ho, N=N, po=po: e.matmul(
                            bank(po)[:, 0:N], wv[:, f, :], g[:, f, ho:ho + N],
                            start=(f == 0), stop=(f == FC - 1)),
                            reads=[("w", sl, 0), ("w", sl, 1), ("g", f)], writes=[bres(po)])
                    s.op("sp", lambda e, e2=e2, m=m, c0=c0, N=N: e.dma_start(out=xr[e2][:, 0:N], in_=src[m, :, c0:c0 + N]),
                         writes=[("xb", e2)], dma=("xb", e2))
                    s.op("dve", lambda e, e2=e2, po=po, N=N, m=m, v=v: e.scalar_tensor_tensor(
                        out=xo[e2][:, 0:N], in0=bank(po)[:, 0:N], scalar=Gsc[:, m, v:v + 1], in1=xr[e2][:, 0:N],
                        op0=ALU.mult, op1=ALU.add),
                        reads=[bres(po), ("xb", e2), "scal"], writes=[("xb", 2 + e2)])
                    dc = c0 - dst_col0
                    s.op("sp", lambda e, e2=e2, m=m, dc=dc, N=N: e.dma_start(out=dst[m, :, dc:dc + N], in_=xo[e2][:, 0:N]),
                         reads=[("xb", 2 + e2)], writes=[(tag, m)], dma=("xb", 2 + e2))
        mem.reset(m0)

    s.barrier()
    mem.reset(pmark)
    tiles1 = [[(0, 512, 0), (512, 512, 0)], [(1024, 512, 0), (1536, 512, 0)], [(2048, 512, 0), (2560, 256, 1)]]
    if stop_after != "ada" and not skip_ffn1:
        ffn_phase(xin, x1, 0, tiles1, w1i, w1o, scal["A1"], scal["B1"], scal["G1"], "x1")

    if stop_after in ("ada", "ffn1"):
        s.emit()
        return nc

    s.barrier()
    mem.reset(pmark)
    sqacc_a = mem.alloc("sqacc_a", [128, OWN], F32)
    sqacc_c = mem.alloc("sqacc_c", [128, OWN], F32)
    s.op("pool", lambda e: e.memset(sqacc_a[:], 0.0), writes=["sqacc_a"])
    s.op("pool", lambda e: e.memset(sqacc_c[:], 0.0), writes=["sqacc_c"])
    pmark2 = mem.mark()
    hm = mem.alloc("hm", [128, KC, TT], BF16)
    xb = [mem.alloc("xb%d" % i, [128, 512], F32) for i in range(4)]
    sq = [mem.alloc("sq%d" % i, [128, 512], BF16) for i in range(2)]
    rstd = mem.alloc("rstd", [128, 512], F32)
    wbig = mem.alloc("wbig", [128, KC, 512], BF16)
    cosT = mem.alloc("cosT", [128, TT], F32)
    sinT = mem.alloc("sinT", [128, TT], F32)
    s.op("sp", lambda e: e.dma_start(out=cosT[:], in_=cs_d[0]), writes=["cosT"], dma="cos")
    s.op("sp", lambda e: e.dma_start(out=sinT[:], in_=cs_d[1]), writes=["sinT"], dma="sin")
    wtiles = [(i * 512, 512, 0) for i in range(5)] + [(WT, CTX, 1)]
    for (c0, N, v) in wtiles:
        norm_stage(x1, c0, N, scal["A2"], scal["B2"], v,
                   lambda kc, c0=c0, N=N: (hm[:, kc, c0:c0 + N], ("hm", kc)), xb, sq, rstd, 6)
    if stop_after == "projN":
        s.emit()
        return nc
    w_in_r = w_in.rearrange("(k p) c -> p k c", p=128)

    vst = [mem.alloc("vst%d" % i, [128, 512], BF16) for i in range(2)]
    cnt = 0
    for hf in range(2):
        s.op("pool", lambda e, hf=hf: e.dma_start(out=wbig[:], in_=w_in_r[:, :, 2048 + hf * 512:2560 + hf * 512]),
             writes=["wbig0"], dma="wbig0")
        for tg in range(22):
            e2 = cnt % 2
            cnt += 1
            pbk = 4 + e2
            for kc in range(KC):
                s.op("pe", lambda e, tg=tg, hf=hf, kc=kc, pbk=pbk: e.matmul(
                    bank(pbk)[:, :], hm[:, kc, tg * 128:(tg + 1) * 128], wbig[:, kc, :],
                    start=(kc == 0), stop=(kc == KC - 1)),
                    reads=["wbig0", ("hm", kc)], writes=[bres(pbk)])
            s.op("act", lambda e, e2=e2, pbk=pbk: e.activation(out=vst[e2][:], in_=bank(pbk)[:, :], func=AF.Copy),
                 reads=[bres(pbk)], writes=[("vst", e2)])
            s.op("sp", lambda e, e2=e2, tg=tg, hf=hf: e.dma_start(out=vtok[tg, :, hf * 512:(hf + 1) * 512], in_=vst[e2][:]),
                 reads=[("vst", e2)], writes=["vtok"], dma=("vst", e2))

    if stop_after == "projV":
        s.emit()
        return nc
    NSL = 4
    wsm = [wbig[:, :, i * 128:(i + 1) * 128] for i in range(NSL)]
    wsm_cnt = [0]

    def load_w(colbase):
        sl = wsm_cnt[0] % NSL
        wsm_cnt[0] += 1
        s.op("pool", lambda e, sl=sl, colbase=colbase: e.dma_start(out=wsm[sl], in_=w_in_r[:, :, colbase:colbase + 128]),
             reads=[], writes=[("wsm", sl), "wbig0"] if wsm_cnt[0] <= NSL else [("wsm", sl)],
             dma=("wsm", sl))
        return sl

    sqf = [xb[0], xb[1]]
    raw = [xb[2], xb[3]]
    sd = [mem.alloc("sd%d" % i, [128, 512], F32) for i in range(2)]
    qn = [mem.alloc("qn%d" % i, [128, 512], F32) for i in range(2)]
    t1 = [mem.alloc("t1%d" % i, [128, 512], F32) for i in range(2)]
    t2 = [mem.alloc("t2%d" % i, [128, 512], F32) for i in range(2)]
    qr = [mem.alloc("qr%d" % i, [128, 512], BF16) for i in range(2)]
    pcnt = [0]

    def qk_stages(sl, hcol0, N, gcol, cscol0, dst_ap, dres):
        i2 = pcnt[0] % 2
        pcnt[0] += 1
        praw, pss, prot = 4 + i2, 0 + i2, 2 + i2

        def stA():
            for kc in range(KC):
                s.op("pe", lambda e, kc=kc: e.matmul(bank(praw)[:, 0:N], wsm[sl][:, kc, :], hm[:, kc, hcol0:hcol0 + N],
                                                     start=(kc == 0), stop=(kc == KC - 1)),
                     reads=[("wsm", sl), ("hm", kc)], writes=[bres(praw)])
            s.op("act", lambda e: e.activation(out=sqf[i2][:, 0:N], in_=bank(praw)[:, 0:N], func=AF.Square),
                 reads=[bres(praw)], writes=[("xb", i2)])
            s.op("dve", lambda e: e.tensor_copy(out=raw[i2][:, 0:N], in_=bank(praw)[:, 0:N]),
                 reads=[bres(praw)], writes=[("xb", 2 + i2)])

        def stB():
            s.op("pe", lambda e: e.matmul(bank(pss)[:, 0:N], bones[:], sqf[i2][:, 0:N], start=True, stop=True),
                 reads=["bones", ("xb", i2)], writes=[bres(pss)])
            s.op("act", lambda e: e.activation(out=sd[i2][:, 0:N], in_=bank(pss)[:, 0:N], func=AF.Sqrt,
                                               scale=1.0 / 64.0, bias=epsb[:, 0:1]),
                 reads=[bres(pss), "epsb"], writes=[("sd", i2)])
            s.op("dve", lambda e: e.reciprocal(out=sd[i2][:, 0:N], in_=sd[i2][:, 0:N]),
                 reads=[("sd", i2)], writes=[("sd", i2)])
            s.op("dve", lambda e: e.scalar_tensor_tensor(out=qn[i2][:, 0:N], in0=raw[i2][:, 0:N],
                                                         scalar=consts[:, gcol:gcol + 1], in1=sd[i2][:, 0:N],
                                                         op0=ALU.mult, op1=ALU.mult),
                 reads=[("xb", 2 + i2), ("sd", i2), "consts"], writes=[("qn", i2)])

        def stC():
            s.op("pe", lambda e: e.matmul(bank(prot)[:, 0:N], pmat[:], qn[i2][:, 0:N], start=True, stop=True),
                 reads=["pmat", ("qn", i2)], writes=[bres(prot)])
            s.op("pool", lambda e: e.tensor_tensor(out=t1[i2][:, 0:N], in0=qn[i2][:, 0:N], in1=cosT[:, cscol0:cscol0 + N],
                                                   op=ALU.mult),
                 reads=[("qn", i2), "cosT"], writes=[("t1", i2)])
            s.op("dve", lambda e: e.tensor_tensor(out=t2[i2][:, 0:N], in0=bank(prot)[:, 0:N], in1=sinT[:, cscol0:cscol0 + N],
                                                  op=ALU.mult),
                 reads=[bres(prot), "sinT"], writes=[("t2", i2)])
            s.op("pool", lambda e: e.tensor_tensor(out=qr[i2][:, 0:N], in0=t1[i2][:, 0:N], in1=t2[i2][:, 0:N], op=ALU.add),
                 reads=[("t1", i2), ("t2", i2)], writes=[("qr", i2)])
            s.op("sp", lambda e: e.dma_start(out=dst_ap, in_=qr[i2][:, 0:N]), reads=[("qr", i2)], writes=[dres],
                 dma=("qr", i2))

        return stA, stB, stC

    chunks = []
    for c in range(8):
        chunks.append((1024 + c * 128,
                       [(c0, N, C_KG, c0, kT[c, :, c0:c0 + N], ("kT", c)) for (c0, N, v) in wtiles]))
        chunks.append((c * 128,
                       [(OWN0 + i * 512, 512, C_QG, OWN0 + i * 512, qT[c, :, i * 512:(i + 1) * 512], ("qT", c))
                        for i in range(4)]))
    slots = {0: load_w(chunks[0][0])}
    pend = []
    for k, (colbase, tl) in enumerate(chunks):
        if k + 1 < len(chunks):
            slots[k + 1] = load_w(chunks[k + 1][0])
        for targs in tl:
            stA, stB, stC = qk_stages(slots[k], *targs)
            stA()
            pend.append((stB, stC))
            if len(pend) >= 2:
                pend[-2][0]()
            if len(pend) >= 3:
                pend[-3][1]()
    pend[-1][0]()
    pend[-2][1]()
    pend[-1][1]()

    if stop_after == "projQK":
        s.emit()
        return nc
    zbuf = mem.alloc("zbuf", [128, OWN + 2], F32)
    acc = mem.alloc("acc", [128, OWN], F32)
    cS = [mem.alloc("cS%d" % i, [128, 512], F32) for i in range(2)]
    zh = mem.alloc("zh", [128, 2], F32)
    yv = sqf
    ysq = raw
    ybf = qr
    ccnt = 0
    for cc in range(8):
        slB = load_w(3072 + cc * 128)
        slC = load_w(4096 + cc * 128)
        slU = load_w(5120 + cc * 128)
        for i in range(4):
            c0 = OWN0 + i * 512
            i2 = ccnt % 2
            ccnt += 1
            pC, pU = 0 + i2, 2 + i2
            for kc in range(KC):
                s.op("pe", lambda e, kc=kc, c0=c0, pC=pC, slC=slC: e.matmul(bank(pC)[:, :], wsm[slC][:, kc, :], hm[:, kc, c0:c0 + 512],
                                                                  start=(kc == 0), stop=(kc == KC - 1)),
                     reads=[("wsm", slC), ("hm", kc)], writes=[bres(pC)])
            for kc in range(KC):
                s.op("pe", lambda e, kc=kc, c0=c0, pU=pU, slU=slU: e.matmul(bank(pU)[:, :], wsm[slU][:, kc, :], hm[:, kc, c0:c0 + 512],
                                                                  start=(kc == 0), stop=(kc == KC - 1)),
                     reads=[("wsm", slU), ("hm", kc)], writes=[bres(pU)])
            s.op("act", lambda e, i2=i2, pC=pC: e.activation(out=cS[i2][:], in_=bank(pC)[:, :], func=AF.Copy),
                 reads=[bres(pC)], writes=[("cS", i2)])
            s.op("dve", lambda e, i2=i2, pU=pU, i=i: e.tensor_tensor(out=zbuf[:, 1 + i * 512:1 + (i + 1) * 512],
                                                                    in0=bank(pU)[:, :], in1=cS[i2][:], op=ALU.mult),
                 reads=[bres(pU), ("cS", i2)], writes=["zbuf"])
        pH = 6
        for hi, col in enumerate((OWN0 - 1, OWN0 + OWN)):
            for wi_, sl_ in enumerate((slC, slU)):
                for kc in range(KC):
                    s.op("pe", lambda e, kc=kc, col=col, hi=hi, wi_=wi_, sl_=sl_: e.matmul(
                        bank(pH)[:, wi_ * 2 + hi:wi_ * 2 + hi + 1], wsm[sl_][:, kc, :], hm[:, kc, col:col + 1],
                        start=(kc == 0), stop=(kc == KC - 1)),
                        reads=[("wsm", sl_), ("hm", kc)], writes=[bres(pH)])
        s.op("act", lambda e: e.activation(out=zh[:], in_=bank(pH)[:, 0:2], func=AF.Copy), reads=[bres(pH)], writes=["zh"])
        s.op("dve", lambda e: e.tensor_tensor(out=zh[:], in0=bank(pH)[:, 2:4], in1=zh[:], op=ALU.mult),
             reads=[bres(pH), "zh"], writes=["zh"])
        s.op("dve", lambda e: e.tensor_tensor(out=zbuf[:, 0:OWN + 2:OWN + 1], in0=zh[:], in1=consts[:, C_HALO:C_HALO + 2],
                                              op=ALU.mult),
             reads=["zh", "consts"], writes=["zbuf"])
        cw = C_CW + cc * 3
        s.op("dve", lambda e, cw=cw, cc=cc: e.tensor_scalar(out=acc[:], in0=zbuf[:, 1:OWN + 1], scalar1=consts[:, cw + 1:cw + 2],
                                                           scalar2=consts[:, C_CB + cc:C_CB + cc + 1], op0=ALU.mult, op1=ALU.add),
             reads=["zbuf", "consts"], writes=["acc"])
        s.op("dve", lambda e, cw=cw: e.scalar_tensor_tensor(out=acc[:], in0=zbuf[:, 0:OWN], scalar=consts[:, cw:cw + 1],
                                                            in1=acc[:], op0=ALU.mult, op1=ALU.add),
             reads=["zbuf", "consts", "acc"], writes=["acc"])
        s.op("dve", lambda e, cw=cw: e.scalar_tensor_tensor(out=acc[:], in0=zbuf[:, 2:OWN + 2], scalar=consts[:, cw + 2:cw + 3],
                                                           in1=acc[:], op0=ALU.mult, op1=ALU.add),
             reads=["zbuf", "consts", "acc"], writes=["acc"])
        for i in range(4):
            c0 = OWN0 + i * 512
            i2 = ccnt % 2
            ccnt += 1
            pB = 4 + i2
            for kc in range(KC):
                s.op("pe", lambda e, kc=kc, c0=c0, pB=pB, slB=slB: e.matmul(bank(pB)[:, :], wsm[slB][:, kc, :], hm[:, kc, c0:c0 + 512],
                                                                  start=(kc == 0), stop=(kc == KC - 1)),
                     reads=[("wsm", slB), ("hm", kc)], writes=[bres(pB)])
            s.op("dve", lambda e, i2=i2, pB=pB, i=i: e.tensor_tensor(out=yv[i2][:], in0=bank(pB)[:, :],
                                                                    in1=acc[:, i * 512:(i + 1) * 512], op=ALU.mult),
                 reads=[bres(pB), "acc"], writes=[("xb", i2)])
            s.op("act", lambda e, i2=i2: e.activation(out=ysq[i2][:], in_=yv[i2][:], func=AF.Square),
                 reads=[("xb", i2)], writes=[("xb", 2 + i2)])
            s.op("pool", lambda e, i2=i2, i=i: e.tensor_tensor(out=sqacc_c[:, i * 512:(i + 1) * 512],
                                                              in0=sqacc_c[:, i * 512:(i + 1) * 512], in1=ysq[i2][:], op=ALU.add),
                 reads=[("xb", 2 + i2), "sqacc_c"], writes=["sqacc_c"])
            s.op("act", lambda e, i2=i2, cc=cc: e.activation(out=ybf[i2][:], in_=yv[i2][:], func=AF.Identity,
                                                            scale=consts[:, C_ONC + cc:C_ONC + cc + 1]),
                 reads=[("xb", i2), "consts"], writes=[("qr", i2)])
            s.op("sp", lambda e, i2=i2, cc=cc, i=i: e.dma_start(out=aoT[8 + cc, :, i * 512:(i + 1) * 512], in_=ybf[i2][:]),
                 reads=[("qr", i2)], writes=[("aoT", 8 + cc)], dma=("qr", i2))

    if stop_after == "proj":
        s.emit()
        return nc

    s.barrier()
    mem.reset(pmark2)
    KTs = [mem.alloc("KT%d" % i, [128, TT], BF16) for i in range(2)]
    QTs = [mem.alloc("QT%d" % i, [128, OWN], BF16) for i in range(2)]
    Vs = [mem.alloc("V%d" % i, [128, 22, 128], BF16) for i in range(2)]
    Bstd = [mem.alloc("Bstd%d" % i, [128, 2 * 5 * 128], F32) for i in range(2)]
    Bspec = [mem.alloc("Bspec%d" % i, [128, 2 * 22 * 128], F32) for i in range(2)]
    aob = [mem.alloc("aob%d" % i, [128, OWN], BF16) for i in range(2)]
    tmp = [mem.alloc("tmp%d" % i, [128, 768], F32) for i in range(3)]
    pt = [mem.alloc("pt%d" % i, [128, 1024], BF16) for i in range(3)]
    rden = [mem.alloc("rden%d" % i, [128, 128], F32) for i in range(2)]
    at = [mem.alloc("at%d" % i, [128, 128], F32) for i in range(2)]
    asq = [mem.alloc("asq%d" % i, [128, 128], F32) for i in range(2)]
    vtok_r = vtok.rearrange("g p f -> p g f")

    def attn_loads(c):
        c2 = c % 2
        s.op("sp", lambda e: e.dma_start(out=KTs[c2][:], in_=kT[c]), reads=[("kT", c)], writes=[("KT", c2)],
             dma=("KT", c2))
        s.op("sp", lambda e: e.dma_start(out=QTs[c2][:], in_=qT[c]), reads=[("qT", c)], writes=[("QT", c2)],
             dma=("QT", c2))
        s.op("sp", lambda e: e.dma_start(out=Vs[c2][:], in_=vtok_r[:, :, c * 128:(c + 1) * 128]),
             reads=["vtok"], writes=[("V", c2)], dma=("V", c2))
        s.op("sp", lambda e: e.dma_start(out=Bstd[c2][:], in_=tstd_d[c]), writes=[("Bstd", c2)], dma=("Bstd", c2))
        s.op("sp", lambda e: e.dma_start(out=Bspec[c2][:], in_=tspec_d[c]), writes=[("Bspec", c2)],
             dma=("Bspec", c2))

    def head_block(idx, c, j, hh):
        c2 = c % 2
        if j == 0:
            pl, tb, so = list(range(0, 6)), Bspec, 0
        elif j == 1:
            pl, tb, so = list(range(1, 6)), Bspec, 6
        elif j == 14:
            pl, tb, so = list(range(14, 19)), Bspec, 11
        elif j == 15:
            pl, tb, so = list(range(14, 20)), Bspec, 16
        else:
            pl, tb, so = list(range(j, j + 5)), Bstd, 0
        nsl = 22 if tb is Bspec else 5
        tres = ("Bspec", c2) if tb is Bspec else ("Bstd", c2)
        L = len(pl)
        b2 = (c * 16 + j) % 2
        pnd = 6 + b2
        h2 = idx % 3
        hp0, hp1 = hh * 64, hh * 64 + 64
        stt = st_ps[h2]
        sres = [bres(2 * h2), bres(2 * h2 + 1)]
        plx = pl + [20, 21]

        def st_S():
            for si, p in enumerate(plx):
                kcol = p * 128
                s.op("pe", lambda e, si=si, kcol=kcol: e.matmul(
                    stt[:, si * 128:(si + 1) * 128], KTs[c2][hp0:hp1, kcol:kcol + 128],
                    QTs[c2][hp0:hp1, j * 128:(j + 1) * 128], start=True, stop=True),
                    reads=[("KT", c2), ("QT", c2)], writes=[sres[si // 4]])

        def st_R():
            tbv = tb[c2][:, (hh * nsl + so) * 128:(hh * nsl + so + L) * 128]
            s.op("act", lambda e: e.activation(out=pt[h2][:, L * 128:(L + 2) * 128],
                                               in_=stt[:, L * 128:(L + 2) * 128], func=AF.Exp, scale=0.125),
                 reads=sres[1:], writes=[("ptc", h2)])
            s.op("dve", lambda e: e.scalar_tensor_tensor(
                out=tmp[h2][:, 0:L * 128], in0=stt[:, 0:L * 128], scalar=0.125, in1=tbv, op0=ALU.mult, op1=ALU.add),
                reads=sres + [tres], writes=[("tmp", h2)])
            s.op("act", lambda e: e.activation(out=pt[h2][:, 0:L * 128], in_=tmp[h2][:, 0:L * 128], func=AF.Exp),
                 reads=[("tmp", h2)], writes=[("pt", h2)])
            for si, p in enumerate(plx):
                first, last = (si == 0), (si == L + 1)
                s.op("pe", lambda e, si=si, p=p, first=first, last=last: e.matmul(
                    bank(pnd)[hp0:hp1, 0:128], Vs[c2][:, p, hh * 64:(hh + 1) * 64], pt[h2][:, si * 128:(si + 1) * 128],
                    start=first, stop=last, skip_group_check=True),
                    reads=[("V", c2), ("pt", h2), ("ptc", h2)], writes=[bres(pnd)])
                s.op("pe", lambda e, si=si, first=first, last=last: e.matmul(
                    bank(pnd)[hp0:hp1, 128:256], ones_bf[:, 0:64], pt[h2][:, si * 128:(si + 1) * 128],
                    start=False, stop=last, skip_group_check=True),
                    reads=["ones_bf", ("pt", h2), ("ptc", h2)], writes=[bres(pnd)])

        def st_E():
            if hh == 1:
                s.op("dve", lambda e: e.reciprocal(out=rden[b2][:], in_=bank(pnd)[:, 128:256]),
                     reads=[bres(pnd)], writes=[("rden", b2)])
                s.op("dve", lambda e: e.tensor_tensor(out=at[b2][:], in0=bank(pnd)[:, 0:128], in1=rden[b2][:],
                                                      op=ALU.mult),
                     reads=[bres(pnd), ("rden", b2)], writes=[("at", b2)])
                s.op("act", lambda e: e.activation(out=asq[b2][:], in_=at[b2][:], func=AF.Square),
                     reads=[("at", b2)], writes=[("asq", b2)])
                s.op("pool", lambda e: e.tensor_tensor(out=sqacc_a[:, j * 128:(j + 1) * 128],
                                                       in0=sqacc_a[:, j * 128:(j + 1) * 128], in1=asq[b2][:], op=ALU.add),
                     reads=[("asq", b2), "sqacc_a"], writes=["sqacc_a"])
                s.op("act", lambda e: e.activation(out=aob[c2][:, j * 128:(j + 1) * 128], in_=at[b2][:],
                                                   func=AF.Identity, scale=consts[:, C_ONA + c:C_ONA + c + 1]),
                     reads=[("at", b2), "consts"], writes=[("aob", c2)])
                if j == 15:
                    s.op("sp", lambda e: e.dma_start(out=aoT[c], in_=aob[c2][:]), reads=[("aob", c2)],
                         writes=[("aoT", c)], dma=("aob", c2))

        return st_S, st_R, st_E

    hbs = [(c, j, hh) for c in range(8) for j in range(16) for hh in range(2)]
    attn_loads(0)
    attn_loads(1)
    stg = {}
    for i in range(min(2, len(hbs))):
        stg[i] = head_block(i, *hbs[i])
        stg[i][0]()
    pend_e = None
    for i in range(len(hbs)):
        if i + 2 < len(hbs):
            stg[i + 2] = head_block(i + 2, *hbs[i + 2])
            stg[i + 2][0]()
        stg[i][1]()
        if pend_e is not None:
            pend_e()
            pend_e = None
        c, j, hh = hbs[i]
        if hh == 1:
            pend_e = stg[i][2]
        if j == 15 and hh == 1 and c + 2 < 8:
            attn_loads(c + 2)
        del stg[i]
    if pend_e is not None:
        pend_e()

    if stop_after == "attn":
        s.emit()
        return nc

    s.barrier()
    mem.reset(pmark2)
    wo = mem.alloc("wo", [128, KC, D], BF16)
    w_out_r = w_out.rearrange("(k p) c -> p k c", p=128)
    for i in range(4):
        s.op("pool", lambda e, i=i: e.dma_start(out=wo[:, i * 4:(i + 1) * 4, :], in_=w_out_r[:, i * 4:(i + 1) * 4, :]),
             writes=[("wo", i)], dma=("wo", i))
    rs = [mem.alloc("rs%d" % i, [128, OWN], F32) for i in range(2)]
    for bi, sqa in enumerate((sqacc_a, sqacc_c)):
        rname = "sqacc_a" if bi == 0 else "sqacc_c"
        for i in range(4):
            pbk = i % 2
            s.op("pe", lambda e, i=i, sqa=sqa, pbk=pbk: e.matmul(bank(pbk)[:, :], ones_f[:], sqa[:, i * 512:(i + 1) * 512],
                                                                start=True, stop=True),
                 reads=["ones_f", rname], writes=[bres(pbk)])
            s.op("act", lambda e, i=i, bi=bi, pbk=pbk: e.activation(out=rs[bi][:, i * 512:(i + 1) * 512], in_=bank(pbk)[:, :],
                                                                   func=AF.Sqrt, scale=1.0 / 1024.0, bias=epsb[:, 0:1]),
                 reads=[bres(pbk), "epsb"], writes=[("rs", bi)])
        s.op("dve", lambda e, bi=bi: e.reciprocal(out=rs[bi][:], in_=rs[bi][:]), reads=[("rs", bi)], writes=[("rs", bi)])
    aot = [mem.alloc("aot%d" % i, [128, KC, 512], BF16) for i in range(2)]
    x1t = [mem.alloc("x1t%d" % i, [128, 512], F32) for i in range(2)]
    u1 = [mem.alloc("u1%d" % i, [128, 512], F32) for i in range(2)]
    u2 = [mem.alloc("u2%d" % i, [128, 512], F32) for i in range(2)]
    u3 = [mem.alloc("u3%d" % i, [128, 512], F32) for i in range(2)]
    aoT_r = aoT.rearrange("k p t -> p k t")
    mcnt = 0
    def load_aot(n):
        n2 = n % 2
        s.op("sp", lambda e: e.dma_start(out=aot[n2][:], in_=aoT_r[:, :, n * 512:(n + 1) * 512]),
             reads=[("aoT", k) for k in range(KC)], writes=[("aot", n2)], dma=("aot", n2))

    load_aot(0)
    for n in range(4):
        n2 = n % 2
        if n + 1 < 4:
            load_aot(n + 1)
        for m in range(KC):
            i2 = mcnt % 2
            mcnt += 1
            pA, pC = 0 + i2, 2 + i2
            for kc in range(8):
                s.op("pe", lambda e, kc=kc, m=m, n2=n2, pA=pA: e.matmul(bank(pA)[:, :], wo[:, kc, m * 128:(m + 1) * 128],
                                                                        aot[n2][:, kc, :], start=(kc == 0), stop=(kc == 7)),
                     reads=[("wo", kc // 4), ("aot", n2)], writes=[bres(pA)])
            for kc in range(8, 16):
                s.op("pe", lambda e, kc=kc, m=m, n2=n2, pC=pC: e.matmul(bank(pC)[:, :], wo[:, kc, m * 128:(m + 1) * 128],
                                                                        aot[n2][:, kc, :], start=(kc == 8), stop=(kc == 15)),
                     reads=[("wo", kc // 4), ("aot", n2)], writes=[bres(pC)])
            s.op("sp", lambda e, i2=i2, m=m, n=n: e.dma_start(out=x1t[i2][:], in_=x1[m, :, OWN0 + n * 512:OWN0 + (n + 1) * 512]),
                 reads=[("x1", m)], writes=[("x1t", i2)], dma=("x1t", i2))
            s.op("dve", lambda e, i2=i2, pA=pA, n=n: e.tensor_tensor(out=u1[i2][:], in0=bank(pA)[:, :],
                                                                    in1=rs[0][:, n * 512:(n + 1) * 512], op=ALU.mult),
                 reads=[bres(pA), ("rs", 0)], writes=[("u1", i2)])
            s.op("dve", lambda e, i2=i2, pC=pC, n=n: e.tensor_tensor(out=u2[i2][:], in0=bank(pC)[:, :],
                                                                    in1=rs[1][:, n * 512:(n + 1) * 512], op=ALU.mult),
                 reads=[bres(pC), ("rs", 1)], writes=[("u2", i2)])
            s.op("pool", lambda e, i2=i2: e.tensor_tensor(out=u3[i2][:], in0=u1[i2][:], in1=u2[i2][:], op=ALU.add),
                 reads=[("u1", i2), ("u2", i2)], writes=[("u3", i2)])
            s.op("dve", lambda e, i2=i2, m=m: e.scalar_tensor_tensor(out=u3[i2][:], in0=u3[i2][:], scalar=scal["GM"][:, m, 0:1],
                                                                     in1=x1t[i2][:], op0=ALU.mult, op1=ALU.add),
                 reads=[("u3", i2), ("x1t", i2), "scal"], writes=[("u3", i2)])
            s.op("sp", lambda e, i2=i2, m=m, n=n: e.dma_start(out=x2[m, :, n * 512:(n + 1) * 512], in_=u3[i2][:]),
                 reads=[("u3", i2)], writes=[("x2", m)], dma=("u3", i2))

    if stop_after == "mix":
        s.emit()
        return nc

    s.barrier()
    mem.reset(pmark)
    tiles2 = [[(0, 512, 0), (512, 512, 0)], [(1024, 512, 0), (1536, 512, 0)]]
    ffn_phase(x2, yout, 0, tiles2, w2i, w2o, scal["A3"], scal["B3"], scal["G3"], "yout")
    s.emit()
    return nc


def _fm(vec, nchunk):
    return np.ascontiguousarray(np.asarray(vec, np.float32).reshape(nchunk, 128).T)


def _bias_table(rpb, core, j, pairs):
    H = rpb.shape[0]
    out = np.full((H, len(pairs), 128, 128), NEG, np.float32)
    rows = 256
    qrows = [32 * core + 2 * j, 32 * core + 2 * j + 1]
    cols = np.arange(64)
    cstart = np.clip(cols - 8, 0, 64 - 16)
    for si, p in enumerate(pairs):
        for ki in range(2):
            kr = 32 * core - 4 + 2 * p + ki
            if kr < 0 or kr >= rows:
                continue
            for qi, qr in enumerate(qrows):
                rs = min(max(qr - 4, 0), rows - 8)
                if not (rs <= kr < rs + 8):
                    continue
                dr = kr - qr + 7
                kcg, qcg = np.meshgrid(cols, cols, indexing="ij")
                valid = (kcg >= cstart[qcg]) & (kcg < cstart[qcg] + 16)
                dc = kcg - qcg + 15
                dcc = np.clip(dc, 0, 30)
                blk = np.where(valid[None], rpb[:, dr][:, dcc], np.float32(NEG))
                out[:, si, ki * 64:(ki + 1) * 64, qi * 64:(qi + 1) * 64] = blk
    return out


def _prep_core(core, inp, shared):
    x = inp["x"][0]
    r0 = 32 * core - 4
    win = np.zeros((WROWS * GW, D), np.float32)
    lo, hi = max(r0, 0), min(r0 + WROWS, 256)
    win[(lo - r0) * GW:(hi - r0) * GW] = x[lo * GW:hi * GW]
    xin = np.concatenate([win, inp["ctx"][0]], axis=0)
    xin = np.ascontiguousarray(xin.T.reshape(KC, 128, TT))
    consts = shared["consts"].copy()
    consts[:, C_HALO] = 0.0 if core == 0 else 1.0
    consts[:, C_HALO + 1] = 0.0 if core == NCORES - 1 else 1.0
    t = np.arange(WT)
    prow = (r0 + t // GW).astype(np.float32)
    pcol = (t % GW).astype(np.float32)
    p = np.arange(128)
    d = p % 64
    inv = (10000.0 ** (-(np.arange(16, dtype=np.float32)) / 16)).astype(np.float32)
    invp = inv[d % 16]
    pos = np.where((d // 32 == 0)[:, None], prow[None, :], pcol[None, :]).astype(np.float32)
    ang = (pos * invp[:, None]).astype(np.float32)
    cs = np.zeros((2, 128, TT), np.float32)
    cs[0, :, :WT] = np.cos(ang)
    cs[1, :, :WT] = np.sin(ang)
    cs[0, :, WT:] = 1.0
    rpb = inp["rpb"][0]
    spec = []
    for j, pairs in ((0, list(range(0, 6))), (1, list(range(1, 6))), (14, list(range(14, 19))), (15, list(range(14, 20)))):
        spec.append(_bias_table(rpb, core, j, pairs))
    spec = np.concatenate(spec, axis=1)
    tspec = spec.reshape(8, 2, 22, 128, 128).transpose(0, 3, 1, 2, 4).reshape(8, 128, 2 * 22 * 128)
    m = {"xin": xin, "consts": consts, "cossin": cs, "tspec": np.ascontiguousarray(tspec)}
    return m


def _prep_shared(inp):
    consts = np.zeros((128, NCONST), np.float32)
    consts[:, C_BADA:C_BADA + 144] = _fm(inp["b_ada"][0], 144)
    consts[:, C_FF1N:C_FF1N + 16] = _fm(inp["ff1_norm"][0], 16)
    consts[:, C_MIXN:C_MIXN + 16] = _fm(inp["mix_norm"][0], 16)
    consts[:, C_FF2N:C_FF2N + 16] = _fm(inp["ff2_norm"][0], 16)
    consts[:, C_QG] = np.tile(inp["q_norm"][0], 2)
    consts[:, C_KG] = np.tile(inp["k_norm"][0], 2)
    consts[:, C_ONA:C_ONA + 8] = _fm(inp["out_norm_attn"][0], 8)
    consts[:, C_ONC:C_ONC + 8] = _fm(inp["out_norm_conv"][0], 8)
    cw = inp["conv_w"][0]
    for cc in range(8):
        for jj in range(3):
            consts[:, C_CW + cc * 3 + jj] = cw[jj, cc * 128:(cc + 1) * 128]
    consts[:, C_CB:C_CB + 8] = _fm(inp["conv_b"][0], 8)
    c2 = np.stack([_fm(inp["c"][0], 16), _fm(inp["c_ctx"], 16)], axis=-1)
    consts[:, C_C2:C_C2 + 32] = c2.reshape(128, 32)
    bones = np.zeros((128, 128), np.float32)
    bones[:64, :64] = 1.0
    bones[64:, 64:] = 1.0
    pm = np.zeros((128, 128), np.float32)
    for mm in range(128):
        if (mm % 32) < 16:
            pm[mm + 16, mm] = -1.0
        else:
            pm[mm - 16, mm] = 1.0
    std = _bias_table(inp["rpb"][0], 1, 5, list(range(5, 10)))
    tstd = std.reshape(8, 2, 5, 128, 128).transpose(0, 3, 1, 2, 4).reshape(8, 128, 2 * 5 * 128)
    sh = {
        "consts": consts, "bones": bones, "pmat": pm, "tstd": np.ascontiguousarray(tstd),
        "w_ada": np.ascontiguousarray(inp["w_ada"][0]),
        "ff1_w_in": np.ascontiguousarray(inp["ff1_w_in"][0]), "ff1_w_out": np.ascontiguousarray(inp["ff1_w_out"][0]),
        "w_in": np.ascontiguousarray(inp["w_in"][0]), "w_out": np.ascontiguousarray(inp["w_out"][0]),
        "ff2_w_in": np.ascontiguousarray(inp["ff2_w_in"][0]), "ff2_w_out": np.ascontiguousarray(inp["ff2_w_out"][0]),
    }
    return sh


def make_in_maps(inp):
    inp = {k: np.asarray(v) for k, v in inp.items()}
    sh = _prep_shared(inp)
    maps = []
    for core in range(NCORES):
        m = dict(sh)
        m.update(_prep_core(core, inp, sh))
        maps.append(m)
    return maps


def kernel(**inputs):
    maps = make_in_maps(inputs)
    nc = build_nc()
    res = run_bass_kernel_spmd(nc, maps, core_ids=list(range(NCORES)))
    outs = []
    for r in res.results:
        y = np.asarray(r["yout"]).reshape(D, OWN)
        outs.append(y.T)
    out = np.concatenate(outs, axis=0).reshape(1, 16384, D).astype(np.float32)
    return out
```

```python
import contextlib
import numpy as np
import concourse.bass as bass
import concourse.mybir as mybir
from concourse.bass_utils import run_bass_kernel_spmd

F32 = mybir.dt.float32
BF16 = mybir.dt.bfloat16
ALU = mybir.AluOpType
AF = mybir.ActivationFunctionType

NCORES = 8
D = 2048
KC = 16
DFF = 5632
FC = 44
GW = 64
WROWS = 40
WT = WROWS * GW
CTX = 256
TT = WT + CTX
OWN0 = 4 * GW
OWN = 2048
EPS = 1e-6
NEG = -30000.0

C_BADA = 0
C_FF1N = C_BADA + 144
C_MIXN = C_FF1N + 16
C_FF2N = C_MIXN + 16
C_QG = C_FF2N + 16
C_KG = C_QG + 1
C_ONA = C_KG + 1
C_ONC = C_ONA + 8
C_CW = C_ONC + 8
C_CB = C_CW + 24
C_HALO = C_CB + 8
C_C2 = C_HALO + 2
NCONST = C_C2 + 32

ENGS = ("pe", "act", "dve", "pool", "sp")


class _Op:
    __slots__ = ("eng", "fn", "deps", "inc", "dma", "idx", "epoch")

    def __init__(self, eng, fn, dma, epoch):
        self.eng = eng
        self.fn = fn
        self.deps = {}
        self.inc = False
        self.dma = dma
        self.idx = -1
        self.epoch = epoch


class Sched:
    def __init__(self, nc):
        self.nc = nc
        self.ops = {e: [] for e in ENGS}
        self.lastw = {}
        self.readers = {}
        self.dma_cnt = {}
        self.known = {e: {} for e in ENGS}
        self.pending = {e: {} for e in ENGS}
        self.epoch = 0

    def _tok_epoch(self, key):
        return key[2] if key[0] == "e" else key[1][0]

    def _add_dep(self, op, tok, kind, force=False):
        key, val = tok
        if not force and self._tok_epoch(key) < self.epoch:
            return
        if key[0] == "e" and key[1] == op.eng:
            if op.eng == "pe":
                return
        if self.known[op.eng].get(key, -1) >= val:
            return
        if op.deps.get(key, -1) < val:
            op.deps[key] = val

    def op(self, eng, fn, reads=(), writes=(), dma=None):
        if dma is not None:
            dma = (self.epoch, dma)
        o = _Op(eng, fn, dma, self.epoch)
        for key, val in self.pending[eng].items():
            self._add_dep(o, (key, val), "raw", force=True)
        self.pending[eng] = {}
        for r in reads:
            t = self.lastw.get(r)
            if t is not None:
                self._add_dep(o, t, "raw")
            if isinstance(r, tuple) and r[0] == "ps":
                for t in self.readers.get(r, ()):
                    if t[0][0] == "e" and t[0][1] != eng:
                        self._add_dep(o, t, "raw")
        for w in writes:
            t = self.lastw.get(w)
            if t is not None:
                self._add_dep(o, t, "waw")
            for t in self.readers.get(w, ()):
                self._add_dep(o, t, "war")
        o.idx = len(self.ops[eng])
        self.ops[eng].append(o)
        for key, val in o.deps.items():
            self.known[eng][key] = val
            if key[0] == "e":
                self.ops[key[1]][val].inc = True
        if dma is not None:
            c = self.dma_cnt.get(dma, 0) + 16
            self.dma_cnt[dma] = c
            tok = (("d", dma), c)
        else:
            tok = (("e", eng, self.epoch), o.idx)
        for r in reads:
            self.readers.setdefault(r, []).append(tok)
        for w in writes:
            self.lastw[w] = tok
            self.readers[w] = []
        return o

    def barrier(self):
        toks = {}
        for e in ENGS:
            if self.ops[e]:
                last = self.ops[e][-1]
                if last.dma is None and last.fn is not None:
                    toks[("e", e, last.epoch)] = last.idx
                else:
                    for o in reversed(self.ops[e]):
                        if o.dma is None and o.fn is not None:
                            toks[("e", e, o.epoch)] = o.idx
                            break
        for k, c in self.dma_cnt.items():
            if k[0] == self.epoch:
                toks[("d", k)] = c
        for e in ENGS:
            for key, val in toks.items():
                if key[0] == "e" and key[1] == e:
                    continue
                if self.pending[e].get(key, -1) < val:
                    self.pending[e][key] = val
        self.epoch += 1

    def emit(self):
        nc = self.nc
        self.barrier()
        self.op("sp", None)
        with contextlib.ExitStack() as st:
            cum = {}
            used = set()
            for e in ENGS:
                cnt = {}
                arr = []
                for o in self.ops[e]:
                    if o.inc and o.dma is None and o.fn is not None:
                        cnt[o.epoch] = cnt.get(o.epoch, 0) + 1
                        used.add((e, o.epoch))
                    arr.append(cnt.get(o.epoch, 0))
                cum[e] = arr
            esem = {k: st.enter_context(nc.semaphore("s_%s_%d" % k)) for k in sorted(used)}
            dsem = {k: st.enter_context(nc.semaphore("d_%d" % i))
                    for i, k in enumerate(self.dma_cnt)}
            self.nsem = len(esem) + len(dsem)
            block = st.enter_context(nc.Block())

            def run(ename, engine):
                for o in self.ops[ename]:
                    for key, val in o.deps.items():
                        if key[0] == "e":
                            v = cum[key[1]][val]
                            if v > 0:
                                engine.wait_ge(esem[(key[1], key[2])], v)
                        else:
                            engine.wait_ge(dsem[key[1]], val)
                    if o.fn is None:
                        continue
                    ins = o.fn(engine)
                    if o.dma is not None:
                        ins.then_inc(dsem[o.dma], 16)
                    elif o.inc:
                        ins.then_inc(esem[(ename, o.epoch)], 1)

            @block.tensor
            def _(eng):
                run("pe", eng)

            @block.scalar
            def _(eng):
                run("act", eng)

            @block.vector
            def _(eng):
                run("dve", eng)

            @block.gpsimd
            def _(eng):
                run("pool", eng)

            @block.sync
            def _(eng):
                run("sp", eng)


def _esize(dt):
    return 4 if dt == F32 else 2


class Mem:
    def __init__(self, nc):
        self.nc = nc
        self.base = (nc.sbuf_base + 63) // 64 * 64
        self.top = nc.sbuf_top
        self.off = self.base
        self.n = 0

    def mark(self):
        return self.off

    def reset(self, m):
        self.off = m

    def alloc(self, name, shape, dt):
        sz = int(np.prod(shape[1:])) * _esize(dt)
        sz = (sz + 63) // 64 * 64
        assert self.off + sz <= self.top, ("SBUF overflow", name, self.off, sz, self.top)
        self.n += 1
        t = self.nc.alloc_sbuf_tensor_at("%s_%d" % (name, self.n), list(shape), dt, offset=self.off)
        self.off += sz
        return t


def build_nc(stop_after=None, debug=False, skip_ffn1=False):
    nc = bass.Bass("TRN2", target_bir_lowering=False)
    dk = "ExternalOutput" if debug else "Internal"

    def din(name, shape, dt=F32):
        return nc.dram_tensor(name, list(shape), dt, kind="ExternalInput").ap()

    xin = din("xin", [KC, 128, TT])
    consts_d = din("consts", [128, NCONST])
    bones_d = din("bones", [128, 128])
    pmat_d = din("pmat", [128, 128])
    cs_d = din("cossin", [2, 128, TT])
    tstd_d = din("tstd", [8, 128, 2 * 5 * 128])
    tspec_d = din("tspec", [8, 128, 2 * 22 * 128])
    w_ada = din("w_ada", [D, 9 * D])
    w1i = din("ff1_w_in", [D, 2 * DFF])
    w1o = din("ff1_w_out", [DFF, D])
    w_in = din("w_in", [D, 6144])
    w_out = din("w_out", [D, D])
    w2i = din("ff2_w_in", [D, 2 * DFF])
    w2o = din("ff2_w_out", [DFF, D])
    yout = nc.dram_tensor("yout", [KC, 128, OWN], F32, kind="ExternalOutput").ap()

    x1 = nc.dram_tensor("x1s", [KC, 128, TT], F32, kind=("ExternalInput" if skip_ffn1 else dk)).ap()
    x2 = nc.dram_tensor("x2s", [KC, 128, OWN], F32, kind=dk).ap()
    qT = nc.dram_tensor("qTs", [8, 128, OWN], BF16, kind=dk).ap()
    kT = nc.dram_tensor("kTs", [8, 128, TT], BF16, kind=dk).ap()
    vtok = nc.dram_tensor("vtoks", [22, 128, 1024], BF16, kind=dk).ap()
    aoT = nc.dram_tensor("aoTs", [KC, 128, OWN], BF16, kind=dk).ap()
    modsd = nc.dram_tensor("modsd", [128, 288], F32, kind=dk).ap()

    s = Sched(nc)
    mem = Mem(nc)

    st_ps = [nc.alloc_psum_tensor("stps%d" % i, [128, 1024], F32) for i in range(3)]
    pb_ps = [nc.alloc_psum_tensor("pbps%d" % i, [128, 512], F32) for i in range(2)]

    def bank(k):
        if k < 6:
            return st_ps[k // 2][:, (k % 2) * 512:(k % 2) * 512 + 512]
        return pb_ps[k - 6][:, :]

    def bres(k):
        return ("ps", k)

    consts = mem.alloc("consts", [128, NCONST], F32)
    ones_bf = mem.alloc("ones_bf", [128, 128], BF16)
    ones_f = mem.alloc("ones_f", [128, 128], F32)
    bones = mem.alloc("bones", [128, 128], F32)
    pmat = mem.alloc("pmat", [128, 128], F32)
    mods = mem.alloc("mods", [128, 144, 2], F32)
    sc_names = ["A1", "B1", "G1", "A2", "B2", "GM", "A3", "B3", "G3"]
    scal = {n: mem.alloc("sc" + n, [128, 16, 2], F32) for n in sc_names}
    epsb = mem.alloc("epsb", [128, 1], F32)

    s.op("sp", lambda e: e.dma_start(out=consts[:], in_=consts_d), writes=["consts"], dma="c0")
    s.op("sp", lambda e: e.dma_start(out=bones[:], in_=bones_d), writes=["bones"], dma="c1")
    s.op("sp", lambda e: e.dma_start(out=pmat[:], in_=pmat_d), writes=["pmat"], dma="c2")
    s.op("pool", lambda e: e.memset(ones_bf[:], 1.0), writes=["ones_bf"])
    s.op("pool", lambda e: e.memset(ones_f[:], 1.0), writes=["ones_f"])
    s.op("pool", lambda e: e.memset(epsb[:], EPS), writes=["epsb"])

    pmark = mem.mark()

    sc_c = mem.alloc("sc_c", [128, 16, 2], BF16)
    wslots = [mem.alloc("wslot%d" % i, [128, 16, 512], BF16) for i in range(3)]
    s.op("act", lambda e: e.activation(out=sc_c[:], in_=consts[:, C_C2:C_C2 + 32].rearrange("p (k v) -> p k v", v=2),
                                       func=AF.Silu), reads=["consts"], writes=["sc_c"])
    w_ada_r = w_ada.rearrange("(k p) c -> p k c", p=128)
    MB = 4
    for t in range(36):
        sl = t % 3
        s.op("pool", lambda e, t=t, sl=sl: e.dma_start(out=wslots[sl][:], in_=w_ada_r[:, :, t * 512:(t + 1) * 512]),
             writes=[("w", sl, 0)], dma=("w", sl, 0))
        for jj in range(4):
            j = t * 4 + jj
            for kc in range(KC):
                s.op("pe", lambda e, sl=sl, jj=jj, j=j, kc=kc: e.matmul(
                    bank(MB)[:, 2 * j:2 * j + 2], wslots[sl][:, kc, jj * 128:(jj + 1) * 128], sc_c[:, kc, :],
                    start=(kc == 0), stop=(kc == KC - 1)),
                    reads=[("w", sl, 0), "sc_c"], writes=[bres(MB)])
    mps = bank(MB)[:, 0:288].rearrange("p (j v) -> p j v", v=2)
    for v in range(2):
        s.op("dve", lambda e, v=v: e.tensor_tensor(out=mods[:, :, v], in0=mps[:, :, v], in1=consts[:, C_BADA:C_BADA + 144],
                                                  op=ALU.add), reads=[bres(MB), "consts"], writes=["mods"])

    def mk_scal(name, modi, kind, normcol):
        dst = scal[name]
        for v in range(2):
            src = mods[:, modi * 16:(modi + 1) * 16, v]
            if kind == "A":
                s.op("dve", lambda e, v=v, src=src: e.scalar_tensor_tensor(
                    out=dst[:, :, v], in0=src, scalar=1.0, in1=consts[:, normcol:normcol + 16],
                    op0=ALU.add, op1=ALU.mult), reads=["mods", "consts"], writes=["scal"])
            elif kind == "B":
                s.op("dve", lambda e, v=v, src=src: e.tensor_copy(out=dst[:, :, v], in_=src),
                     reads=["mods"], writes=["scal"])
            else:
                s.op("dve", lambda e, v=v, src=src: e.tensor_scalar(
                    out=dst[:, :, v], in0=src, scalar1=(0.5 if kind == "G" else 1.0), scalar2=None, op0=ALU.mult),
                    reads=["mods"], writes=["scal"])

    mk_scal("B1", 0, "B", 0)
    mk_scal("A1", 1, "A", C_FF1N)
    mk_scal("G1", 2, "G", 0)
    mk_scal("B2", 3, "B", 0)
    mk_scal("A2", 4, "A", C_MIXN)
    mk_scal("GM", 5, "GM", 0)
    mk_scal("B3", 6, "B", 0)
    mk_scal("A3", 7, "A", C_FF2N)
    mk_scal("G3", 8, "G", 0)
    if debug:
        s.op("sp", lambda e: e.dma_start(out=modsd, in_=mods[:].rearrange("p j v -> p (j v)")), reads=["mods"],
             writes=["modsd"], dma="dbg")

    def norm_stage(src, col0, N, Asc, Bsc, v, dst_of, xb, sq, rstd, ssb):
        nx = len(xb)
        for kc in range(KC):
            xs = kc % nx
            s.op("sp", lambda e, kc=kc, xs=xs: e.dma_start(out=xb[xs][:, 0:N], in_=src[kc, :, col0:col0 + N]),
                 writes=[("xb", xs)], dma=("xb", xs))
            q2 = kc % 2
            s.op("act", lambda e, xs=xs, q2=q2: e.activation(out=sq[q2][:, 0:N], in_=xb[xs][:, 0:N], func=AF.Square),
                 reads=[("xb", xs)], writes=[("sq", q2)])
            s.op("pe", lambda e, q2=q2, kc=kc: e.matmul(bank(ssb)[:, 0:N], ones_bf[:], sq[q2][:, 0:N],
                                                       start=(kc == 0), stop=(kc == KC - 1)),
                 reads=[("sq", q2), "ones_bf"], writes=[bres(ssb)])
        s.op("act", lambda e: e.activation(out=rstd[:, 0:N], in_=bank(ssb)[:, 0:N], func=AF.Sqrt,
                                           scale=1.0 / D, bias=epsb[:, 0:1]),
             reads=[bres(ssb), "epsb"], writes=["rstd"])
        s.op("dve", lambda e: e.reciprocal(out=rstd[:, 0:N], in_=rstd[:, 0:N]), reads=["rstd"], writes=["rstd"])
        for kc in range(KC):
            xs = kc % nx
            s.op("sp", lambda e, kc=kc, xs=xs: e.dma_start(out=xb[xs][:, 0:N], in_=src[kc, :, col0:col0 + N]),
                 writes=[("xb", xs)], dma=("xb", xs))
            s.op("dve", lambda e, xs=xs: e.tensor_tensor(out=xb[xs][:, 0:N], in0=xb[xs][:, 0:N], in1=rstd[:, 0:N],
                                                        op=ALU.mult),
                 reads=[("xb", xs), "rstd"], writes=[("xb", xs)])
            dst, dres = dst_of(kc)
            s.op("act", lambda e, xs=xs, kc=kc, dst=dst: e.activation(
                out=dst, in_=xb[xs][:, 0:N], func=AF.Identity, bias=Bsc[:, kc, v:v + 1], scale=Asc[:, kc, v:v + 1]),
                reads=[("xb", xs), "scal"], writes=[dres])

    def ffn_phase(src, dst, dst_col0, tiles, wi, wo, Asc, Bsc, Gsc, tag):
        m0 = mem.mark()
        h = mem.alloc("h", [128, KC, 1024], BF16)
        g = mem.alloc("g", [128, FC, 1024], BF16)
        wsl = [mem.alloc("wsl%d" % i, [128, 16 * 512], BF16) for i in range(3)]
        xb = [mem.alloc("xb%d" % i, [128, 512], F32) for i in range(4)]
        sq = [mem.alloc("sq%d" % i, [128, 512], BF16) for i in range(2)]
        rstd2 = [mem.alloc("rstd%d" % i, [128, 512], F32) for i in range(2)]
        sa = [mem.alloc("sa%d" % i, [128, 512], F32) for i in range(2)]
        xr = [mem.alloc("xr%d" % i, [128, 512], F32) for i in range(2)]
        xo = [mem.alloc("xo%d" % i, [128, 512], F32) for i in range(2)]
        wi_r = wi.rearrange("(k p) c -> p k c", p=128)
        wo_r = wo.rearrange("(f p) c -> p f c", p=128)
        SSB = 6
        wcnt = [0]
        ecnt = [0]

        def next_slot():
            sl = wcnt[0] % 3
            wcnt[0] += 1
            return sl

        def stat_steps(tile):
            steps = []
            for hi, (c0, N, v) in enumerate(tile):
                def stA(kc, c0=c0, N=N):
                    xs, q2 = kc % 4, kc % 2

                    def f():
                        s.op("sp", lambda e: e.dma_start(out=xb[xs][:, 0:N], in_=src[kc, :, c0:c0 + N]),
                             writes=[("xb", xs)], dma=("xb", xs))
                        s.op("act", lambda e: e.activation(out=sq[q2][:, 0:N], in_=xb[xs][:, 0:N], func=AF.Square),
                             reads=[("xb", xs)], writes=[("sq", q2)])
                    return f

                def stB(kc, N=N):
                    q2 = kc % 2

                    def f():
                        s.op("pe", lambda e: e.matmul(bank(SSB)[:, 0:N], ones_bf[:], sq[q2][:, 0:N],
                                                      start=(kc == 0), stop=(kc == KC - 1)),
                             reads=[("sq", q2), "ones_bf"], writes=[bres(SSB)])
                    return f

                def stC(hi=hi, N=N):
                    def f():
                        s.op("act", lambda e: e.activation(out=rstd2[hi][:, 0:N], in_=bank(SSB)[:, 0:N], func=AF.Sqrt,
                                                           scale=1.0 / D, bias=epsb[:, 0:1]),
                             reads=[bres(SSB), "epsb"], writes=[("rstd", hi)])
                        s.op("dve", lambda e: e.reciprocal(out=rstd2[hi][:, 0:N], in_=rstd2[hi][:, 0:N]),
                             reads=[("rstd", hi)], writes=[("rstd", hi)])
                    return f

                for kc in range(KC):
                    steps.append(stA(kc))
                    if kc >= 1:
                        steps.append(stB(kc - 1))
                steps.append(stB(KC - 1))
                steps.append(stC())
            return steps

        def apply_norm(tile, hoffs):
            for hi, (c0, N, v) in enumerate(tile):
                ho = hoffs[hi]
                for kc in range(KC):
                    xs = kc % 4
                    s.op("sp", lambda e, kc=kc, xs=xs, c0=c0, N=N: e.dma_start(out=xb[xs][:, 0:N], in_=src[kc, :, c0:c0 + N]),
                         writes=[("xb", xs)], dma=("xb", xs))
                    s.op("dve", lambda e, xs=xs, hi=hi, N=N: e.tensor_tensor(out=xb[xs][:, 0:N], in0=xb[xs][:, 0:N],
                                                                            in1=rstd2[hi][:, 0:N], op=ALU.mult),
                         reads=[("xb", xs), ("rstd", hi)], writes=[("xb", xs)])
                    s.op("act", lambda e, xs=xs, kc=kc, ho=ho, N=N, v=v: e.activation(
                        out=h[:, kc, ho:ho + N], in_=xb[xs][:, 0:N], func=AF.Identity,
                        bias=Bsc[:, kc, v:v + 1], scale=Asc[:, kc, v:v + 1]),
                        reads=[("xb", xs), "scal"], writes=[("h", kc)])

        for f_ in stat_steps(tiles[0]):
            f_()
        for ti, tile in enumerate(tiles):
            hoff = 0
            hoffs = []
            for (c0, N, v) in tile:
                hoffs.append(hoff)
                hoff += N
            apply_norm(tile, hoffs)
            nxt_steps = stat_steps(tiles[ti + 1]) if ti + 1 < len(tiles) else []
            per_grp = 3
            for jg in range(FC // 2):
                sl = next_slot()
                wv = wsl[sl][:].rearrange("p (k c) -> p k c", c=512)
                s.op("pool", lambda e, jg=jg, wv=wv: e.dma_start(out=wv[:, :, 0:256],
                                                               in_=wi_r[:, :, jg * 256:(jg + 1) * 256]),
                     writes=[("w", sl, 0)], dma=("w", sl, 0))
                s.op("pool", lambda e, jg=jg, wv=wv: e.dma_start(out=wv[:, :, 256:512],
                                                               in_=wi_r[:, :, DFF + jg * 256:DFF + (jg + 1) * 256]),
                     writes=[("w", sl, 1)], dma=("w", sl, 1))
                for jj in range(2):
                    j = jg * 2 + jj
                    for hi, (c0, N, v) in enumerate(tile):
                        ho = hoffs[hi]
                        e2 = ecnt[0] % 2
                        ecnt[0] += 1
                        pa, pb = e2 * 2, e2 * 2 + 1
                        for kc in range(KC):
                            s.op("pe", lambda e, wv=wv, jj=jj, kc=kc, ho=ho, N=N, pa=pa: e.matmul(
                                bank(pa)[:, 0:N], wv[:, kc, jj * 128:(jj + 1) * 128], h[:, kc, ho:ho + N],
                                start=(kc == 0), stop=(kc == KC - 1)),
                                reads=[("w", sl, 0), ("h", kc)], writes=[bres(pa)])
                        for kc in range(KC):
                            s.op("pe", lambda e, wv=wv, jj=jj, kc=kc, ho=ho, N=N, pb=pb: e.matmul(
                                bank(pb)[:, 0:N], wv[:, kc, 256 + jj * 128:256 + (jj + 1) * 128], h[:, kc, ho:ho + N],
                                start=(kc == 0), stop=(kc == KC - 1)),
                                reads=[("w", sl, 1), ("h", kc)], writes=[bres(pb)])
                        s.op("act", lambda e, e2=e2, pa=pa, N=N: e.activation(out=sa[e2][:, 0:N], in_=bank(pa)[:, 0:N],
                                                                             func=AF.Silu),
                             reads=[bres(pa)], writes=[("sa", e2)])
                        s.op("dve", lambda e, e2=e2, pb=pb, N=N, j=j, ho=ho: e.tensor_tensor(
                            out=g[:, j, ho:ho + N], in0=bank(pb)[:, 0:N], in1=sa[e2][:, 0:N], op=ALU.mult),
                            reads=[bres(pb), ("sa", e2)], writes=[("g", j)])
            for m in range(KC):
                sl = next_slot()
                wv = wsl[sl][:, 0:FC * 128].rearrange("p (f c) -> p f c", c=128)
                s.op("pool", lambda e, m=m, wv=wv: e.dma_start(out=wv, in_=wo_r[:, :, m * 128:(m + 1) * 128]),
                     writes=[("w", sl, 0), ("w", sl, 1)], dma=("w", sl, 0))
                for hi, (c0, N, v) in enumerate(tile):
                    ho = hoffs[hi]
                    e2 = ecnt[0] % 2
                    ecnt[0] += 1
                    po = 4 + e2
                    for f in range(FC):
                        s.op("pe", lambda e, wv=wv, f=f, ho=ho, N=N, po=po: e.matmul(
                            bank(po)[:, 0:N], wv[:, f, :], g[:, f, ho:ho + N],
                            start=(f == 0), stop=(f == FC - 1)),
                            reads=[("w", sl, 0), ("w", sl, 1), ("g", f)], writes=[bres(po)])
                    s.op("sp", lambda e, e2=e2, m=m, c0=c0, N=N: e.dma_start(out=xr[e2][:, 0:N], in_=src[m, :, c0:c0 + N]),
                         writes=[("xr", e2)], dma=("xr", e2))
                    s.op("dve", lambda e, e2=e2, po=po, N=N, m=m, v=v: e.scalar_tensor_tensor(
                        out=xo[e2][:, 0:N], in0=bank(po)[:, 0:N], scalar=Gsc[:, m, v:v + 1], in1=xr[e2][:, 0:N],
                        op0=ALU.mult, op1=ALU.add),
                        reads=[bres(po), ("xr", e2), "scal"], writes=[("xo", e2)])
                    dc = c0 - dst_col0
                    s.op("sp", lambda e, e2=e2, m=m, dc=dc, N=N: e.dma_start(out=dst[m, :, dc:dc + N], in_=xo[e2][:, 0:N]),
                         reads=[("xo", e2)], writes=[(tag, m)], dma=("xo", e2))
                    for _ in range(per_grp):
                        if nxt_steps:
                            nxt_steps.pop(0)()
            while nxt_steps:
                nxt_steps.pop(0)()
        mem.reset(m0)

    s.barrier()
    mem.reset(pmark)
    tiles1 = [[(0, 512, 0), (512, 512, 0)], [(1024, 512, 0), (1536, 512, 0)], [(2048, 512, 0), (2560, 256, 1)]]
    if stop_after != "ada" and not skip_ffn1:
        ffn_phase(xin, x1, 0, tiles1, w1i, w1o, scal["A1"], scal["B1"], scal["G1"], "x1")

    if stop_after in ("ada", "ffn1"):
        s.emit()
        return nc

    s.barrier()
    mem.reset(pmark)
    sqacc_a = mem.alloc("sqacc_a", [128, OWN], F32)
    sqacc_c = mem.alloc("sqacc_c", [128, OWN], F32)
    s.op("pool", lambda e: e.memset(sqacc_a[:], 0.0), writes=["sqacc_a"])
    s.op("pool", lambda e: e.memset(sqacc_c[:], 0.0), writes=["sqacc_c"])
    pmark2 = mem.mark()
    hm = mem.alloc("hm", [128, KC, TT], BF16)
    xb = [mem.alloc("xb%d" % i, [128, 512], F32) for i in range(4)]
    sq = [mem.alloc("sq%d" % i, [128, 512], BF16) for i in range(2)]
    rstd = mem.alloc("rstd", [128, 512], F32)
    wbig = mem.alloc("wbig", [128, KC, 512], BF16)
    cosT = mem.alloc("cosT", [128, TT], F32)
    sinT = mem.alloc("sinT", [128, TT], F32)
    s.op("sp", lambda e: e.dma_start(out=cosT[:], in_=cs_d[0]), writes=["cosT"], dma="cos")
    s.op("sp", lambda e: e.dma_start(out=sinT[:], in_=cs_d[1]), writes=["sinT"], dma="sin")
    wtiles = [(i * 512, 512, 0) for i in range(5)] + [(WT, CTX, 1)]
    for (c0, N, v) in wtiles:
        norm_stage(x1, c0, N, scal["A2"], scal["B2"], v,
                   lambda kc, c0=c0, N=N: (hm[:, kc, c0:c0 + N], ("hm", kc)), xb, sq, rstd, 6)
    if stop_after == "projN":
        s.emit()
        return nc
    w_in_r = w_in.rearrange("(k p) c -> p k c", p=128)

    vst = [mem.alloc("vst%d" % i, [128, 512], BF16) for i in range(2)]
    cnt = 0
    for hf in range(2):
        s.op("pool", lambda e, hf=hf: e.dma_start(out=wbig[:], in_=w_in_r[:, :, 2048 + hf * 512:2560 + hf * 512]),
             writes=["wbig0"], dma="wbig0")
        for tg in range(22):
            e2 = cnt % 2
            cnt += 1
            pbk = 4 + e2
            for kc in range(KC):
                s.op("pe", lambda e, tg=tg, hf=hf, kc=kc, pbk=pbk: e.matmul(
                    bank(pbk)[:, :], hm[:, kc, tg * 128:(tg + 1) * 128], wbig[:, kc, :],
                    start=(kc == 0), stop=(kc == KC - 1)),
                    reads=["wbig0", ("hm", kc)], writes=[bres(pbk)])
            s.op("act", lambda e, e2=e2, pbk=pbk: e.activation(out=vst[e2][:], in_=bank(pbk)[:, :], func=AF.Copy),
                 reads=[bres(pbk)], writes=[("vst", e2)])
            s.op("sp", lambda e, e2=e2, tg=tg, hf=hf: e.dma_start(out=vtok[tg, :, hf * 512:(hf + 1) * 512], in_=vst[e2][:]),
                 reads=[("vst", e2)], writes=["vtok"], dma=("vst", e2))

    if stop_after == "projV":
        s.emit()
        return nc
    NSL = 4
    wsm = [wbig[:, :, i * 128:(i + 1) * 128] for i in range(NSL)]
    wsm_cnt = [0]

    def load_w(colbase):
        sl = wsm_cnt[0] % NSL
        wsm_cnt[0] += 1
        s.op("pool", lambda e, sl=sl, colbase=colbase: e.dma_start(out=wsm[sl], in_=w_in_r[:, :, colbase:colbase + 128]),
             reads=[], writes=[("wsm", sl), "wbig0"] if wsm_cnt[0] <= NSL else [("wsm", sl)],
             dma=("wsm", sl))
        return sl

    sqf = [xb[0], xb[1]]
    raw = [xb[2], xb[3]]
    sd = [mem.alloc("sd%d" % i, [128, 512], F32) for i in range(2)]
    qn = [mem.alloc("qn%d" % i, [128, 512], F32) for i in range(2)]
    t1 = [mem.alloc("t1%d" % i, [128, 512], F32) for i in range(2)]
    t2 = [mem.alloc("t2%d" % i, [128, 512], F32) for i in range(2)]
    qr = [mem.alloc("qr%d" % i, [128, 512], BF16) for i in range(2)]
    pcnt = [0]

    def qk_stages(sl, hcol0, N, gcol, cscol0, dst_ap, dres):
        i2 = pcnt[0] % 2
        pcnt[0] += 1
        praw, pss, prot = 4 + i2, 0 + i2, 2 + i2

        def stA():
            for kc in range(KC):
                s.op("pe", lambda e, kc=kc: e.matmul(bank(praw)[:, 0:N], wsm[sl][:, kc, :], hm[:, kc, hcol0:hcol0 + N],
                                                     start=(kc == 0), stop=(kc == KC - 1)),
                     reads=[("wsm", sl), ("hm", kc)], writes=[bres(praw)])
            s.op("act", lambda e: e.activation(out=sqf[i2][:, 0:N], in_=bank(praw)[:, 0:N], func=AF.Square),
                 reads=[bres(praw)], writes=[("xb", i2)])
            s.op("dve", lambda e: e.tensor_copy(out=raw[i2][:, 0:N], in_=bank(praw)[:, 0:N]),
                 reads=[bres(praw)], writes=[("xb", 2 + i2)])

        def stB():
            s.op("pe", lambda e: e.matmul(bank(pss)[:, 0:N], bones[:], sqf[i2][:, 0:N], start=True, stop=True),
                 reads=["bones", ("xb", i2)], writes=[bres(pss)])
            s.op("act", lambda e: e.activation(out=sd[i2][:, 0:N], in_=bank(pss)[:, 0:N], func=AF.Sqrt,
                                               scale=1.0 / 64.0, bias=epsb[:, 0:1]),
                 reads=[bres(pss), "epsb"], writes=[("sd", i2)])
            s.op("dve", lambda e: e.reciprocal(out=sd[i2][:, 0:N], in_=sd[i2][:, 0:N]),
                 reads=[("sd", i2)], writes=[("sd", i2)])
            s.op("dve", lambda e: e.scalar_tensor_tensor(out=qn[i2][:, 0:N], in0=raw[i2][:, 0:N],
                                                         scalar=consts[:, gcol:gcol + 1], in1=sd[i2][:, 0:N],
                                                         op0=ALU.mult, op1=ALU.mult),
                 reads=[("xb", 2 + i2), ("sd", i2), "consts"], writes=[("qn", i2)])

        def stC():
            s.op("pe", lambda e: e.matmul(bank(prot)[:, 0:N], pmat[:], qn[i2][:, 0:N], start=True, stop=True),
                 reads=["pmat", ("qn", i2)], writes=[bres(prot)])
            s.op("pool", lambda e: e.tensor_tensor(out=t1[i2][:, 0:N], in0=qn[i2][:, 0:N], in1=cosT[:, cscol0:cscol0 + N],
                                                   op=ALU.mult),
                 reads=[("qn", i2), "cosT"], writes=[("t1", i2)])
            s.op("dve", lambda e: e.tensor_tensor(out=t2[i2][:, 0:N], in0=bank(prot)[:, 0:N], in1=sinT[:, cscol0:cscol0 + N],
                                                  op=ALU.mult),
                 reads=[bres(prot), "sinT"], writes=[("t2", i2)])
            s.op("pool", lambda e: e.tensor_tensor(out=qr[i2][:, 0:N], in0=t1[i2][:, 0:N], in1=t2[i2][:, 0:N], op=ALU.add),
                 reads=[("t1", i2), ("t2", i2)], writes=[("qr", i2)])
            s.op("sp", lambda e: e.dma_start(out=dst_ap, in_=qr[i2][:, 0:N]), reads=[("qr", i2)], writes=[dres],
                 dma=("qr", i2))

        return stA, stB, stC

    chunks = []
    for c in range(8):
        chunks.append((1024 + c * 128,
                       [(c0, N, C_KG, c0, kT[c, :, c0:c0 + N], ("kT", c)) for (c0, N, v) in wtiles]))
        chunks.append((c * 128,
                       [(OWN0 + i * 512, 512, C_QG, OWN0 + i * 512, qT[c, :, i * 512:(i + 1) * 512], ("qT", c))
                        for i in range(4)]))
    slots = {0: load_w(chunks[0][0])}
    pend = []
    for k, (colbase, tl) in enumerate(chunks):
        if k + 1 < len(chunks):
            slots[k + 1] = load_w(chunks[k + 1][0])
        for targs in tl:
            stA, stB, stC = qk_stages(slots[k], *targs)
            stA()
            pend.append((stB, stC))
            if len(pend) >= 2:
                pend[-2][0]()
            if len(pend) >= 3:
                pend[-3][1]()
    pend[-1][0]()
    pend[-2][1]()
    pend[-1][1]()

    if stop_after == "projQK":
        s.emit()
        return nc
    zbuf = mem.alloc("zbuf", [128, OWN + 2], F32)
    acc = mem.alloc("acc", [128, OWN], F32)
    cS = [mem.alloc("cS%d" % i, [128, 512], F32) for i in range(2)]
    zh = mem.alloc("zh", [128, 2], F32)
    yv = sqf
    ysq = raw
    ybf = qr
    ccnt = 0
    for cc in range(8):
        slB = load_w(3072 + cc * 128)
        slC = load_w(4096 + cc * 128)
        slU = load_w(5120 + cc * 128)
        for i in range(4):
            c0 = OWN0 + i * 512
            i2 = ccnt % 2
            ccnt += 1
            pC, pU = 0 + i2, 2 + i2
            for kc in range(KC):
                s.op("pe", lambda e, kc=kc, c0=c0, pC=pC, slC=slC: e.matmul(bank(pC)[:, :], wsm[slC][:, kc, :], hm[:, kc, c0:c0 + 512],
                                                                  start=(kc == 0), stop=(kc == KC - 1)),
                     reads=[("wsm", slC), ("hm", kc)], writes=[bres(pC)])
            for kc in range(KC):
                s.op("pe", lambda e, kc=kc, c0=c0, pU=pU, slU=slU: e.matmul(bank(pU)[:, :], wsm[slU][:, kc, :], hm[:, kc, c0:c0 + 512],
                                                                  start=(kc == 0), stop=(kc == KC - 1)),
                     reads=[("wsm", slU), ("hm", kc)], writes=[bres(pU)])
            s.op("act", lambda e, i2=i2, pC=pC: e.activation(out=cS[i2][:], in_=bank(pC)[:, :], func=AF.Copy),
                 reads=[bres(pC)], writes=[("cS", i2)])
            s.op("dve", lambda e, i2=i2, pU=pU, i=i: e.tensor_tensor(out=zbuf[:, 1 + i * 512:1 + (i + 1) * 512],
                                                                    in0=bank(pU)[:, :], in1=cS[i2][:], op=ALU.mult),
                 reads=[bres(pU), ("cS", i2)], writes=["zbuf"])
        pH = 6
        for hi, col in enumerate((OWN0 - 1, OWN0 + OWN)):
            for wi_, sl_ in enumerate((slC, slU)):
                for kc in range(KC):
                    s.op("pe", lambda e, kc=kc, col=col, hi=hi, wi_=wi_, sl_=sl_: e.matmul(
                        bank(pH)[:, wi_ * 2 + hi:wi_ * 2 + hi + 1], wsm[sl_][:, kc, :], hm[:, kc, col:col + 1],
                        start=(kc == 0), stop=(kc == KC - 1)),
                        reads=[("wsm", sl_), ("hm", kc)], writes=[bres(pH)])
        s.op("act", lambda e: e.activation(out=zh[:], in_=bank(pH)[:, 0:2], func=AF.Copy), reads=[bres(pH)], writes=["zh"])
        s.op("dve", lambda e: e.tensor_tensor(out=zh[:], in0=bank(pH)[:, 2:4], in1=zh[:], op=ALU.mult),
             reads=[bres(pH), "zh"], writes=["zh"])
        s.op("dve", lambda e: e.tensor_tensor(out=zbuf[:, 0:OWN + 2:OWN + 1], in0=zh[:], in1=consts[:, C_HALO:C_HALO + 2],
                                              op=ALU.mult),
             reads=["zh", "consts"], writes=["zbuf"])
        cw = C_CW + cc * 3
        s.op("dve", lambda e, cw=cw, cc=cc: e.tensor_scalar(out=acc[:], in0=zbuf[:, 1:OWN + 1], scalar1=consts[:, cw + 1:cw + 2],
                                                           scalar2=consts[:, C_CB + cc:C_CB + cc + 1], op0=ALU.mult, op1=ALU.add),
             reads=["zbuf", "consts"], writes=["acc"])
        s.op("dve", lambda e, cw=cw: e.scalar_tensor_tensor(out=acc[:], in0=zbuf[:, 0:OWN], scalar=consts[:, cw:cw + 1],
                                                            in1=acc[:], op0=ALU.mult, op1=ALU.add),
             reads=["zbuf", "consts", "acc"], writes=["acc"])
        s.op("dve", lambda e, cw=cw: e.scalar_tensor_tensor(out=acc[:], in0=zbuf[:, 2:OWN + 2], scalar=consts[:, cw + 2:cw + 3],
                                                           in1=acc[:], op0=ALU.mult, op1=ALU.add),
             reads=["zbuf", "consts", "acc"], writes=["acc"])
        for i in range(4):
            c0 = OWN0 + i * 512
            i2 = ccnt % 2
            ccnt += 1
            pB = 4 + i2
            for kc in range(KC):
                s.op("pe", lambda e, kc=kc, c0=c0, pB=pB, slB=slB: e.matmul(bank(pB)[:, :], wsm[slB][:, kc, :], hm[:, kc, c0:c0 + 512],
                                                                  start=(kc == 0), stop=(kc == KC - 1)),
                     reads=[("wsm", slB), ("hm", kc)], writes=[bres(pB)])
            s.op("dve", lambda e, i2=i2, pB=pB, i=i: e.tensor_tensor(out=yv[i2][:], in0=bank(pB)[:, :],
                                                                    in1=acc[:, i * 512:(i + 1) * 512], op=ALU.mult),
                 reads=[bres(pB), "acc"], writes=[("xb", i2)])
            s.op("act", lambda e, i2=i2: e.activation(out=ysq[i2][:], in_=yv[i2][:], func=AF.Square),
                 reads=[("xb", i2)], writes=[("xb", 2 + i2)])
            s.op("pool", lambda e, i2=i2, i=i: e.tensor_tensor(out=sqacc_c[:, i * 512:(i + 1) * 512],
                                                              in0=sqacc_c[:, i * 512:(i + 1) * 512], in1=ysq[i2][:], op=ALU.add),
                 reads=[("xb", 2 + i2), "sqacc_c"], writes=["sqacc_c"])
            s.op("act", lambda e, i2=i2, cc=cc: e.activation(out=ybf[i2][:], in_=yv[i2][:], func=AF.Identity,
                                                            scale=consts[:, C_ONC + cc:C_ONC + cc + 1]),
                 reads=[("xb", i2), "consts"], writes=[("qr", i2)])
            s.op("sp", lambda e, i2=i2, cc=cc, i=i: e.dma_start(out=aoT[8 + cc, :, i * 512:(i + 1) * 512], in_=ybf[i2][:]),
                 reads=[("qr", i2)], writes=[("aoT", 8 + cc)], dma=("qr", i2))

    if stop_after == "proj":
        s.emit()
        return nc

    s.barrier()
    mem.reset(pmark2)
    KTs = [mem.alloc("KT%d" % i, [128, TT], BF16) for i in range(2)]
    QTs = [mem.alloc("QT%d" % i, [128, OWN], BF16) for i in range(2)]
    Vs = [mem.alloc("V%d" % i, [128, 22, 128], BF16) for i in range(2)]
    Bstd = [mem.alloc("Bstd%d" % i, [128, 2 * 5 * 128], F32) for i in range(2)]
    Bspec = [mem.alloc("Bspec%d" % i, [128, 2 * 22 * 128], F32) for i in range(2)]
    aob = [mem.alloc("aob%d" % i, [128, OWN], BF16) for i in range(2)]
    tmp = [mem.alloc("tmp%d" % i, [128, 768], F32) for i in range(3)]
    pt = [mem.alloc("pt%d" % i, [128, 1024], BF16) for i in range(3)]
    rden = [mem.alloc("rden%d" % i, [128, 128], F32) for i in range(2)]
    at = [mem.alloc("at%d" % i, [128, 128], F32) for i in range(2)]
    asq = [mem.alloc("asq%d" % i, [128, 128], F32) for i in range(2)]
    vtok_r = vtok.rearrange("g p f -> p g f")

    def attn_loads(c):
        c2 = c % 2
        s.op("sp", lambda e: e.dma_start(out=KTs[c2][:], in_=kT[c]), reads=[("kT", c)], writes=[("KT", c2)],
             dma=("KT", c2))
        s.op("sp", lambda e: e.dma_start(out=QTs[c2][:], in_=qT[c]), reads=[("qT", c)], writes=[("QT", c2)],
             dma=("QT", c2))
        s.op("sp", lambda e: e.dma_start(out=Vs[c2][:], in_=vtok_r[:, :, c * 128:(c + 1) * 128]),
             reads=["vtok"], writes=[("V", c2)], dma=("V", c2))
        s.op("sp", lambda e: e.dma_start(out=Bstd[c2][:], in_=tstd_d[c]), writes=[("Bstd", c2)], dma=("Bstd", c2))
        s.op("sp", lambda e: e.dma_start(out=Bspec[c2][:], in_=tspec_d[c]), writes=[("Bspec", c2)],
             dma=("Bspec", c2))

    def head_block(idx, c, j, hh):
        c2 = c % 2
        if j == 0:
            pl, tb, so = list(range(0, 6)), Bspec, 0
        elif j == 1:
            pl, tb, so = list(range(1, 6)), Bspec, 6
        elif j == 14:
            pl, tb, so = list(range(14, 19)), Bspec, 11
        elif j == 15:
            pl, tb, so = list(range(14, 20)), Bspec, 16
        else:
            pl, tb, so = list(range(j, j + 5)), Bstd, 0
        nsl = 22 if tb is Bspec else 5
        tres = ("Bspec", c2) if tb is Bspec else ("Bstd", c2)
        L = len(pl)
        b2 = (c * 16 + j) % 2
        pnd = 6 + b2
        h2 = idx % 3
        hp0, hp1 = hh * 64, hh * 64 + 64
        stt = st_ps[h2]
        sres = [bres(2 * h2), bres(2 * h2 + 1)]
        plx = pl + [20, 21]

        def st_S():
            for si, p in enumerate(plx):
                kcol = p * 128
                s.op("pe", lambda e, si=si, kcol=kcol: e.matmul(
                    stt[:, si * 128:(si + 1) * 128], KTs[c2][hp0:hp1, kcol:kcol + 128],
                    QTs[c2][hp0:hp1, j * 128:(j + 1) * 128], start=True, stop=True),
                    reads=[("KT", c2), ("QT", c2)], writes=[sres[si // 4]])

        def st_R():
            tbv = tb[c2][:, (hh * nsl + so) * 128:(hh * nsl + so + L) * 128]
            s.op("act", lambda e: e.activation(out=pt[h2][:, L * 128:(L + 2) * 128],
                                               in_=stt[:, L * 128:(L + 2) * 128], func=AF.Exp, scale=0.125),
                 reads=sres[1:], writes=[("ptc", h2)])
            s.op("dve", lambda e: e.scalar_tensor_tensor(
                out=tmp[h2][:, 0:L * 128], in0=stt[:, 0:L * 128], scalar=0.125, in1=tbv, op0=ALU.mult, op1=ALU.add),
                reads=sres + [tres], writes=[("tmp", h2)])
            s.op("act", lambda e: e.activation(out=pt[h2][:, 0:L * 128], in_=tmp[h2][:, 0:L * 128], func=AF.Exp),
                 reads=[("tmp", h2)], writes=[("pt", h2)])
            for si, p in enumerate(plx):
                first, last = (si == 0), (si == L + 1)
                s.op("pe", lambda e, si=si, p=p, first=first, last=last: e.matmul(
                    bank(pnd)[hp0:hp1, 0:128], Vs[c2][:, p, hh * 64:(hh + 1) * 64], pt[h2][:, si * 128:(si + 1) * 128],
                    start=first, stop=last, skip_group_check=True),
                    reads=[("V", c2), ("pt", h2), ("ptc", h2)], writes=[bres(pnd)])
                s.op("pe", lambda e, si=si, first=first, last=last: e.matmul(
                    bank(pnd)[hp0:hp1, 128:256], ones_bf[:, 0:64], pt[h2][:, si * 128:(si + 1) * 128],
                    start=False, stop=last, skip_group_check=True),
                    reads=["ones_bf", ("pt", h2), ("ptc", h2)], writes=[bres(pnd)])

        def st_E():
            if hh == 1:
                s.op("dve", lambda e: e.reciprocal(out=rden[b2][:], in_=bank(pnd)[:, 128:256]),
                     reads=[bres(pnd)], writes=[("rden", b2)])
                s.op("dve", lambda e: e.tensor_tensor(out=at[b2][:], in0=bank(pnd)[:, 0:128], in1=rden[b2][:],
                                                      op=ALU.mult),
                     reads=[bres(pnd), ("rden", b2)], writes=[("at", b2)])
                s.op("act", lambda e: e.activation(out=asq[b2][:], in_=at[b2][:], func=AF.Square),
                     reads=[("at", b2)], writes=[("asq", b2)])
                s.op("pool", lambda e: e.tensor_tensor(out=sqacc_a[:, j * 128:(j + 1) * 128],
                                                       in0=sqacc_a[:, j * 128:(j + 1) * 128], in1=asq[b2][:], op=ALU.add),
                     reads=[("asq", b2), "sqacc_a"], writes=["sqacc_a"])
                s.op("act", lambda e: e.activation(out=aob[c2][:, j * 128:(j + 1) * 128], in_=at[b2][:],
                                                   func=AF.Identity, scale=consts[:, C_ONA + c:C_ONA + c + 1]),
                     reads=[("at", b2), "consts"], writes=[("aob", c2)])
                if j == 15:
                    s.op("sp", lambda e: e.dma_start(out=aoT[c], in_=aob[c2][:]), reads=[("aob", c2)],
                         writes=[("aoT", c)], dma=("aob", c2))

        return st_S, st_R, st_E

    hbs = [(c, j, hh) for c in range(8) for j in range(16) for hh in range(2)]
    attn_loads(0)
    attn_loads(1)
    stg = {}
    for i in range(min(2, len(hbs))):
        stg[i] = head_block(i, *hbs[i])
        stg[i][0]()
    pend_e = None
    for i in range(len(hbs)):
        if i + 2 < len(hbs):
            stg[i + 2] = head_block(i + 2, *hbs[i + 2])
            stg[i + 2][0]()
        stg[i][1]()
        if pend_e is not None:
            pend_e()
            pend_e = None
        c, j, hh = hbs[i]
        if hh == 1:
            pend_e = stg[i][2]
        if j == 15 and hh == 1 and c + 2 < 8:
            attn_loads(c + 2)
        del stg[i]
    if pend_e is not None:
        pend_e()

    if stop_after == "attn":
        s.emit()
        return nc

    s.barrier()
    mem.reset(pmark2)
    wo = mem.alloc("wo", [128, KC, D], BF16)
    w_out_r = w_out.rearrange("(k p) c -> p k c", p=128)
    for i in range(4):
        s.op("pool", lambda e, i=i: e.dma_start(out=wo[:, i * 4:(i + 1) * 4, :], in_=w_out_r[:, i * 4:(i + 1) * 4, :]),
             writes=[("wo", i)], dma=("wo", i))
    rs = [mem.alloc("rs%d" % i, [128, OWN], F32) for i in range(2)]
    for bi, sqa in enumerate((sqacc_a, sqacc_c)):
        rname = "sqacc_a" if bi == 0 else "sqacc_c"
        for i in range(4):
            pbk = i % 2
            s.op("pe", lambda e, i=i, sqa=sqa, pbk=pbk: e.matmul(bank(pbk)[:, :], ones_f[:], sqa[:, i * 512:(i + 1) * 512],
                                                                start=True, stop=True),
                 reads=["ones_f", rname], writes=[bres(pbk)])
            s.op("act", lambda e, i=i, bi=bi, pbk=pbk: e.activation(out=rs[bi][:, i * 512:(i + 1) * 512], in_=bank(pbk)[:, :],
                                                                   func=AF.Sqrt, scale=1.0 / 1024.0, bias=epsb[:, 0:1]),
                 reads=[bres(pbk), "epsb"], writes=[("rs", bi)])
        s.op("dve", lambda e, bi=bi: e.reciprocal(out=rs[bi][:], in_=rs[bi][:]), reads=[("rs", bi)], writes=[("rs", bi)])
    aot = [mem.alloc("aot%d" % i, [128, KC, 512], BF16) for i in range(2)]
    x1t = [mem.alloc("x1t%d" % i, [128, 512], F32) for i in range(2)]
    u1 = [mem.alloc("u1%d" % i, [128, 512], F32) for i in range(2)]
    u2 = [mem.alloc("u2%d" % i, [128, 512], F32) for i in range(2)]
    u3 = [mem.alloc("u3%d" % i, [128, 512], F32) for i in range(2)]
    aoT_r = aoT.rearrange("k p t -> p k t")
    mcnt = 0
    def load_aot(n):
        n2 = n % 2
        s.op("sp", lambda e: e.dma_start(out=aot[n2][:], in_=aoT_r[:, :, n * 512:(n + 1) * 512]),
             reads=[("aoT", k) for k in range(KC)], writes=[("aot", n2)], dma=("aot", n2))

    def load_x1t(n, m, i2):
        s.op("sp", lambda e: e.dma_start(out=x1t[i2][:], in_=x1[m, :, OWN0 + n * 512:OWN0 + (n + 1) * 512]),
             reads=[("x1", m)], writes=[("x1t", i2)], dma=("x1t", i2))

    load_aot(0)
    for n in range(4):
        n2 = n % 2
        if n + 1 < 4:
            load_aot(n + 1)
        for m in range(KC):
            i2 = mcnt % 2
            mcnt += 1
            pA, pC = 0 + i2, 2 + i2
            for kc in range(8):
                s.op("pe", lambda e, kc=kc, m=m, n2=n2, pA=pA: e.matmul(bank(pA)[:, :], wo[:, kc, m * 128:(m + 1) * 128],
                                                                        aot[n2][:, kc, :], start=(kc == 0), stop=(kc == 7)),
                     reads=[("wo", kc // 4), ("aot", n2)], writes=[bres(pA)])
            for kc in range(8, 16):
                s.op("pe", lambda e, kc=kc, m=m, n2=n2, pC=pC: e.matmul(bank(pC)[:, :], wo[:, kc, m * 128:(m + 1) * 128],
                                                                        aot[n2][:, kc, :], start=(kc == 8), stop=(kc == 15)),
                     reads=[("wo", kc // 4), ("aot", n2)], writes=[bres(pC)])
            if mcnt == 1:
                load_x1t(n, m, i2)
            nm = n * KC + m + 1
            if nm < 4 * KC:
                load_x1t(nm // KC, nm % KC, (i2 + 1) % 2)
            s.op("dve", lambda e, i2=i2, pA=pA, n=n: e.tensor_tensor(out=u1[i2][:], in0=bank(pA)[:, :],
                                                                    in1=rs[0][:, n * 512:(n + 1) * 512], op=ALU.mult),
                 reads=[bres(pA), ("rs", 0)], writes=[("u1", i2)])
            s.op("dve", lambda e, i2=i2, pC=pC, n=n: e.tensor_tensor(out=u2[i2][:], in0=bank(pC)[:, :],
                                                                    in1=rs[1][:, n * 512:(n + 1) * 512], op=ALU.mult),
                 reads=[bres(pC), ("rs", 1)], writes=[("u2", i2)])
            s.op("pool", lambda e, i2=i2: e.tensor_tensor(out=u3[i2][:], in0=u1[i2][:], in1=u2[i2][:], op=ALU.add),
                 reads=[("u1", i2), ("u2", i2)], writes=[("u3", i2)])
            s.op("dve", lambda e, i2=i2, m=m: e.scalar_tensor_tensor(out=u3[i2][:], in0=u3[i2][:], scalar=scal["GM"][:, m, 0:1],
                                                                     in1=x1t[i2][:], op0=ALU.mult, op1=ALU.add),
                 reads=[("u3", i2), ("x1t", i2), "scal"], writes=[("u3", i2)])
            s.op("sp", lambda e, i2=i2, m=m, n=n: e.dma_start(out=x2[m, :, n * 512:(n + 1) * 512], in_=u3[i2][:]),
                 reads=[("u3", i2)], writes=[("x2", m)], dma=("u3", i2))

    if stop_after == "mix":
        s.emit()
        return nc

    s.barrier()
    mem.reset(pmark)
    tiles2 = [[(0, 512, 0), (512, 512, 0)], [(1024, 512, 0), (1536, 512, 0)]]
    ffn_phase(x2, yout, 0, tiles2, w2i, w2o, scal["A3"], scal["B3"], scal["G3"], "yout")
    s.emit()
    return nc


def _fm(vec, nchunk):
    return np.ascontiguousarray(np.asarray(vec, np.float32).reshape(nchunk, 128).T)


def _bias_table(rpb, core, j, pairs):
    H = rpb.shape[0]
    out = np.full((H, len(pairs), 128, 128), NEG, np.float32)
    rows = 256
    qrows = [32 * core + 2 * j, 32 * core + 2 * j + 1]
    cols = np.arange(64)
    cstart = np.clip(cols - 8, 0, 64 - 16)
    for si, p in enumerate(pairs):
        for ki in range(2):
            kr = 32 * core - 4 + 2 * p + ki
            if kr < 0 or kr >= rows:
                continue
            for qi, qr in enumerate(qrows):
                rs = min(max(qr - 4, 0), rows - 8)
                if not (rs <= kr < rs + 8):
                    continue
                dr = kr - qr + 7
                kcg, qcg = np.meshgrid(cols, cols, indexing="ij")
                valid = (kcg >= cstart[qcg]) & (kcg < cstart[qcg] + 16)
                dc = kcg - qcg + 15
                dcc = np.clip(dc, 0, 30)
                blk = np.where(valid[None], rpb[:, dr][:, dcc], np.float32(NEG))
                out[:, si, ki * 64:(ki + 1) * 64, qi * 64:(qi + 1) * 64] = blk
    return out


def _prep_core(core, inp, shared):
    x = inp["x"][0]
    r0 = 32 * core - 4
    win = np.zeros((WROWS * GW, D), np.float32)
    lo, hi = max(r0, 0), min(r0 + WROWS, 256)
    win[(lo - r0) * GW:(hi - r0) * GW] = x[lo * GW:hi * GW]
    xin = np.concatenate([win, inp["ctx"][0]], axis=0)
    xin = np.ascontiguousarray(xin.T.reshape(KC, 128, TT))
    consts = shared["consts"].copy()
    consts[:, C_HALO] = 0.0 if core == 0 else 1.0
    consts[:, C_HALO + 1] = 0.0 if core == NCORES - 1 else 1.0
    t = np.arange(WT)
    prow = (r0 + t // GW).astype(np.float32)
    pcol = (t % GW).astype(np.float32)
    p = np.arange(128)
    d = p % 64
    inv = (10000.0 ** (-(np.arange(16, dtype=np.float32)) / 16)).astype(np.float32)
    invp = inv[d % 16]
    pos = np.where((d // 32 == 0)[:, None], prow[None, :], pcol[None, :]).astype(np.float32)
    ang = (pos * invp[:, None]).astype(np.float32)
    cs = np.zeros((2, 128, TT), np.float32)
    cs[0, :, :WT] = np.cos(ang)
    cs[1, :, :WT] = np.sin(ang)
    cs[0, :, WT:] = 1.0
    rpb = inp["rpb"][0]
    spec = []
    for j, pairs in ((0, list(range(0, 6))), (1, list(range(1, 6))), (14, list(range(14, 19))), (15, list(range(14, 20)))):
        spec.append(_bias_table(rpb, core, j, pairs))
    spec = np.concatenate(spec, axis=1)
    tspec = spec.reshape(8, 2, 22, 128, 128).transpose(0, 3, 1, 2, 4).reshape(8, 128, 2 * 22 * 128)
    m = {"xin": xin, "consts": consts, "cossin": cs, "tspec": np.ascontiguousarray(tspec)}
    return m


def _prep_shared(inp):
    consts = np.zeros((128, NCONST), np.float32)
    consts[:, C_BADA:C_BADA + 144] = _fm(inp["b_ada"][0], 144)
    consts[:, C_FF1N:C_FF1N + 16] = _fm(inp["ff1_norm"][0], 16)
    consts[:, C_MIXN:C_MIXN + 16] = _fm(inp["mix_norm"][0], 16)
    consts[:, C_FF2N:C_FF2N + 16] = _fm(inp["ff2_norm"][0], 16)
    consts[:, C_QG] = np.tile(inp["q_norm"][0], 2)
    consts[:, C_KG] = np.tile(inp["k_norm"][0], 2)
    consts[:, C_ONA:C_ONA + 8] = _fm(inp["out_norm_attn"][0], 8)
    consts[:, C_ONC:C_ONC + 8] = _fm(inp["out_norm_conv"][0], 8)
    cw = inp["conv_w"][0]
    for cc in range(8):
        for jj in range(3):
            consts[:, C_CW + cc * 3 + jj] = cw[jj, cc * 128:(cc + 1) * 128]
    consts[:, C_CB:C_CB + 8] = _fm(inp["conv_b"][0], 8)
    c2 = np.stack([_fm(inp["c"][0], 16), _fm(inp["c_ctx"], 16)], axis=-1)
    consts[:, C_C2:C_C2 + 32] = c2.reshape(128, 32)
    bones = np.zeros((128, 128), np.float32)
    bones[:64, :64] = 1.0
    bones[64:, 64:] = 1.0
    pm = np.zeros((128, 128), np.float32)
    for mm in range(128):
        if (mm % 32) < 16:
            pm[mm + 16, mm] = -1.0
        else:
            pm[mm - 16, mm] = 1.0
    std = _bias_table(inp["rpb"][0], 1, 5, list(range(5, 10)))
    tstd = std.reshape(8, 2, 5, 128, 128).transpose(0, 3, 1, 2, 4).reshape(8, 128, 2 * 5 * 128)
    sh = {
        "consts": consts, "bones": bones, "pmat": pm, "tstd": np.ascontiguousarray(tstd),
        "w_ada": np.ascontiguousarray(inp["w_ada"][0]),
        "ff1_w_in": np.ascontiguousarray(inp["ff1_w_in"][0]), "ff1_w_out": np.ascontiguousarray(inp["ff1_w_out"][0]),
        "w_in": np.ascontiguousarray(inp["w_in"][0]), "w_out": np.ascontiguousarray(inp["w_out"][0]),
        "ff2_w_in": np.ascontiguousarray(inp["ff2_w_in"][0]), "ff2_w_out": np.ascontiguousarray(inp["ff2_w_out"][0]),
    }
    return sh


def make_in_maps(inp):
    inp = {k: np.asarray(v) for k, v in inp.items()}
    sh = _prep_shared(inp)
    maps = []
    for core in range(NCORES):
        m = dict(sh)
        m.update(_prep_core(core, inp, sh))
        maps.append(m)
    return maps


def kernel(**inputs):
    maps = make_in_maps(inputs)
    nc = build_nc()
    res = run_bass_kernel_spmd(nc, maps, core_ids=list(range(NCORES)))
    outs = []
    for r in res.results:
        y = np.asarray(r["yout"]).reshape(D, OWN)
        outs.append(y.T)
    out = np.concatenate(outs, axis=0).reshape(1, 16384, D).astype(np.float32)
    return out
```
